# Optimizing a Trainium2 kernel written in Bass

```python
import math
import jax, jax.numpy as jnp
from jax import lax
import numpy as np

D_MODEL = 1024
BATCH = 4
SEQ = 8192
DEPTH = 1

ATTN_HEADS = 8
ATTN_QK_DIM = 32
ATTN_V_DIM = 2 * ATTN_QK_DIM
ATTN_WIDTH = ATTN_HEADS * ATTN_V_DIM
Q_BLOCK = 128
SSM_HEADS = 16
SSM_HEAD_DIM = 64
SSM_WIDTH = SSM_HEADS * SSM_HEAD_DIM
SSM_GROUPS = 2
SSM_HEADS_PER_GROUP = SSM_HEADS // SSM_GROUPS
SSM_STATE = 128
CONV_WIDTH = 4
CHUNK = 128
MIX_WIDTH = ATTN_WIDTH + SSM_WIDTH
D_FF = 2816
EPS = 1e-6
Q_COLS = ATTN_HEADS * 2 * ATTN_QK_DIM
K_COLS = ATTN_HEADS * 2 * ATTN_QK_DIM
V_COLS = ATTN_WIDTH
Z_COLS = SSM_WIDTH
XBC_COLS = SSM_WIDTH + 2 * SSM_GROUPS * SSM_STATE
DT_COLS = SSM_HEADS
IN_COLS = Q_COLS + K_COLS + V_COLS + Z_COLS + XBC_COLS + DT_COLS
IN_SPLITS = (Q_COLS, Q_COLS + K_COLS, Q_COLS + K_COLS + V_COLS,
             Q_COLS + K_COLS + V_COLS + Z_COLS,
             Q_COLS + K_COLS + V_COLS + Z_COLS + XBC_COLS)

kernel_name = 'hymba_diffattn_ssd_macaron_layer'


def rms_norm(x, g):
    xf = x.astype(jnp.float32)
    y = xf * lax.rsqrt(jnp.mean(xf * xf, axis=-1, keepdims=True) + EPS)
    return (y * g.astype(jnp.float32)).astype(x.dtype)


def swiglu(h, w_gate, w_up, w_down):
    return (jax.nn.silu(h @ w_gate) * (h @ w_up)) @ w_down


def alibi_slopes(n):
    return jnp.asarray(2.0 ** (-8.0 * np.arange(1, n + 1) / n), dtype=jnp.float32)


def diff_attention(q, k, v, lam, subln_g, lambda_init):
    b, s, _ = q.shape
    f32 = jnp.float32
    q = q.astype(f32).reshape(b, s, ATTN_HEADS, 2, ATTN_QK_DIM) * (ATTN_QK_DIM ** -0.5)
    k = k.astype(f32).reshape(b, s, ATTN_HEADS, 2, ATTN_QK_DIM)
    v = v.astype(f32).reshape(b, s, ATTN_HEADS, ATTN_V_DIM).transpose(0, 2, 1, 3)
    q1, q2 = q[..., 0, :].transpose(0, 2, 1, 3), q[..., 1, :].transpose(0, 2, 1, 3)
    k1, k2 = k[..., 0, :].transpose(0, 2, 1, 3), k[..., 1, :].transpose(0, 2, 1, 3)
    slopes = alibi_slopes(ATTN_HEADS)
    kpos = jnp.arange(s)
    nb = s // Q_BLOCK

    def to_blocks(t):
        return t.reshape(b, ATTN_HEADS, nb, Q_BLOCK, ATTN_QK_DIM).transpose(2, 0, 1, 3, 4)

    def block(args):
        q1b, q2b, start = args
        qpos = start + jnp.arange(Q_BLOCK)
        dist = qpos[:, None] - kpos[None, :]
        causal = dist >= 0
        bias = -slopes[:, None, None] * dist.astype(f32)

        def probs(qb, kk):
            sc = jnp.einsum('bhqd,bhkd->bhqk', qb, kk) + bias
            return jax.nn.softmax(jnp.where(causal, sc, -jnp.inf), axis=-1)

        a = probs(q1b, k1) - lam * probs(q2b, k2)
        return jnp.einsum('bhqk,bhkv->bhqv', a, v)

    starts = jnp.arange(nb) * Q_BLOCK
    o = lax.map(block, (to_blocks(q1), to_blocks(q2), starts))
    o = o.transpose(1, 0, 3, 2, 4).reshape(b, s, ATTN_HEADS, ATTN_V_DIM)
    o = rms_norm(o, subln_g) * (1.0 - lambda_init)
    return o.reshape(b, s, ATTN_WIDTH)


def causal_depthwise_conv(u, w, bias):
    c = u.shape[-1]
    out = lax.conv_general_dilated(u, w[:, None, :], window_strides=(1,),
                                   padding=[(CONV_WIDTH - 1, 0)],
                                   dimension_numbers=('NWC', 'WIO', 'NWC'),
                                   feature_group_count=c)
    return out + bias


def segsum(a):
    t = a.shape[-1]
    rep = jnp.broadcast_to(a[..., :, None], a.shape + (t,))
    rep = jnp.where(jnp.tril(jnp.ones((t, t), dtype=bool), -1), rep, 0.0)
    ss = jnp.cumsum(rep, axis=-2)
    return jnp.where(jnp.tril(jnp.ones((t, t), dtype=bool)), ss, -jnp.inf)


def ssd_chunked(x, a, bm, cm):
    b, s = x.shape[:2]
    c = s // CHUNK
    x = x.reshape(b, c, CHUNK, SSM_GROUPS, SSM_HEADS_PER_GROUP, SSM_HEAD_DIM)
    a = a.reshape(b, c, CHUNK, SSM_GROUPS, SSM_HEADS_PER_GROUP).transpose(0, 3, 4, 1, 2)
    bm = bm.reshape(b, c, CHUNK, SSM_GROUPS, SSM_STATE)
    cm = cm.reshape(b, c, CHUNK, SSM_GROUPS, SSM_STATE)
    a_cs = jnp.cumsum(a, axis=-1)
    decay = jnp.exp(segsum(a))
    cb = jnp.einsum('bclgn,bcsgn->bgcls', cm, bm)
    y_diag = jnp.einsum('bgjcls,bcsgjp->bclgjp', cb[:, :, None] * decay, x)
    decay_states = jnp.exp(a_cs[..., -1:] - a_cs)
    states = jnp.einsum('bclgn,bgjcl,bclgjp->bcgjpn', bm, decay_states, x)
    states = jnp.concatenate([jnp.zeros_like(states[:, :1]), states], axis=1)
    decay_chunk = jnp.exp(segsum(jnp.pad(a_cs[..., -1], ((0, 0), (0, 0), (0, 0), (1, 0)))))
    states = jnp.einsum('bgjzc,bcgjpn->bzgjpn', decay_chunk, states)[:, :-1]
    y_off = jnp.einsum('bclgn,bcgjpn,bgjcl->bclgjp', cm, states, jnp.exp(a_cs))
    return (y_diag + y_off).reshape(b, s, SSM_GROUPS, SSM_HEADS_PER_GROUP, SSM_HEAD_DIM)


def ssd_mixer(z, xbc, dt, conv_w, conv_b, dt_bias, a_log, d_skip, norm_g):
    b, s, _ = z.shape
    f32 = jnp.float32
    xbc = jax.nn.silu(causal_depthwise_conv(xbc, conv_w, conv_b))
    xs, bm, cm = jnp.split(xbc, [SSM_WIDTH, SSM_WIDTH + SSM_GROUPS * SSM_STATE], axis=-1)
    xs = xs.astype(f32).reshape(b, s, SSM_GROUPS, SSM_HEADS_PER_GROUP, SSM_HEAD_DIM)
    bm = bm.astype(f32).reshape(b, s, SSM_GROUPS, SSM_STATE)
    cm = cm.astype(f32).reshape(b, s, SSM_GROUPS, SSM_STATE)
    dt = jax.nn.softplus(dt.astype(f32) + dt_bias.astype(f32)).reshape(b, s, SSM_GROUPS, SSM_HEADS_PER_GROUP)
    a = -jnp.exp(a_log.astype(f32)).reshape(SSM_GROUPS, SSM_HEADS_PER_GROUP)
    y = ssd_chunked(xs * dt[..., None], a * dt, bm, cm)
    y = y + xs * d_skip.astype(f32).reshape(SSM_GROUPS, SSM_HEADS_PER_GROUP, 1)
    y = y.reshape(b, s, SSM_WIDTH) * jax.nn.silu(z.astype(f32))
    y = rms_norm(y.reshape(b, s, SSM_GROUPS, SSM_WIDTH // SSM_GROUPS),
                 norm_g.reshape(SSM_GROUPS, SSM_WIDTH // SSM_GROUPS))
    return y.reshape(b, s, SSM_WIDTH).astype(z.dtype)


def hybrid_layer(x, ffn1_pre_g, ffn1_w_gate, ffn1_w_up, ffn1_w_down, ffn1_post_g,
                 mix_pre_g, w_in, lambda_q1, lambda_k1, lambda_q2, lambda_k2, attn_subln_g,
                 conv_w, conv_b, dt_bias, a_log, d_skip, ssm_norm_g, w_out, mix_post_g,
                 ffn2_pre_g, ffn2_w_gate, ffn2_w_up, ffn2_w_down, ffn2_post_g, lambda_init):
    h = rms_norm(x, ffn1_pre_g)
    x = x + 0.5 * rms_norm(swiglu(h, ffn1_w_gate, ffn1_w_up, ffn1_w_down), ffn1_post_g)
    h = rms_norm(x, mix_pre_g)
    proj = h @ w_in
    q, k, v, z, xbc, dt = jnp.split(proj, IN_SPLITS, axis=-1)
    f32 = jnp.float32
    lam = (jnp.exp(jnp.sum(lambda_q1.astype(f32) * lambda_k1.astype(f32)))
           - jnp.exp(jnp.sum(lambda_q2.astype(f32) * lambda_k2.astype(f32))) + lambda_init)
    attn = diff_attention(q, k, v, lam, attn_subln_g, lambda_init).astype(x.dtype)
    ssm = ssd_mixer(z, xbc, dt, conv_w, conv_b, dt_bias, a_log, d_skip, ssm_norm_g)
    mixed = jnp.concatenate([attn, ssm], axis=-1) @ w_out
    x = x + rms_norm(mixed, mix_post_g)
    h = rms_norm(x, ffn2_pre_g)
    x = x + 0.5 * rms_norm(swiglu(h, ffn2_w_gate, ffn2_w_up, ffn2_w_down), ffn2_post_g)
    return x


def setup_inputs(seed: int = 0) -> dict:
    key = jax.random.key(seed)
    ks = jax.random.split(key, 28)
    f32 = jnp.float32

    def normal(k, shape, scale):
        return jax.random.normal(k, shape, f32) * scale

    def gain(k, shape):
        return 1.0 + 0.05 * jax.random.normal(k, shape, f32)

    L = DEPTH
    u = jax.random.uniform(ks[15], (L, SSM_HEADS), f32)
    dt0 = jnp.exp(u * (math.log(0.1) - math.log(0.001)) + math.log(0.001))
    dt_bias = dt0 + jnp.log(-jnp.expm1(-dt0))
    return {
        'x': normal(ks[0], (BATCH, SEQ, D_MODEL), 1.0),
        'ffn1_pre_g': gain(ks[1], (L, D_MODEL)),
        'ffn1_w_gate': normal(ks[2], (L, D_MODEL, D_FF), D_MODEL ** -0.5),
        'ffn1_w_up': normal(ks[3], (L, D_MODEL, D_FF), D_MODEL ** -0.5),
        'ffn1_w_down': normal(ks[4], (L, D_FF, D_MODEL), D_FF ** -0.5),
        'ffn1_post_g': gain(ks[5], (L, D_MODEL)),
        'mix_pre_g': gain(ks[6], (L, D_MODEL)),
        'w_in': normal(ks[7], (L, D_MODEL, IN_COLS), D_MODEL ** -0.5),
        'lambda_q1': normal(ks[8], (L, ATTN_QK_DIM), 0.1),
        'lambda_k1': normal(ks[9], (L, ATTN_QK_DIM), 0.1),
        'lambda_q2': normal(ks[10], (L, ATTN_QK_DIM), 0.1),
        'lambda_k2': normal(ks[11], (L, ATTN_QK_DIM), 0.1),
        'attn_subln_g': gain(ks[12], (L, ATTN_V_DIM)),
        'conv_w': normal(ks[13], (L, CONV_WIDTH, XBC_COLS), CONV_WIDTH ** -0.5),
        'conv_b': normal(ks[14], (L, XBC_COLS), 0.02),
        'dt_bias': dt_bias,
        'a_log': jnp.log(jax.random.uniform(ks[16], (L, SSM_HEADS), f32, 1.0, 16.0)),
        'd_skip': 1.0 + 0.1 * jax.random.normal(ks[17], (L, SSM_HEADS), f32),
        'ssm_norm_g': gain(ks[18], (L, SSM_WIDTH)),
        'w_out': normal(ks[19], (L, MIX_WIDTH, D_MODEL), MIX_WIDTH ** -0.5),
        'mix_post_g': gain(ks[20], (L, D_MODEL)),
        'ffn2_pre_g': gain(ks[21], (L, D_MODEL)),
        'ffn2_w_gate': normal(ks[22], (L, D_MODEL, D_FF), D_MODEL ** -0.5),
        'ffn2_w_up': normal(ks[23], (L, D_MODEL, D_FF), D_MODEL ** -0.5),
        'ffn2_w_down': normal(ks[24], (L, D_FF, D_MODEL), D_FF ** -0.5),
        'ffn2_post_g': gain(ks[25], (L, D_MODEL)),
    }


def reference(x, ffn1_pre_g, ffn1_w_gate, ffn1_w_up, ffn1_w_down, ffn1_post_g,
              mix_pre_g, w_in, lambda_q1, lambda_k1, lambda_q2, lambda_k2, attn_subln_g,
              conv_w, conv_b, dt_bias, a_log, d_skip, ssm_norm_g, w_out, mix_post_g,
              ffn2_pre_g, ffn2_w_gate, ffn2_w_up, ffn2_w_down, ffn2_post_g):
    for i in range(DEPTH):
        lambda_init = 0.8 - 0.6 * math.exp(-0.3 * i)
        x = hybrid_layer(x, ffn1_pre_g[i], ffn1_w_gate[i], ffn1_w_up[i], ffn1_w_down[i], ffn1_post_g[i],
                         mix_pre_g[i], w_in[i], lambda_q1[i], lambda_k1[i], lambda_q2[i], lambda_k2[i],
                         attn_subln_g[i], conv_w[i], conv_b[i], dt_bias[i], a_log[i], d_skip[i],
                         ssm_norm_g[i], w_out[i], mix_post_g[i],
                         ffn2_pre_g[i], ffn2_w_gate[i], ffn2_w_up[i], ffn2_w_down[i], ffn2_post_g[i],
                         lambda_init)
    return x
```

```python
import numpy as np
import ml_dtypes
import concourse.bass as bass
import concourse.mybir as mybir
from concourse.bass_utils import run_bass_kernel_spmd

F32 = mybir.dt.float32
BF16 = mybir.dt.bfloat16
AF = mybir.ActivationFunctionType
ALU = mybir.AluOpType
AX = mybir.AxisListType

EPOCH = 4000
D = 1024
DFF = 2816
NFC = 22
T = 512
EPS = 1e-6
LAMBDA_INIT = 0.8 - 0.6 * 1.0
ENGS = ("pe", "act", "dve", "pool", "sp")


class Res:
    __slots__ = ("name", "w", "r")

    def __init__(self, name):
        self.name = name
        self.w = None
        self.r = []


class Op:
    __slots__ = ("eng", "fn", "deps", "signal", "cnt", "dma_key")

    def __init__(self, eng, fn):
        self.eng = eng
        self.fn = fn
        self.deps = []
        self.signal = False
        self.cnt = None
        self.dma_key = None


class Sched:
    def __init__(self, nc):
        self.nc = nc
        self.ops = {e: [] for e in ENGS}
        self.dma_cnt = {}
        self.nops = 0

    def add(self, eng, fn, reads=(), writes=(), excl=(), dma_key=None):
        op = Op(eng, fn)
        op.dma_key = dma_key
        self.nops += 1
        cand = []
        for r in reads:
            if r.w is not None:
                cand.append((r.w, True))
        for w in writes:
            if w.w is not None:
                cand.append((w.w, False))
            for x in w.r:
                cand.append((x, False))
        for w in excl:
            if w.w is not None:
                cand.append((w.w, False))
            for x in w.r:
                cand.append((x, False))
        seen = {}
        for d, raw in cand:
            if d is op:
                continue
            seen[id(d)] = (d, seen.get(id(d), (d, False))[1] or raw)
        for d, raw in seen.values():
            if d.dma_key is None and d.eng == eng:
                if eng == "pe":
                    continue
            op.deps.append(d)
        for r in reads:
            r.r.append(op)
        for w in writes:
            w.w = op
            w.r = []
        for w in excl:
            w.r.append(op)
        if dma_key is not None:
            self.dma_cnt[dma_key] = self.dma_cnt.get(dma_key, 0) + 16
            op.cnt = self.dma_cnt[dma_key]
        self.ops[eng].append(op)
        return op

    def emit(self, final_waits=()):
        nc = self.nc
        for e in ENGS:
            for op in self.ops[e]:
                for d in op.deps:
                    d.signal = True
        for op in final_waits:
            op.signal = True
        for e in ENGS:
            c = 0
            for op in self.ops[e]:
                if op.dma_key is None and op.signal:
                    c += 1
                    op.cnt = c
        sems = {}

        def sem_for(op):
            if op.dma_key is not None:
                k = ("dma", op.dma_key)
                v = op.cnt
            else:
                k = (op.eng, (op.cnt - 1) // EPOCH)
                v = (op.cnt - 1) % EPOCH + 1
            if k not in sems:
                sems[k] = nc.alloc_semaphore("s_%s_%s" % (k[0], k[1]))
            return k, sems[k], v

        with nc.Block() as block:
            def run(ename):
                def body(eng):
                    waited = {}
                    for op in self.ops[ename]:
                        need = {}
                        for d in op.deps:
                            k, s, v = sem_for(d)
                            if waited.get(k, 0) >= v:
                                continue
                            if k not in need or need[k][1] < v:
                                need[k] = (s, v)
                        for k, (s, v) in need.items():
                            eng.wait_ge(s, v)
                            waited[k] = v
                        ins = op.fn(eng)
                        if op.dma_key is not None:
                            k, s, v = sem_for(op)
                            ins.then_inc(s, 16)
                        elif op.signal:
                            k, s, v = sem_for(op)
                            ins.then_inc(s, 1)
                    if ename == "sp":
                        for op in final_waits:
                            k, s, v = sem_for(op)
                            eng.wait_ge(s, v)
                return body
            block.tensor(run("pe"))
            block.scalar(run("act"))
            block.vector(run("dve"))
            block.gpsimd(run("pool"))
            block.sync(run("sp"))
        return len(sems)


class Tl:
    def __init__(self, t, name):
        self.t = t
        self.r = Res(name)

    def __getitem__(self, k):
        return self.t[k]


def build(NPT, NMT, debug=False):
    nc = bass.Bass("TRN2", target_bir_lowering=False)
    S = Sched(nc)
    NT = NPT + NMT
    NTOK = NT * T
    NBLK = NT * 4
    NJ = NBLK + 3

    def din(name, shape, dt=F32):
        return nc.dram_tensor(name, list(shape), dt, kind="ExternalInput").ap()

    def dscr(name, shape, dt=BF16):
        return nc.dram_tensor(name, list(shape), dt, kind="Internal").ap()

    xp = din("xp", [max(NPT, 1) * T, D])
    xm = din("xm", [NMT * T, D])
    out = nc.dram_tensor("out", [NMT * T, D], F32, kind="ExternalOutput").ap()
    wshape = {"wgu1": [NFC, 128, 2048], "wd1": [NFC, 128, 1024], "wfm": [20, 128, 1024],
              "wtm": [3, 128, 4096], "wdt": [128, 128], "wo": [12, 128, 1024],
              "wgu2": [NFC, 128, 2048], "wd2": [NFC, 128, 1024]}
    wf = {k: din(k, v) for k, v in wshape.items()}
    wb = {k: dscr(k + "_bf", v) for k, v in wshape.items()}
    wres = {k: Res(k) for k in wshape}
    gains = din("gains", [6, D])
    ssm_g = din("ssm_g", [D])
    sub_g = din("sub_g", [64])
    lamv = din("lamv", [4, 32])
    conv_w = din("conv_w", [4, 1536])
    conv_b = din("conv_b", [1536])
    hp = din("hp", [3, 16])
    ident_d = din("ident", [128, 128])
    umat_d = din("umat", [128, 128])
    negm_d = din("negm", [128, 128])
    bias_d = din("biastab", [128, 8 * NJ])
    slopes_d = din("slopes2", [128, 8 * 128])
    qpos_d = din("qpos", [128, T])
    flag_d = din("flags", [128, 2])
    vscr = dscr("vscr", [8, 128, NBLK, 65])
    vres = Res("vscr")
    kscr = dscr("kscr", [8 * 64, NTOK])
    kres = Res("kscr")
    dbg = {}

    def sb(name, shape, dt=F32):
        return Tl(nc.alloc_sbuf_tensor("sb_" + name, list(shape), dt), name)

    psum_all = nc.alloc_psum_tensor("psum_all", [128, 4096], F32)

    class _B:
        def __init__(self, i):
            self.t = psum_all[:, i * 512:(i + 1) * 512]
            self.r = Res("bank%d" % i)
    banks = [_B(i) for i in range(8)]

    def bview(b, dt, shape):
        t = banks[b].t
        v = t[:] if dt == F32 else t[:].bitcast(dt)
        return v

    ktmp = [sb("ktmp%d" % i, [128, T], BF16) for i in range(2)]
    KG = 8 * 128
    kbuf = [sb("kbuf%d" % i, [128, KG], BF16) for i in range(3)]
    xt = sb("xt", [128, 4, D])
    hT = sb("hT", [128, 8, T], BF16)
    AT = sb("AT", [128, NFC, T], BF16)
    wbuf = [sb("wbuf%d" % i, [128, 2048], BF16) for i in range(4)]
    gpost = sb("gpost", [128, D])
    gpre = sb("gpre", [128, 3, 8])
    gmixT = sb("gmixT", [128, 12])
    gsub = sb("gsub", [128, 64])
    identb = sb("identb", [128, 128], BF16)
    identf = sb("identf", [128, 128])
    umf = sb("umf", [128, 128])
    umb = sb("umb", [128, 128], BF16)
    onesf = sb("onesf", [128, 128])
    biasM = sb("biasM", [128, 8 * NJ])
    biasP = sb("biasP", [128, 8 * NJ])
    slopes2 = sb("slopes2", [128, 8 * 128], BF16)
    qpos = sb("qpos", [128, T], BF16)
    flags = sb("flags", [128, 2])
    cw = sb("cw", [128, 12, 4])
    cb = sb("cb", [128, 12])
    hpb = sb("hpb", [128, 48])
    lam = sb("lam", [128, 4])
    lamt = sb("lamt", [128, 4, 32])
    halo = sb("halo", [128, 12, 3])
    Sst = sb("Sst", [128, 2, 512])
    Sbf = sb("Sbf", [128, 2, 512], BF16)
    xn = [sb("xn%d" % i, [128, D], BF16) for i in range(2)]
    junk = sb("junk", [128, 2048])
    stat = sb("stat", [128, 16])
    QT = sb("QT", [128, 4, T], BF16)
    BCT = sb("BCT", [128, 4, T], BF16)
    raw = [sb("raw%d" % i, [128, T + 3]) for i in range(2)]
    cacc = [sb("cacc%d" % i, [128, T]) for i in range(2)]
    xsT = [sb("xsT%d" % i, [128, T], BF16) for i in range(2)]
    xs_tok = sb("xs_tok", [128, 4, D], BF16)
    vtmp = [sb("vtmp%d" % i, [128, 8, 65], BF16) for i in range(2)]
    VB_BLK = 8
    vbuf = [sb("vbuf%d" % i, [128, VB_BLK, 65], BF16) for i in range(3)]
    PT = [sb("PT%d" % i, [128, 2, T], BF16) for i in range(3)]
    OT = sb("OT", [65, 2, T])
    mixed = sb("mixed", [128, 4, 1536], BF16)
    mixedT = AT
    dtt = sb("dtt", [128, 96])
    Dg = sb("Dg", [128, 8, 128])
    Eb = sb("Eb", [128, 8, 128], BF16)
    Lb = sb("Lb", [128, 8, 128], BF16)
    MTb = sb("MTb", [128, 8, 128], BF16)
    Cdec = sb("Cdec", [128, 8, 128], BF16)
    cbm = sb("cbm", [128, 128], BF16)
    Btok = sb("Btok", [128, 128], BF16)
    Xg = sb("Xg", [128, 8, 64], BF16)
    Xdec = sb("Xdec", [128, 8, 64], BF16)
    ytmp = sb("ytmp", [128, 512])

    dma_rr = [0]

    def dma(outap, inap, reads, writes, key, eng="sp", **kw):
        return S.add(eng, lambda e: e.dma_start(out=outap, in_=inap, **kw), reads=reads, writes=writes, dma_key=key)

    def mm(bank_r, outap, lhsT, rhs, start, stop, reads, tp=None, sgc=False):
        if sgc:
            return S.add("pe", lambda e: e.matmul(outap, lhsT=lhsT, rhs=rhs, start=start, stop=stop, skip_group_check=True),
                         reads=reads, writes=[bank_r])
        if tp is not None:
            return S.add("pe", lambda e: e.matmul(outap, lhsT=lhsT, rhs=rhs, start=start, stop=stop, tile_position=tp),
                         reads=reads, writes=[bank_r])
        return S.add("pe", lambda e: e.matmul(outap, lhsT=lhsT, rhs=rhs, start=start, stop=stop),
                     reads=reads, writes=[bank_r])

    def tr(bank_r, outap, inap, ident, reads):
        return S.add("pe", lambda e: e.transpose(out=outap, in_=inap, identity=ident), reads=reads, writes=[bank_r])

    def act(outap, inap, func, reads, writes, excl=(), **kw):
        return S.add("act", lambda e: e.activation(out=outap, in_=inap, func=func, **kw), reads=reads, writes=writes, excl=excl)

    def tt(eng, outap, in0, in1, op, reads, writes, excl=()):
        return S.add(eng, lambda e: e.tensor_tensor(out=outap, in0=in0, in1=in1, op=op), reads=reads, writes=writes, excl=excl)

    def ts(eng, outap, in0, s1, s2, op0, op1, reads, writes, excl=()):
        if op1 is None:
            return S.add(eng, lambda e: e.tensor_scalar(out=outap, in0=in0, scalar1=s1, scalar2=None, op0=op0),
                         reads=reads, writes=writes, excl=excl)
        return S.add(eng, lambda e: e.tensor_scalar(out=outap, in0=in0, scalar1=s1, scalar2=s2, op0=op0, op1=op1),
                     reads=reads, writes=writes, excl=excl)

    def stt(outap, in0, scalar, in1, op0, op1, reads, writes, excl=()):
        return S.add("dve", lambda e: e.scalar_tensor_tensor(out=outap, in0=in0, scalar=scalar, in1=in1, op0=op0, op1=op1),
                     reads=reads, writes=writes, excl=excl)

    def cp(eng, outap, inap, reads, writes, excl=()):
        if eng == "act":
            return S.add("act", lambda e: e.copy(out=outap, in_=inap), reads=reads, writes=writes, excl=excl)
        return S.add(eng, lambda e: e.tensor_copy(out=outap, in_=inap), reads=reads, writes=writes, excl=excl)

    def memset(eng, tl, ap, val):
        return S.add(eng, lambda e: e.memset(ap, val), writes=[tl.r])

    def bc(ap, shape):
        return ap.broadcast_to(shape)

    dma(identb[:], ident_d, [], [identb.r], "c_idb", eng="pool")
    dma(umb[:], negm_d, [], [umb.r], "c_umb", eng="pool")
    dma(slopes2[:], slopes_d, [], [slopes2.r], "c_sl", eng="pool")
    dma(qpos[:], qpos_d, [], [qpos.r], "c_qp", eng="pool")
    memset("pool", gmixT, gmixT[:], 1.0)
    memset("pool", onesf, onesf[:], 1.0)
    memset("pool", halo, halo[:], 0.0)
    memset("pool", Sst, Sst[:], 0.0)
    memset("pool", Sbf, Sbf[:], 0.0)
    for i in range(2):
        memset("pool", vtmp[i], vtmp[i][:], 1.0)
    wgu1_res = {}
    bg = []

    def conv_thunks(k):
        th = []
        n0 = wshape[k][0]
        if len(wshape[k]) == 2:
            th.append(lambda: dma(wb[k], wf[k], [], [wres[k]], "cv_" + k, eng="pool"))
            return th
        step = 4096 // wshape[k][2]
        for i in range(0, n0, step):
            j = min(n0, i + step)
            if k == "wgu1":
                wgu1_res[i] = Res("wgu1_%d" % i)
                th.append(lambda i=i, j=j: dma(wb[k][i:j].rearrange("a p n -> p a n"), wf[k][i:j].rearrange("a p n -> p a n"), [],
                                               [wgu1_res[i], wres[k]], "cv_%s_%d" % (k, i), eng="pool"))
            else:
                th.append(lambda i=i, j=j: dma(wb[k][i:j].rearrange("a p n -> p a n"), wf[k][i:j].rearrange("a p n -> p a n"), [],
                                               [wres[k]], "cv_" + k, eng="pool"))
        return th
    for k in ["wgu1", "wd1", "wfm", "wtm", "wdt"]:
        for f in conv_thunks(k):
            f()
    late_conv = []
    for k in ["wo", "wgu2", "wd2"]:
        late_conv.extend(conv_thunks(k))
    if NPT == 0:
        for f in late_conv:
            f()
        late_conv = []
    dma(identf[:], ident_d, [], [identf.r], "c0")
    dma(umf[:], umat_d, [], [umf.r], "c1")
    dma(biasM[:], bias_d, [], [biasM.r], "c2")
    dma(flags[:], flag_d, [], [flags.r], "c3")
    dma(gpre[:, 0, :], gains[0].rearrange("(c p) -> p c", p=128), [], [gpre.r], "c4", allow_slow_non_contiguous=True)
    dma(gpre[:, 1, :], gains[2].rearrange("(c p) -> p c", p=128), [], [gpre.r], "c4", allow_slow_non_contiguous=True)
    dma(gpre[:, 2, :], gains[4].rearrange("(c p) -> p c", p=128), [], [gpre.r], "c4", allow_slow_non_contiguous=True)
    dma(gmixT[:, 4:12], ssm_g.rearrange("(c p) -> p c", p=128), [], [gmixT.r], "c5", allow_slow_non_contiguous=True)
    dma(gsub[:], sub_g.partition_broadcast(128), [], [gsub.r], "c6")
    for k_ in range(4):
        dma(cw[:, :, k_], conv_w[k_].rearrange("(c p) -> p c", p=128), [], [cw.r], "c7", allow_slow_non_contiguous=True)
    dma(cb[:], conv_b.rearrange("(c p) -> p c", p=128), [], [cb.r], "c8", allow_slow_non_contiguous=True)
    dma(hpb[:], hp.rearrange("a b -> (a b)").partition_broadcast(128), [], [hpb.r], "c9")
    dma(lamt[:], lamv.rearrange("a b -> (a b)").partition_broadcast(128), [], [lamt.r], "c10")
    ts("dve", biasP[:], biasM[:], flags[:, 0:1], None, ALU.add, None, [biasM.r, flags.r], [biasP.r])
    ts("dve", gsub[:], gsub[:], 1.0 - LAMBDA_INIT, None, ALU.mult, None, [gsub.r], [gsub.r])
    act(hpb[:, 16:32], hpb[:, 16:32], AF.Exp, [hpb.r], [hpb.r])
    ts("dve", hpb[:, 16:32], hpb[:, 16:32], -1.0, None, ALU.mult, None, [hpb.r], [hpb.r])
    tt("dve", junk[:, 0:32], lamt[:, 0, :], lamt[:, 1, :], ALU.mult, [lamt.r], [junk.r])
    tt("dve", junk[:, 32:64], lamt[:, 2, :], lamt[:, 3, :], ALU.mult, [lamt.r], [junk.r])
    S.add("dve", lambda e: e.tensor_reduce(out=lam[:, 0:2], in_=junk[:, 0:64].rearrange("p (a b) -> p a b", a=2), axis=AX.X, op=ALU.add),
          reads=[junk.r], writes=[lam.r])
    act(lam[:, 0:2], lam[:, 0:2], AF.Exp, [lam.r], [lam.r])
    tt("dve", lam[:, 2:3], lam[:, 0:1], lam[:, 1:2], ALU.subtract, [lam.r], [lam.r])
    ts("dve", lam[:, 3:4], lam[:, 2:3], LAMBDA_INIT, None, ALU.add, None, [lam.r], [lam.r])

    wslot = [0]
    wslotB = [0]
    wbufB = [sb("wbufB%d" % i, [128, 2048], BF16) for i in range(2)]

    def wload(src_ap, ncols, wr):
        i = wslot[0] % 4
        wslot[0] += 1
        assert ncols <= 2048
        dma(wbuf[i][:, 0:ncols], src_ap, [wr], [wbuf[i].r], "wb%d" % i)
        return wbuf[i]

    def rstd_from_ss(ssap, n, mult, reads_extra=()):
        pass

    class Ctx:
        pass

    def mkctx(xt_, hT_, stat_, sq_ap, sq_r, tb, two_pass):
        c = Ctx()
        c.xt, c.hT, c.stat, c.sq, c.sq_r, c.tb, c.two_pass = xt_, hT_, stat_, sq_ap, sq_r, tb, two_pass
        return c

    def prenorm(gi, cx=None):
        cx = cx or ctxM
        xt_, hT_, st_ = cx.xt, cx.hT, cx.stat
        for s in range(4):
            act(cx.sq, xt_[:, s, :], AF.Square, [xt_.r], [cx.sq_r, st_.r], accum_out=st_[:, s:s + 1])
        act(st_[:, 0:4], st_[:, 0:4], AF.Sqrt, [st_.r], [st_.r], scale=1.0 / D, bias=EPS)
        S.add("dve", lambda e: e.reciprocal(out=st_[:, 0:4], in_=st_[:, 0:4]), reads=[st_.r], writes=[st_.r])
        for s in range(4):
            x_ = xn[s % 2]
            ts("dve", x_[:], xt_[:, s, :], st_[:, s:s + 1], None, ALU.mult, None, [xt_.r, st_.r], [x_.r])
            b = cx.tb[s % 2]
            pv = banks[b].t[:].bitcast(BF16).rearrange("p (c n) -> p c n", c=8)
            for c in range(8):
                tr(banks[b].r, pv[:, c, :], x_[:, c * 128:(c + 1) * 128], identb[:], [x_.r, identb.r])
            tt("dve", hT_[:, :, s * 128:(s + 1) * 128], pv, bc(gpre[:, gi, :].unsqueeze(2), [128, 8, 128]), ALU.mult,
               [banks[b].r, gpre.r], [hT_.r], excl=[banks[b].r])

    def postnorm_residual(cmul, cx, subs, bank_of):
        xt_, st_ = cx.xt, cx.stat
        for s in subs:
            for half in range(2):
                b = bank_of(s, half)
                act(junk[:, half * 512:(half + 1) * 512], banks[b].t[:], AF.Square, [banks[b].r], [junk.r, st_.r], excl=[banks[b].r],
                    accum_out=st_[:, 4 + 2 * s + half:5 + 2 * s + half])
        s0, s1 = subs[0], subs[-1] + 1
        tt("dve", st_[:, 12 + s0:12 + s1], st_[:, 4 + 2 * s0:4 + 2 * s1:2], st_[:, 5 + 2 * s0:5 + 2 * s1:2], ALU.add, [st_.r], [st_.r])
        act(st_[:, 12 + s0:12 + s1], st_[:, 12 + s0:12 + s1], AF.Sqrt, [st_.r], [st_.r], scale=1.0 / D, bias=EPS)
        S.add("dve", lambda e: e.reciprocal(out=st_[:, 12 + s0:12 + s1], in_=st_[:, 12 + s0:12 + s1]), reads=[st_.r], writes=[st_.r])
        if cmul != 1.0:
            ts("dve", st_[:, 12 + s0:12 + s1], st_[:, 12 + s0:12 + s1], cmul, None, ALU.mult, None, [st_.r], [st_.r])
        for s in subs:
            for half in range(2):
                b = bank_of(s, half)
                hs = slice(half * 512, (half + 1) * 512)
                jj = junk[:, 1024 + half * 512:1536 + half * 512]
                stt(jj, banks[b].t[:], st_[:, 12 + s:13 + s], gpost[:, hs], ALU.mult, ALU.mult,
                    [banks[b].r, st_.r, gpost.r], [junk.r], excl=[banks[b].r])
                tt(getattr(cx, "add_eng", "pool"), xt_[:, s, hs], xt_[:, s, hs], jj, ALU.add, [xt_.r, junk.r], [xt_.r])

    def down_post_gen(src, nchunks, wkey, gidx, cmul, cx, hook=None):
        dma(gpost[:], gains[gidx].partition_broadcast(128), [], [gpost.r], "gpost")
        if cx.two_pass:
            pb_ = getattr(cx, "passB_base", 0)
            passes = [([0, 1], lambda s, half: 2 * (s % 2) + half), ([2, 3], lambda s, half: pb_ + 2 * (s % 2) + half)]
        else:
            passes = [([0, 1, 2, 3], lambda s, half: 2 * s + half)]
        for subs, bank_of in passes:
            for c0 in range(0, nchunks, 2):
                c1 = min(nchunks, c0 + 2)
                wsl = wload(wb[wkey][c0:c1].rearrange("a p n -> p a n"), (c1 - c0) * 1024, wres[wkey])
                wv = wsl.t[:].rearrange("p (a n) -> p a n", n=1024)
                for c in range(c0, c1):
                    for s in subs:
                        for half in range(2):
                            b = bank_of(s, half)
                            mm(banks[b].r, banks[b].t[:], src[:, c, s * 128:(s + 1) * 128], wv[:, c - c0, half * 512:(half + 1) * 512],
                               c == 0, c == nchunks - 1, [src.r, wsl.r])
                if c0 % 4 == 2 or c1 == nchunks:
                    yield
            if hook is not None and subs[-1] == 3:
                hook()
            postnorm_residual(cmul, cx, subs, bank_of)
            yield

    first_ffn = [True]

    def ffn_gen(wgu, wd, gpre_i, gpost_i, cx, skip_prenorm=False, hook=None):
        if not skip_prenorm:
            prenorm(gpre_i, cx)
        yield
        for f0 in range(0, NFC, 2):
            for fc in range(f0, f0 + 2):
                wsl = wload(wb[wgu][fc], 2048, wgu1_res[f0] if wgu == "wgu1" else wres[wgu])
                wv = wsl.t[:].rearrange("p (g k n) -> p g k n", g=2, k=8)
                bg = (fc % 2) * 2
                for g in range(2):
                    for kc in range(8):
                        mm(banks[bg + g].r, banks[bg + g].t[:], wv[:, g, kc, :], cx.hT[:, kc, :], kc == 0, kc == 7, [wsl.r, cx.hT.r])
                jg = junk[:, (fc % 2) * 512:(fc % 2 + 1) * 512]
                act(jg, banks[bg].t[:], AF.Silu, [banks[bg].r], [junk.r], excl=[banks[bg].r])
                tt("dve", AT[:, fc, :], jg, banks[bg + 1].t[:], ALU.mult, [junk.r, banks[bg + 1].r], [AT.r], excl=[banks[bg + 1].r])
            yield
        first_ffn[0] = False
        for _ in down_post_gen(AT, NFC, wd, gpost_i, 0.5, cx, hook=hook):
            yield

    def run_gen(g):
        for _ in g:
            pass

    def interleave(ga, gb, lead=0):
        da = db = False
        for _ in range(lead):
            try:
                next(ga)
            except StopIteration:
                da = True
                break
        while not (da and db):
            if bg:
                bg.pop(0)()
            if not da:
                try:
                    next(ga)
                except StopIteration:
                    da = True
            if not db:
                try:
                    next(gb)
                except StopIteration:
                    db = True

    def ffn(wgu, wd, gpre_i, gpost_i):
        run_gen(ffn_gen(wgu, wd, gpre_i, gpost_i, ctxM))

    def prefix_mixer_gen(tile_abs, cx):
        tok0 = tile_abs * T
        hT_ = cx.hT
        prenorm(1, cx)
        yield
        chunks = list(range(4, 20))
        wslB = {}

        def loadB(pi):
            if pi * 2 < len(chunks) and pi not in wslB:
                c0 = chunks[pi * 2]
                i_ = wslotB[0] % 2
                wslotB[0] += 1
                dma(wbufB[i_][:, 0:2048], wb["wfm"][c0:c0 + 2].rearrange("a p n -> p a n"), [wres["wfm"]], [wbufB[i_].r], "wbB%d" % i_)
                wslB[pi] = wbufB[i_]
        loadB(0)
        pend = []

        def flush():
            for f in pend:
                f()
            del pend[:]
        for ci, c in enumerate(chunks):
            if ci % 2 == 0:
                loadB(ci // 2 + 1)
            wsl = wslB[ci // 2]
            wv = wsl.t[:, 0:2048].rearrange("p (a k n) -> p a k n", k=8, n=128)
            a = ci % 2
            b = 6 + ci % 2
            for kc in range(8):
                mm(banks[b].r, banks[b].t[:], wv[:, a, kc, :], hT_[:, kc, :], kc == 0, kc == 7, [wsl.r, hT_.r])
            flush()
            if c < 8:
                kt_ = ktmp[c % 2]
                cp("act", kt_[:], banks[b].t[:], [banks[b].r], [kt_.r], excl=[banks[b].r])
                dma(kscr[(c - 4) * 128:(c - 3) * 128, tok0:tok0 + T], kt_[:], [kt_.r], [kres], "kw", eng="act")
            else:
                x = c - 8
                rw = raw[x % 2]
                ca = cacc[x % 2]
                cp("pool", rw[:, 0:3], halo[:, x, :], [halo.r], [rw.r])
                cp("act", rw[:, 3:T + 3], banks[b].t[:], [banks[b].r], [rw.r], excl=[banks[b].r])
                ts("dve", ca[:], rw[:, 0:T], cw[:, x, 0:1], cb[:, x:x + 1], ALU.mult, ALU.add, [rw.r, cw.r, cb.r], [ca.r])
                for k in range(1, 4):
                    stt(ca[:], rw[:, k:k + T], cw[:, x, k:k + 1], ca[:], ALU.mult, ALU.add, [rw.r, cw.r, ca.r], [ca.r])
                cp("pool", halo[:, x, :], rw[:, T:T + 3], [rw.r], [halo.r])
                if x < 8:
                    xo = xsT[x % 2]
                    act(xo[:], ca[:], AF.Silu, [ca.r], [xo.r])
                    def _trs(x=x, xo=xo, bt=6 + ci % 2):
                        pv = banks[bt].t[:].bitcast(BF16)[:, 0:512].rearrange("p (s n) -> p s n", s=4)
                        for s in range(4):
                            tr(banks[bt].r, pv[:, s, :], xo[:, s * 128:(s + 1) * 128], identb[:], [xo.r, identb.r])
                        cp("dve", xs_tok[:, :, x * 128:(x + 1) * 128], pv, [banks[bt].r], [xs_tok.r], excl=[banks[bt].r])
                    pend.append(_trs)
                elif x < 10:
                    act(BCT[:, x - 8, :], ca[:], AF.Silu, [ca.r], [BCT.r])
            yield
        flush()
        wvs = []
        for hf in range(2):
            i_ = wslotB[0] % 2
            wslotB[0] += 1
            dma(wbufB[i_][:, 0:2048].rearrange("p (k n) -> p k n", k=8), wb["wtm"][0].rearrange("p (k n) -> p k n", k=8)[:, :, hf * 256:(hf + 1) * 256],
                [wres["wtm"]], [wbufB[i_].r], "wbB%d" % i_)
            wvs.append(wbufB[i_])
        for s in range(4):
            b = 6 + s % 2
            for hf in range(2):
                wv = wvs[hf].t[:, 0:2048].rearrange("p (k n) -> p k n", k=8)
                for kc in range(8):
                    mm(banks[b].r, banks[b].t[:, hf * 256:(hf + 1) * 256], hT_[:, kc, s * 128:(s + 1) * 128], wv[:, kc, :], kc == 0, kc == 7,
                       [wvs[hf].r, hT_.r])
            vt = vtmp[s % 2]
            cp("act", vt[:, :, 0:64], banks[b].t[:].rearrange("p (h v) -> p h v", h=8), [banks[b].r], [vt.r], excl=[banks[b].r])
            blk = tile_abs * 4 + s
            dma(vscr[:, :, blk, :].rearrange("h p v -> p h v"), vt[:], [vt.r], [vres], "vw", eng="act")
            yield
        dt_proj(hT_, 6)
        yield
        for s in range(4):
            for _ in ssd_chunk(s, False, (6, 7, (6, 7))):
                yield

    def proj_and_state(tile_abs, is_main):
        tok0 = tile_abs * T
        prenorm(1)
        chunks = (list(range(0, 4)) if is_main else []) + list(range(4, 20))
        wslM = {}

        def st0(ci):
            if ci % 2 == 0:
                grp = chunks[ci:ci + 2]
                wslM[ci // 2] = wload(wb["wfm"][grp[0]:grp[0] + len(grp)].rearrange("a p n -> p a n"), len(grp) * 1024, wres["wfm"])
            wsl = wslM[ci // 2]
            wv = wsl.t[:].rearrange("p (a k n) -> p a k n", k=8, n=128)
            b = ci % 4
            for kc in range(8):
                mm(banks[b].r, banks[b].t[:], wv[:, ci % 2, kc, :], hT[:, kc, :], kc == 0, kc == 7, [wsl.r, hT.r])

        def st1(ci):
            c = chunks[ci]
            b = ci % 4
            if c < 4:
                act(QT[:, c, :], banks[b].t[:], AF.Copy, [banks[b].r], [QT.r], excl=[banks[b].r], scale=32.0 ** -0.5)
            elif c < 8:
                kt_ = ktmp[c % 2]
                cp("act", kt_[:], banks[b].t[:], [banks[b].r], [kt_.r], excl=[banks[b].r])
            else:
                x = c - 8
                rw = raw[x % 2]
                cp("pool", rw[:, 0:3], halo[:, x, :], [halo.r], [rw.r])
                cp("act", rw[:, 3:T + 3], banks[b].t[:], [banks[b].r], [rw.r], excl=[banks[b].r])

        def st2(ci):
            c = chunks[ci]
            if 4 <= c < 8:
                kt_ = ktmp[c % 2]
                dma(kscr[(c - 4) * 128:(c - 3) * 128, tok0:tok0 + T], kt_[:], [kt_.r], [kres], "kw", eng="act")
            elif c >= 8:
                x = c - 8
                rw = raw[x % 2]
                ca = cacc[x % 2]
                ts("dve", ca[:], rw[:, 0:T], cw[:, x, 0:1], cb[:, x:x + 1], ALU.mult, ALU.add, [rw.r, cw.r, cb.r], [ca.r])
                for k in range(1, 4):
                    stt(ca[:], rw[:, k:k + T], cw[:, x, k:k + 1], ca[:], ALU.mult, ALU.add, [rw.r, cw.r, ca.r], [ca.r])
                cp("pool", halo[:, x, :], rw[:, T:T + 3], [rw.r], [halo.r])

        def st3(ci):
            c = chunks[ci]
            if c >= 8:
                x = c - 8
                ca = cacc[x % 2]
                if x < 8:
                    act(xsT[x % 2][:], ca[:], AF.Silu, [ca.r], [xsT[x % 2].r])
                else:
                    act(BCT[:, x - 8, :], ca[:], AF.Silu, [ca.r], [BCT.r])

        def st4(ci):
            c = chunks[ci]
            if 8 <= c < 16:
                x = c - 8
                xo = xsT[x % 2]
                bt = 6 + (x % 2)
                pv = banks[bt].t[:].bitcast(BF16)[:, 0:512].rearrange("p (s n) -> p s n", s=4)
                for s in range(4):
                    tr(banks[bt].r, pv[:, s, :], xo[:, s * 128:(s + 1) * 128], identb[:], [xo.r, identb.r])
                cp("dve", xs_tok[:, :, x * 128:(x + 1) * 128], pv, [banks[bt].r], [xs_tok.r], excl=[banks[bt].r])
        stages = [st0, st1, st2, st3, st4]
        n = len(chunks)
        for u in range(n + len(stages) - 1):
            for k, f in enumerate(stages):
                ci = u - k
                if 0 <= ci < n:
                    f(ci)
        ngrp = 3 if is_main else 1
        for g in range(ngrp):
            wsrc = wb["wtm"][g].rearrange("p (k n) -> p k n", k=8)
            wsl2 = []
            for hf in range(2):
                i_ = wslot[0] % 4
                wslot[0] += 1
                dma(wbuf[i_][:].rearrange("p (k n) -> p k n", k=4), wsrc[:, hf * 4:(hf + 1) * 4, :], [wres["wtm"]], [wbuf[i_].r], "wb%d" % i_)
                wsl2.append(wbuf[i_])
            for s in range(4):
                if g > 0:
                    b = 4 * ((g - 1) % 2) + s
                else:
                    b = 4 + s
                for kc in range(8):
                    wsl = wsl2[kc // 4]
                    wv = wsl.t[:].rearrange("p (k n) -> p k n", k=4)
                    mm(banks[b].r, banks[b].t[:], hT[:, kc, s * 128:(s + 1) * 128], wv[:, kc % 4, :], kc == 0, kc == 7, [wsl.r, hT.r])
                if g == 0:
                    vt = vtmp[s % 2]
                    cp("act", vt[:, :, 0:64], banks[b].t[:].rearrange("p (h v) -> p h v", h=8), [banks[b].r], [vt.r], excl=[banks[b].r])
                    blk = tile_abs * 4 + s
                    dma(vscr[:, :, blk, :].rearrange("h p v -> p h v"), vt[:], [vt.r], [vres], "vw", eng="act")
                else:
                    zc = slice((g - 1) * 512, g * 512)
                    act(zs_all[s][:, zc], banks[b].t[:], AF.Silu, [banks[b].r], [zs_all[s].r], excl=[banks[b].r])
        return tok0

    class _V:
        def __init__(self, ap, r):
            self.ap = ap
            self.r = r

        def __getitem__(self, k):
            return self.ap[k]
    _atf = AT.t[:].rearrange("p c n -> p (c n)").bitcast(F32)
    zs_all = [_V(_atf[:, i * D:(i + 1) * D], AT.r) for i in range(4)]
    ao = _V(xs_tok.t[:].rearrange("p s n -> p (s n)").bitcast(F32).rearrange("p (s n) -> p s n", s=4), xs_tok.r)
    dt_all = sb("dt_all", [128, 4, 16])

    def dt_proj(hT_=None, b=3):
        if hT_ is None:
            hT_ = hT
            wsl = wload(wb["wdt"], 128, wres["wdt"])
        else:
            i_ = wslotB[0] % 2
            wslotB[0] += 1
            dma(wbufB[i_][:, 0:128], wb["wdt"], [wres["wdt"]], [wbufB[i_].r], "wbB%d" % i_)
            wsl = wbufB[i_]
        wv = wsl.t[:, 0:128].rearrange("p (k n) -> p k n", k=8)
        for s in range(4):
            for kc in range(8):
                mm(banks[b].r, banks[b].t[:, s * 16:(s + 1) * 16], hT_[:, kc, s * 128:(s + 1) * 128], wv[:, kc, :], kc == 0, kc == 7, [wsl.r, hT_.r])
        tt("dve", dt_all[:], banks[b].t[:, 0:64].rearrange("p (s j) -> p s j", s=4), bc(hpb[:, 0:16].unsqueeze(1), [128, 4, 16]), ALU.add,
           [banks[b].r, hpb.r], [dt_all.r], excl=[banks[b].r])
        act(dt_all[:], dt_all[:], AF.Exp, [dt_all.r], [dt_all.r])
        act(dt_all[:], dt_all[:], AF.Ln, [dt_all.r], [dt_all.r], bias=1.0)

    def ssd_chunk(s, is_main, bk=(3, 2, (6, 7))):
        cs = slice(s * 128, (s + 1) * 128)
        dt_ = dt_all[:, s, :]
        A_ = dtt[:, 16:32]
        tt("dve", A_, dt_, hpb[:, 16:32], ALU.mult, [dt_all.r, hpb.r], [dtt.r])
        b = bk[0]
        mm(banks[b].r, banks[b].t[:, 64:80], umf[:], A_, True, True, [umf.r, dtt.r])
        mm(banks[b].r, banks[b].t[:, 80:96], onesf[:], A_, True, True, [onesf.r, dtt.r])
        cp("dve", dtt[:, 32:64], banks[b].t[:, 64:96], [banks[b].r], [dtt.r], excl=[banks[b].r])
        acs = dtt[:, 32:48]
        tot = dtt[:, 48:64]
        tt("dve", dtt[:, 64:80], tot, acs, ALU.subtract, [dtt.r], [dtt.r])
        act(dtt[:, 64:80], dtt[:, 64:80], AF.Exp, [dtt.r], [dtt.r])
        act(dtt[:, 80:96], tot, AF.Exp, [dtt.r], [dtt.r])
        for g in range(2):
            gs = slice(8 * g, 8 * g + 8)
            xsg = xs_tok[:, s, g * 512:(g + 1) * 512].rearrange("p (j d) -> p j d", j=8)
            BTg = BCT[:, g, cs]
            CTg = BCT[:, 2 + g, cs]
            tt("dve", Xg[:], xsg, bc(dt_all[:, s, gs].unsqueeze(2), [128, 8, 64]), ALU.mult, [xs_tok.r, dt_all.r], [Xg.r])
            if is_main:
                tt("dve", Dg[:], bc(umf[:].unsqueeze(1), [128, 8, 128]), bc(dtt[:, 16 + 8 * g:24 + 8 * g].unsqueeze(2), [128, 8, 128]), ALU.mult,
                   [umf.r, dtt.r], [Dg.r])
                Dv = Dg[:].rearrange("p j l -> p (j l)")
                for hh in range(2):
                    mm(banks[hh].r, banks[hh].t[:], onesf[:], Dv[:, hh * 512:(hh + 1) * 512], True, True, [onesf.r, Dg.r])
                for hh in range(2):
                    act(Eb[:, hh * 4:(hh + 1) * 4, :], banks[hh].t[:].rearrange("p (j l) -> p j l", j=4), AF.Exp, [banks[hh].r], [Eb.r], excl=[banks[hh].r])
                tt("dve", Cdec[:], Eb[:], bc(CTg.unsqueeze(1), [128, 8, 128]), ALU.mult, [Eb.r, BCT.r], [Cdec.r])
                for j in range(8):
                    hh = j // 4
                    ts("dve", Dg[:, j, :], banks[hh].t[:, (j % 4) * 128:(j % 4 + 1) * 128], dtt[:, 32 + 8 * g + j:33 + 8 * g + j], 0.0,
                       ALU.subtract, ALU.min, [banks[hh].r, dtt.r], [Dg.r], excl=[banks[hh].r])
                act(Lb[:], Dg[:], AF.Exp, [Dg.r], [Lb.r])
                mm(banks[2].r, banks[2].t[:, 0:128], BTg, CTg, True, True, [BCT.r])
                tt("dve", cbm[:], banks[2].t[:, 0:128], umf[:], ALU.mult, [banks[2].r, umf.r], [cbm.r], excl=[banks[2].r])
                tt("dve", MTb[:], Lb[:], bc(cbm[:].unsqueeze(1), [128, 8, 128]), ALU.mult, [Lb.r, cbm.r], [MTb.r])
                yb = 4 + g
                for j in range(8):
                    mm(banks[yb].r, banks[yb].t[:, j * 64:(j + 1) * 64], MTb[:, j, :], Xg[:, j, :], True, False, [MTb.r, Xg.r])
                    mm(banks[yb].r, banks[yb].t[:, j * 64:(j + 1) * 64], Cdec[:, j, :], Sbf[:, g, j * 64:(j + 1) * 64], False, True, [Cdec.r, Sbf.r])
                tt("dve", ytmp[:].rearrange("p (j d) -> p j d", j=8), xsg, bc(hpb[:, 32 + 8 * g:40 + 8 * g].unsqueeze(2), [128, 8, 64]), ALU.mult,
                   [xs_tok.r, hpb.r], [ytmp.r])
                tt("dve", ytmp[:], ytmp[:], banks[yb].t[:], ALU.add, [ytmp.r, banks[yb].r], [ytmp.r], excl=[banks[yb].r])
                tt("dve", ytmp[:], ytmp[:], zs_all[s][:, g * 512:(g + 1) * 512], ALU.mult, [ytmp.r, zs_all[s].r], [ytmp.r])
                act(junk[:, 1024:1536], ytmp[:], AF.Square, [ytmp.r], [junk.r, stat.r], accum_out=stat[:, g:g + 1])
                act(stat[:, g:g + 1], stat[:, g:g + 1], AF.Sqrt, [stat.r], [stat.r], scale=1.0 / 512, bias=EPS)
                S.add("dve", lambda e, g=g: e.reciprocal(out=stat[:, g:g + 1], in_=stat[:, g:g + 1]), reads=[stat.r], writes=[stat.r])
                ts("dve", mixed[:, s, 512 + g * 512:1024 + g * 512], ytmp[:], stat[:, g:g + 1], None, ALU.mult, None, [ytmp.r, stat.r], [mixed.r])
            tt("dve", Xdec[:], Xg[:], bc(dtt[:, 64 + 8 * g:72 + 8 * g].unsqueeze(2), [128, 8, 64]), ALU.mult, [Xg.r, dtt.r], [Xdec.r])
            pvb = banks[bk[1]].t[:].bitcast(BF16)[:, 512:640]
            tr(banks[bk[1]].r, pvb, BTg, identb[:], [BCT.r, identb.r])
            cp("dve", Btok[:], pvb, [banks[bk[1]].r], [Btok.r], excl=[banks[bk[1]].r])
            lb = bk[2][g]
            mm(banks[lb].r, banks[lb].t[:], Btok[:], Xdec[:].rearrange("p j d -> p (j d)"), True, True, [Btok.r, Xdec.r])
            Sg = Sst[:, g, :].rearrange("p (j d) -> p j d", j=8)
            tt("dve", Sg, Sg, bc(dtt[:, 80 + 8 * g:88 + 8 * g].unsqueeze(2), [128, 8, 64]), ALU.mult, [Sst.r, dtt.r], [Sst.r])
            tt("dve", Sst[:, g, :], Sst[:, g, :], banks[lb].t[:], ALU.add, [Sst.r, banks[lb].r], [Sst.r], excl=[banks[lb].r])
            cp("pool", Sbf[:, g, :], Sst[:, g, :], [Sst.r], [Sbf.r])
            yield

    def ssd_main_gen(s):
        cs = slice(s * 128, (s + 1) * 128)
        dt_ = dt_all[:, s, :]
        A_ = dtt[:, 16:32]
        st_ = statB
        b7, b6 = banks[7], banks[6]
        tt("dve", A_, dt_, hpb[:, 16:32], ALU.mult, [dt_all.r, hpb.r], [dtt.r])
        yield True
        mm(b7.r, b7.t[:, 64:80], umf[:], A_, True, True, [umf.r, dtt.r])
        mm(b7.r, b7.t[:, 80:96], onesf[:], A_, True, True, [onesf.r, dtt.r])
        yield False
        cp("dve", dtt[:, 32:64], b7.t[:, 64:96], [b7.r], [dtt.r], excl=[b7.r])
        acs = dtt[:, 32:48]
        tot = dtt[:, 48:64]
        tt("dve", dtt[:, 64:80], tot, acs, ALU.subtract, [dtt.r], [dtt.r])
        yield True
        act(dtt[:, 64:80], dtt[:, 64:80], AF.Exp, [dtt.r], [dtt.r])
        act(dtt[:, 80:96], tot, AF.Exp, [dtt.r], [dtt.r])
        yield True
        hb = [b6, b7]
        for g in range(2):
            gs = slice(8 * g, 8 * g + 8)
            xsg = xs_tok[:, s, g * 512:(g + 1) * 512].rearrange("p (j d) -> p j d", j=8)
            BTg = BCT[:, g, cs]
            CTg = BCT[:, 2 + g, cs]
            tt("dve", Xg[:], xsg, bc(dt_all[:, s, gs].unsqueeze(2), [128, 8, 64]), ALU.mult, [xs_tok.r, dt_all.r], [Xg.r])
            tt("dve", Dg[:], bc(umf[:].unsqueeze(1), [128, 8, 128]), bc(dtt[:, 16 + 8 * g:24 + 8 * g].unsqueeze(2), [128, 8, 128]), ALU.mult,
               [umf.r, dtt.r], [Dg.r])
            tt("dve", Xdec[:], Xg[:], bc(dtt[:, 64 + 8 * g:72 + 8 * g].unsqueeze(2), [128, 8, 64]), ALU.mult, [Xg.r, dtt.r], [Xdec.r])
            mm(b7.r, b7.t[:, 0:128], BTg, CTg, True, True, [BCT.r])
            yield False
            tt("dve", cbm[:], b7.t[:, 0:128], umf[:], ALU.mult, [b7.r, umf.r], [cbm.r], excl=[b7.r])
            Dv = Dg[:].rearrange("p j l -> p (j l)")
            for hh in range(2):
                mm(hb[hh].r, hb[hh].t[:], onesf[:], Dv[:, hh * 512:(hh + 1) * 512], True, True, [onesf.r, Dg.r])
            yield False
            for hh in range(2):
                act(Eb[:, hh * 4:(hh + 1) * 4, :], hb[hh].t[:].rearrange("p (j l) -> p j l", j=4), AF.Exp, [hb[hh].r], [Eb.r], excl=[hb[hh].r])
            for j in range(8):
                hh = j // 4
                ts("dve", Dg[:, j, :], hb[hh].t[:, (j % 4) * 128:(j % 4 + 1) * 128], dtt[:, 32 + 8 * g + j:33 + 8 * g + j], 0.0,
                   ALU.subtract, ALU.min, [hb[hh].r, dtt.r], [Dg.r], excl=[hb[hh].r])
            yield True
            yield True
            tt("dve", Cdec[:], Eb[:], bc(CTg.unsqueeze(1), [128, 8, 128]), ALU.mult, [Eb.r, BCT.r], [Cdec.r])
            act(Lb[:], Dg[:], AF.Exp, [Dg.r], [Lb.r])
            pvb = b7.t[:].bitcast(BF16)[:, 512:640]
            tr(b7.r, pvb, BTg, identb[:], [BCT.r, identb.r])
            yield False
            tt("dve", MTb[:], Lb[:], bc(cbm[:].unsqueeze(1), [128, 8, 128]), ALU.mult, [Lb.r, cbm.r], [MTb.r])
            cp("dve", Btok[:], pvb, [b7.r], [Btok.r], excl=[b7.r])
            yield True
            for j in range(8):
                mm(b6.r, b6.t[:, j * 64:(j + 1) * 64], MTb[:, j, :], Xg[:, j, :], True, False, [MTb.r, Xg.r])
                mm(b6.r, b6.t[:, j * 64:(j + 1) * 64], Cdec[:, j, :], Sbf[:, g, j * 64:(j + 1) * 64], False, True, [Cdec.r, Sbf.r])
            mm(b7.r, b7.t[:], Btok[:], Xdec[:].rearrange("p j d -> p (j d)"), True, True, [Btok.r, Xdec.r])
            yield False
            tt("dve", ytmp[:].rearrange("p (j d) -> p j d", j=8), xsg, bc(hpb[:, 32 + 8 * g:40 + 8 * g].unsqueeze(2), [128, 8, 64]), ALU.mult,
               [xs_tok.r, hpb.r], [ytmp.r])
            tt("dve", ytmp[:], ytmp[:], b6.t[:], ALU.add, [ytmp.r, b6.r], [ytmp.r], excl=[b6.r])
            tt("dve", ytmp[:], ytmp[:], zs_all[s][:, g * 512:(g + 1) * 512], ALU.mult, [ytmp.r, zs_all[s].r], [ytmp.r])
            Dsq = Dg[:].rearrange("p j l -> p (j l)")[:, 0:512]
            tt("dve", Dsq, ytmp[:], ytmp[:], ALU.mult, [ytmp.r], [Dg.r])
            S.add("dve", lambda e, g=g, Dsq=Dsq: e.tensor_reduce(out=st_[:, g:g + 1], in_=Dsq, axis=AX.X, op=ALU.add),
                  reads=[Dg.r], writes=[st_.r])
            Sg = Sst[:, g, :].rearrange("p (j d) -> p j d", j=8)
            tt("dve", Sg, Sg, bc(dtt[:, 80 + 8 * g:88 + 8 * g].unsqueeze(2), [128, 8, 64]), ALU.mult, [Sst.r, dtt.r], [Sst.r])
            tt("dve", Sst[:, g, :], Sst[:, g, :], b7.t[:], ALU.add, [Sst.r, b7.r], [Sst.r], excl=[b7.r])
            cp("dve", Sbf[:, g, :], Sst[:, g, :], [Sst.r], [Sbf.r])
            yield True
            yield True
            act(st_[:, g:g + 1], st_[:, g:g + 1], AF.Ln, [st_.r], [st_.r], scale=1.0 / 512, bias=EPS)
            act(st_[:, g:g + 1], st_[:, g:g + 1], AF.Exp, [st_.r], [st_.r], scale=-0.5)
            yield True
            ts("dve", mixed[:, s, 512 + g * 512:1024 + g * 512], ytmp[:], st_[:, g:g + 1], None, ALU.mult, None, [ytmp.r, st_.r], [mixed.r])
            yield True

    ssd_clean = [True]

    def ssd_all_gen():
        for s_ in range(4):
            for c_ in ssd_main_gen(s_):
                yield c_

    pt_i = [0]
    vb_i = [0]

    def att_units(tile_abs):
        tb = tile_abs * 4
        n = 0
        for h in range(8):
            dmax = 140.0 / (2.0 ** -(h + 1))
            kb_lo = 0
            while kb_lo < tb and (tb * 128 - (kb_lo * 128 + 127)) > dmax:
                kb_lo += 1
            n += tb + 4 - kb_lo
        return n

    def attention_gen(tile_abs):
        tb = tile_abs * 4
        nkb = tb + 4
        grp_all = {}
        obase = 4

        def kb_lo_of(hh):
            dm = 140.0 / (2.0 ** -(hh + 1))
            lo = 0
            while lo < tb and (tb * 128 - (lo * 128 + 127)) > dm:
                lo += 1
            return lo

        def get_grp(hh, kb):
            g = kb // VB_BLK
            if (hh, g) not in grp_all:
                k0 = max(g * VB_BLK, kb_lo_of(hh))
                k1 = min(nkb, (g + 1) * VB_BLK)
                vi = vb_i[0] % 3
                vb_i[0] += 1
                p0 = (hh % 2) * 64
                dma(vbuf[vi][:, 0:k1 - k0, :], vscr[hh, :, k0:k1, :], [vres], [vbuf[vi].r], "vb%d" % vi)
                dma(kbuf[vi][p0:p0 + 64, 0:(k1 - k0) * 128], kscr[hh * 64:(hh + 1) * 64, k0 * 128:k1 * 128], [kres], [kbuf[vi].r], "kb%d" % vi)
                grp_all[(hh, g)] = (vi, k0)
            return grp_all[(hh, g)]

        def prefetch(hh, kb):
            g = kb // VB_BLK
            if kb != max(kb_lo_of(hh), g * VB_BLK):
                return
            nxt = (g + 1) * VB_BLK
            if nxt < nkb:
                get_grp(hh, nxt)
            elif hh + 1 < 8:
                get_grp(hh + 1, kb_lo_of(hh + 1))

        items = [(hh, kb) for hh in range(8) for kb in range(kb_lo_of(hh), nkb)]
        st = {}

        def s1(idx):
            h, kb = items[idx]
            c = h // 2
            pb0 = (h % 2) * 64
            use_pos = h < 3
            vi, k0 = get_grp(h, kb)
            prefetch(h, kb)
            di = kb - tb
            q0 = 128 * di if di > 0 else 0
            pair = idx % 2
            kl = (kb - k0) * 128
            diag = di >= 0
            for m in range(2):
                pb = pb0 + 32 * m
                sbank = banks[2 * pair + m]
                tp = (96, 0) if pb == 96 else None
                mm(sbank.r, sbank.t[:, q0:T], kbuf[vi][pb:pb + 32, kl:kl + 128], QT[pb:pb + 32, c, q0:T], True, not (use_pos or diag),
                   [kbuf[vi].r, QT.r], tp=tp)
            if use_pos:
                for m in range(2):
                    pb = pb0 + 32 * m
                    sbank = banks[2 * pair + m]
                    tp = (96, 0) if pb == 96 else None
                    mm(sbank.r, sbank.t[:, q0:T], slopes2[pb:pb + 2, h * 128:(h + 1) * 128], qpos[pb:pb + 2, q0:T], False, not diag,
                       [slopes2.r, qpos.r], tp=tp)
            if diag:
                for m in range(2):
                    sbank = banks[2 * pair + m]
                    mm(sbank.r, sbank.t[:, q0:q0 + 128], identb[:], umb[:], False, True, [identb.r, umb.r])
            st[idx] = (vi, k0, q0, pair, di)

        def s2(idx):
            h, kb = items[idx]
            vi, k0, q0, pair, di = st[idx]
            p_ = PT[pt_i[0] % 3]
            pt_i[0] += 1
            st[idx] = (vi, k0, q0, pair, di, p_)
            btab = biasP if kb < NPT * 4 else biasM
            jidx = (tb - kb) + 3
            src = psum_all[:, pair * 1024:(pair + 1) * 1024].rearrange("p (m n) -> p m n", m=2)[:, :, q0:T]
            b0, b1 = banks[2 * pair], banks[2 * pair + 1]
            act(p_[:, :, q0:T], src, AF.Exp, [b0.r, b1.r, btab.r], [p_.r], excl=[b0.r, b1.r],
                bias=btab[:, h * NJ + jidx:h * NJ + jidx + 1])

        def s3(idx):
            h, kb = items[idx]
            vi, k0, q0, pair, di, p_ = st[idx]
            first = kb == kb_lo_of(h)
            for m in range(2):
                ob = obase + m
                for sq in range(max(di, 0), 4):
                    mm(banks[ob].r, banks[ob].t[:, sq * 65:(sq + 1) * 65], p_[:, m, sq * 128:(sq + 1) * 128], vbuf[vi][:, kb - k0, :],
                       first and sq == 0, kb == nkb - 1, [p_.r, vbuf[vi].r], sgc=True)

        ott = junk[:, 0:520].rearrange("p (m s v) -> p m s v", m=2, s=4)
        otmp = junk[:, 1792:2048].rearrange("p (s v) -> p s v", s=4)
        sq_ = junk[:, 520:776].rearrange("p (s v) -> p s v", s=4)
        tn_ = junk[:, 776:1032].rearrange("p (s v) -> p s v", s=4)

        def epi0(h):
            for m in range(2):
                ob = obase + m
                cp("act", ott[:, m], banks[ob].t[:, 0:4 * 65].rearrange("p (s v) -> p s v", s=4), [banks[ob].r], [junk.r], excl=[banks[ob].r])

        def epi1(h):
            o1 = ott[:, 0]
            o2 = ott[:, 1]
            S.add("dve", lambda e: e.reciprocal(out=stat[:, 0:4], in_=o1[:, :, 64]), reads=[junk.r], writes=[stat.r])
            S.add("dve", lambda e: e.reciprocal(out=stat[:, 4:8], in_=o2[:, :, 64]), reads=[junk.r], writes=[stat.r])
            ts("dve", stat[:, 4:8], stat[:, 4:8], lam[:, 3:4], None, ALU.mult, None, [stat.r, lam.r], [stat.r])
            for s in range(4):
                ts("dve", junk[:, 1536 + s * 64:1600 + s * 64], o2[:, s, 0:64], stat[:, 4 + s:5 + s], None, ALU.mult, None,
                   [junk.r, stat.r], [junk.r])
                stt(otmp[:, s, :], o1[:, s, 0:64], stat[:, s:s + 1], junk[:, 1536 + s * 64:1600 + s * 64], ALU.mult, ALU.subtract,
                    [stat.r, junk.r], [junk.r])
            tt("dve", sq_, otmp, otmp, ALU.mult, [junk.r], [junk.r])
            S.add("dve", lambda e: e.tensor_reduce(out=stat[:, 8:12], in_=sq_, axis=AX.X, op=ALU.add), reads=[junk.r], writes=[stat.r])

        def epi2(h):
            act(stat[:, 8:12], stat[:, 8:12], AF.Ln, [stat.r], [stat.r], scale=1.0 / 64, bias=EPS)
            act(stat[:, 8:12], stat[:, 8:12], AF.Exp, [stat.r], [stat.r], scale=-0.5)

        def epi3(h):
            tt("dve", tn_, otmp, bc(stat[:, 8:12].unsqueeze(2), [128, 4, 64]), ALU.mult, [junk.r, stat.r], [junk.r])
            tt("dve", mixed[:, :, h * 64:(h + 1) * 64], tn_, bc(gsub[:].unsqueeze(1), [128, 4, 64]), ALU.mult, [junk.r, gsub.r], [mixed.r])

        sched = {}

        def plan_epi(h, idx_last):
            for k, f in enumerate((epi0, epi1, epi2, epi3)):
                sched.setdefault(idx_last + 2 * k, []).append(lambda f=f, h=h: f(h))
        for idx, (h, kb) in enumerate(items):
            if kb == nkb - 1:
                plan_epi(h, idx)
        s1(0)
        for idx in range(len(items)):
            s2(idx)
            if idx + 1 < len(items):
                s1(idx + 1)
            s3(idx)
            for f in sched.pop(idx, []):
                f()
            yield
        for idx in sorted(sched):
            for f in sched[idx]:
                f()

    def out_proj():
        for s in range(4):
            for part, (c0, c1) in enumerate([(0, 8), (8, 12)]):
                b = 2 * (s % 2) + part
                pv = banks[b].t[:].bitcast(BF16)[:, 0:(c1 - c0) * 128].rearrange("p (c n) -> p c n", n=128)
                for c in range(c0, c1):
                    tr(banks[b].r, pv[:, c - c0, :], mixed[:, s, c * 128:(c + 1) * 128], identb[:], [mixed.r, identb.r])
                tt("dve", mixedT[:, c0:c1, s * 128:(s + 1) * 128], pv, bc(gmixT[:, c0:c1].unsqueeze(2), [128, c1 - c0, 128]), ALU.mult,
                   [banks[b].r, gmixT.r], [mixedT.r], excl=[banks[b].r])
        run_gen(down_post_gen(mixedT, 12, "wo", 3, 1.0, ctxM))

    def load_x(src, i):
        dma(xt[:], src[i * T:(i + 1) * T, :].rearrange("(s p) d -> p s d", p=128), [], [xt.r], "xt")

    xt2 = sb("xt2", [128, 4, D])
    hTB = sb("hTB", [128, 8, T], BF16)
    statA = sb("statA", [128, 16])
    statB = sb("statB", [128, 16])
    sqB = sb("sqB", [128, D], BF16)
    ctxM = mkctx(xt, hT, stat, junk[:, 0:D], junk.r, (4, 5), True)
    ctxM.passB_base = 4
    xts = [xt, xt2]
    _ysq = ytmp.t[:].bitcast(BF16)
    ctxA = [mkctx(xts[k], hT, statA, _ysq, ytmp.r, (4, 5), True) for k in range(2)]
    ctxB = [mkctx(xts[k], hTB, statB, sqB[:], sqB.r, (6, 7), True) for k in range(2)]

    def load_x(src, i, xt_):
        dma(xt_[:], src[i * T:(i + 1) * T, :].rearrange("(s p) d -> p s d", p=128), [], [xt_.r], "xt_" + xt_.r.name)

    if NPT > 0:
        load_x(xp, 0, xts[0])
        run_gen(ffn_gen("wgu1", "wd1", 0, 1, ctxA[0]))
        for i in range(NPT):
            if i == 1 or NPT == 1:
                bg.extend(late_conv)
            gb = prefix_mixer_gen(i, ctxB[i % 2])
            if i + 1 < NPT:
                load_x(xp, i + 1, xts[(i + 1) % 2])
                ga = ffn_gen("wgu1", "wd1", 0, 1, ctxA[(i + 1) % 2])
            else:
                ga = iter(())
            interleave(ga, gb, lead=3)
        while bg:
            bg.pop(0)()
        ts("dve", Sst[:].rearrange("p g n -> p (g n)"), Sst[:].rearrange("p g n -> p (g n)"), flags[:, 1:2], None, ALU.mult, None, [Sst.r, flags.r], [Sst.r])
        cp("pool", Sbf[:].rearrange("p g n -> p (g n)"), Sst[:].rearrange("p g n -> p (g n)"), [Sst.r], [Sbf.r])
        ts("dve", halo[:].rearrange("p c k -> p (c k)"), halo[:].rearrange("p c k -> p (c k)"), flags[:, 1:2], None, ALU.mult, None, [halo.r, flags.r], [halo.r])
    last = None
    lasts = {}
    if NMT > 0:
        load_x(xm, 0, xts[0])
    ctxF1 = [mkctx(xts[k], hTB, statA, sqB[:], sqB.r, (0, 1), True) for k in range(2)]
    for k in range(2):
        ctxF1[k].passB_base = 4
    for i in range(NMT):
        ctxM.xt = xts[i % 2]
        xc = xts[i % 2]
        run_gen(ffn_gen("wgu1", "wd1", 0, 1, ctxF1[i % 2], skip_prenorm=(i > 0)))
        if i + 1 < NMT:
            load_x(xm, i + 1, xts[(i + 1) % 2])
        proj_and_state(NPT + i, True)
        dt_proj()
        ga = attention_gen(NPT + i)
        gs_ = ssd_all_gen()
        n_att = att_units(NPT + i)
        ratio = max(1, n_att // 110)
        da = ds = False
        while not (da and ds):
            for _ in range(ratio):
                if not da:
                    try:
                        next(ga)
                    except StopIteration:
                        da = True
            if not ds:
                try:
                    ssd_clean[0] = bool(next(gs_))
                except StopIteration:
                    ds = True
                    ssd_clean[0] = True
        out_proj()
        hk = (lambda i=i: prenorm(0, ctxF1[(i + 1) % 2])) if i + 1 < NMT else None
        run_gen(ffn_gen("wgu2", "wd2", 2, 5, ctxM, hook=hk))
        last = dma(out[i * T:(i + 1) * T, :].rearrange("(s p) d -> p s d", p=128), xc[:], [xc.r], [Res("o")], "out%d" % (i % 2), eng="pool")
        lasts[i % 2] = last
    nsem = S.emit(final_waits=list(lasts.values()))
    return nc, (S.nops, nsem, nc.sbuf_bytes_remaining() if callable(nc.sbuf_bytes_remaining) else nc.sbuf_bytes_remaining)


def _prep_weights(inp):
    f = np.float32
    w = {}

    def gu(g, u):
        g = np.asarray(g[0], f).reshape(8, 128, NFC, 128)
        u = np.asarray(u[0], f).reshape(8, 128, NFC, 128)
        a = np.stack([g, u], 0)
        return np.ascontiguousarray(a.transpose(3, 2, 0, 1, 4)).reshape(NFC, 128, 2048)

    w["wgu1"] = gu(inp["ffn1_w_gate"], inp["ffn1_w_up"])
    w["wgu2"] = gu(inp["ffn2_w_gate"], inp["ffn2_w_up"])
    w["wd1"] = np.ascontiguousarray(np.asarray(inp["ffn1_w_down"][0], f).reshape(NFC, 128, 1024))
    w["wd2"] = np.ascontiguousarray(np.asarray(inp["ffn2_w_down"][0], f).reshape(NFC, 128, 1024))
    win = np.asarray(inp["w_in"][0], f)
    q, k, v, z, xbc, dt = np.split(win, [512, 1024, 1536, 2560, 4096], axis=1)
    fm = np.concatenate([q, k, xbc], axis=1)
    fm = fm.reshape(8, 128, 20, 128).transpose(2, 1, 0, 3)
    w["wfm"] = np.ascontiguousarray(fm).reshape(20, 128, 1024)
    tm = np.concatenate([v, z], axis=1).reshape(8, 128, 3, 512).transpose(2, 1, 0, 3)
    w["wtm"] = np.ascontiguousarray(tm).reshape(3, 128, 4096)
    w["wdt"] = np.ascontiguousarray(dt.reshape(8, 128, 16).transpose(1, 0, 2)).reshape(128, 128)
    w["wo"] = np.ascontiguousarray(np.asarray(inp["w_out"][0], f).reshape(12, 128, 1024))
    return w


def _consts(NPT, NMT):
    f = np.float32
    NBLK = (NPT + NMT) * 4
    NJ = NBLK + 3
    slopes = (2.0 ** (-8.0 * np.arange(1, 9) / 8)).astype(np.float64)
    p = np.arange(128)[:, None, None]
    j = (np.arange(NJ) - 3)[None, None, :]
    bias = slopes[None, :, None] * (p - 128.0 * j)
    bias[:, 3:, :] -= slopes[None, 3:, None] * (T - 1)
    c = {"ident": np.eye(128, dtype=f),
         "umat": np.triu(np.ones((128, 128), f)),
         "negm": (-30000.0 * np.tril(np.ones((128, 128), f), -1)).astype(f),
         "biastab": bias.reshape(128, 8 * NJ).astype(f),
         }
    sl2 = np.zeros((128, 8 * 128), f)
    qp = np.zeros((128, T), f)
    qq = np.arange(T)
    for pb in (0, 32, 64, 96):
        sl2[pb:pb + 2] = np.repeat(slopes, 128)[None, :]
        qp[pb] = -(16.0 * (qq // 16))
        qp[pb + 1] = -(qq % 16)
    c["slopes2"] = sl2
    c["qpos"] = qp
    return c


_CACHE = {}


def run(inp, NPT, NMT, nbatch, debug=False):
    key = (NPT, NMT)
    if key not in _CACHE:
        _CACHE[key] = build(NPT, NMT)
    nc, info = _CACHE[key]
    f = np.float32
    w = _prep_weights(inp)
    c = _consts(NPT, NMT)
    small = {
        "gains": np.stack([np.asarray(inp[k][0], f) for k in
                           ["ffn1_pre_g", "ffn1_post_g", "mix_pre_g", "mix_post_g", "ffn2_pre_g", "ffn2_post_g"]], 0),
        "ssm_g": np.asarray(inp["ssm_norm_g"][0], f),
        "sub_g": np.asarray(inp["attn_subln_g"][0], f),
        "lamv": np.stack([np.asarray(inp[k][0], f) for k in ["lambda_q1", "lambda_k1", "lambda_q2", "lambda_k2"]], 0),
        "conv_w": np.asarray(inp["conv_w"][0], f),
        "conv_b": np.asarray(inp["conv_b"][0], f),
        "hp": np.stack([np.asarray(inp[k][0], f) for k in ["dt_bias", "a_log", "d_skip"]], 0),
    }
    x = np.asarray(inp["x"], f)
    half = NMT * T
    in_maps = []
    for b in range(nbatch):
        for j in range(2):
            m = dict(w)
            m.update(c)
            m.update(small)
            m["xp"] = np.ascontiguousarray(x[b, 0:max(NPT, 1) * T])
            m["xm"] = np.ascontiguousarray(x[b, j * half:(j + 1) * half])
            fl = np.zeros((128, 2), f)
            fl[:, 0] = 0.0 if j == 1 else -30000.0
            fl[:, 1] = 1.0 if j == 1 else 0.0
            m["flags"] = fl
            in_maps.append(m)
    res = run_bass_kernel_spmd(nc, in_maps, core_ids=list(range(2 * nbatch)))
    outs = [r["out"] for r in res.results]
    y = np.zeros((nbatch, 2 * half, D), f)
    for b in range(nbatch):
        for j in range(2):
            y[b, j * half:(j + 1) * half] = outs[2 * b + j]
    return y


def kernel(**inputs):
    return run(inputs, 8, 8, 4)
```

```python
import numpy as np
import ml_dtypes
import concourse.bass as bass
import concourse.mybir as mybir
from concourse.bass_utils import run_bass_kernel_spmd

F32 = mybir.dt.float32
BF16 = mybir.dt.bfloat16
AF = mybir.ActivationFunctionType
ALU = mybir.AluOpType
AX = mybir.AxisListType

EPOCH = 4000
D = 1024
DFF = 2816
NFC = 22
T = 512
EPS = 1e-6
LAMBDA_INIT = 0.8 - 0.6 * 1.0
ENGS = ("pe", "act", "dve", "pool", "sp")


class Res:
    __slots__ = ("name", "w", "r")

    def __init__(self, name):
        self.name = name
        self.w = None
        self.r = []


class Op:
    __slots__ = ("eng", "fn", "deps", "signal", "cnt", "dma_key")

    def __init__(self, eng, fn):
        self.eng = eng
        self.fn = fn
        self.deps = []
        self.signal = False
        self.cnt = None
        self.dma_key = None


class Sched:
    def __init__(self, nc):
        self.nc = nc
        self.ops = {e: [] for e in ENGS}
        self.dma_cnt = {}
        self.nops = 0

    def add(self, eng, fn, reads=(), writes=(), excl=(), dma_key=None):
        op = Op(eng, fn)
        op.dma_key = dma_key
        self.nops += 1
        cand = []
        for r in reads:
            if r.w is not None:
                cand.append((r.w, True))
        for w in writes:
            if w.w is not None:
                cand.append((w.w, False))
            for x in w.r:
                cand.append((x, False))
        for w in excl:
            if w.w is not None:
                cand.append((w.w, False))
            for x in w.r:
                cand.append((x, False))
        seen = {}
        for d, raw in cand:
            if d is op:
                continue
            seen[id(d)] = (d, seen.get(id(d), (d, False))[1] or raw)
        for d, raw in seen.values():
            if d.dma_key is None and d.eng == eng:
                if eng == "pe":
                    continue
            op.deps.append(d)
        for r in reads:
            r.r.append(op)
        for w in writes:
            w.w = op
            w.r = []
        for w in excl:
            w.r.append(op)
        if dma_key is not None:
            self.dma_cnt[dma_key] = self.dma_cnt.get(dma_key, 0) + 16
            op.cnt = self.dma_cnt[dma_key]
        self.ops[eng].append(op)
        return op

    def emit(self, final_waits=()):
        nc = self.nc
        for e in ENGS:
            for op in self.ops[e]:
                for d in op.deps:
                    d.signal = True
        for op in final_waits:
            op.signal = True
        for e in ENGS:
            c = 0
            for op in self.ops[e]:
                if op.dma_key is None and op.signal:
                    c += 1
                    op.cnt = c
        sems = {}

        def sem_for(op):
            if op.dma_key is not None:
                k = ("dma", op.dma_key)
                v = op.cnt
            else:
                k = (op.eng, (op.cnt - 1) // EPOCH)
                v = (op.cnt - 1) % EPOCH + 1
            if k not in sems:
                sems[k] = nc.alloc_semaphore("s_%s_%s" % (k[0], k[1]))
            return k, sems[k], v

        with nc.Block() as block:
            def run(ename):
                def body(eng):
                    waited = {}
                    for op in self.ops[ename]:
                        need = {}
                        for d in op.deps:
                            k, s, v = sem_for(d)
                            if waited.get(k, 0) >= v:
                                continue
                            if k not in need or need[k][1] < v:
                                need[k] = (s, v)
                        for k, (s, v) in need.items():
                            eng.wait_ge(s, v)
                            waited[k] = v
                        ins = op.fn(eng)
                        if op.dma_key is not None:
                            k, s, v = sem_for(op)
                            ins.then_inc(s, 16)
                        elif op.signal:
                            k, s, v = sem_for(op)
                            ins.then_inc(s, 1)
                    if ename == "sp":
                        for op in final_waits:
                            k, s, v = sem_for(op)
                            eng.wait_ge(s, v)
                return body
            block.tensor(run("pe"))
            block.scalar(run("act"))
            block.vector(run("dve"))
            block.gpsimd(run("pool"))
            block.sync(run("sp"))
        return len(sems)


class Tl:
    def __init__(self, t, name):
        self.t = t
        self.r = Res(name)

    def __getitem__(self, k):
        return self.t[k]


def build(NPT, NMT, debug=False):
    nc = bass.Bass("TRN2", target_bir_lowering=False)
    S = Sched(nc)
    NT = NPT + NMT
    NTOK = NT * T
    NBLK = NT * 4
    NJ = NBLK + 3

    def din(name, shape, dt=F32):
        return nc.dram_tensor(name, list(shape), dt, kind="ExternalInput").ap()

    def dscr(name, shape, dt=BF16):
        return nc.dram_tensor(name, list(shape), dt, kind="Internal").ap()

    xp = din("xp", [max(NPT, 1) * T, D])
    xm = din("xm", [NMT * T, D])
    out = nc.dram_tensor("out", [NMT * T, D], F32, kind="ExternalOutput").ap()
    wshape = {"wgu1": [NFC, 128, 2048], "wd1": [NFC, 128, 1024], "wfm": [20, 128, 1024],
              "wtm": [3, 128, 4096], "wdt": [128, 128], "wo": [12, 128, 1024],
              "wgu2": [NFC, 128, 2048], "wd2": [NFC, 128, 1024]}
    wf = {k: din(k, v) for k, v in wshape.items()}
    wb = {k: dscr(k + "_bf", v) for k, v in wshape.items()}
    wres = {k: Res(k) for k in wshape}
    gains = din("gains", [6, D])
    ssm_g = din("ssm_g", [D])
    sub_g = din("sub_g", [64])
    lamv = din("lamv", [4, 32])
    conv_w = din("conv_w", [4, 1536])
    conv_b = din("conv_b", [1536])
    hp = din("hp", [3, 16])
    ident_d = din("ident", [128, 128])
    umat_d = din("umat", [128, 128])
    negm_d = din("negm", [128, 128])
    bias_d = din("biastab", [128, 8 * NJ])
    slopes_d = din("slopes2", [128, 8 * 128])
    qpos_d = din("qpos", [128, T])
    flag_d = din("flags", [128, 2])
    vscr = dscr("vscr", [8, 128, NBLK, 65])
    vres = Res("vscr")
    kscr = dscr("kscr", [8 * 64, NTOK])
    kres = Res("kscr")
    dbg = {}

    def sb(name, shape, dt=F32):
        return Tl(nc.alloc_sbuf_tensor("sb_" + name, list(shape), dt), name)

    psum_all = nc.alloc_psum_tensor("psum_all", [128, 4096], F32)

    class _B:
        def __init__(self, i):
            self.t = psum_all[:, i * 512:(i + 1) * 512]
            self.r = Res("bank%d" % i)
    banks = [_B(i) for i in range(8)]

    def bview(b, dt, shape):
        t = banks[b].t
        v = t[:] if dt == F32 else t[:].bitcast(dt)
        return v

    ktmp = [sb("ktmp%d" % i, [128, T], BF16) for i in range(2)]
    KG = 8 * 128
    kbuf = [sb("kbuf%d" % i, [128, KG], BF16) for i in range(3)]
    xt = sb("xt", [128, 4, D])
    hT = sb("hT", [128, 8, T], BF16)
    AT = sb("AT", [128, NFC, T], BF16)
    wbuf = [sb("wbuf%d" % i, [128, 2048], BF16) for i in range(4)]
    gpost = sb("gpost", [128, D])
    gpre = sb("gpre", [128, 3, 8])
    gmixT = sb("gmixT", [128, 12])
    gsub = sb("gsub", [128, 64])
    identb = sb("identb", [128, 128], BF16)
    identf = sb("identf", [128, 128])
    umf = sb("umf", [128, 128])
    umb = sb("umb", [128, 128], BF16)
    onesf = sb("onesf", [128, 128])
    biasM = sb("biasM", [128, 8 * NJ])
    biasP = sb("biasP", [128, 8 * NJ])
    slopes2 = sb("slopes2", [128, 8 * 128], BF16)
    qpos = sb("qpos", [128, T], BF16)
    flags = sb("flags", [128, 2])
    cw = sb("cw", [128, 12, 4])
    cb = sb("cb", [128, 12])
    hpb = sb("hpb", [128, 48])
    lam = sb("lam", [128, 4])
    lamt = sb("lamt", [128, 4, 32])
    halo = sb("halo", [128, 12, 3])
    Sst = sb("Sst", [128, 2, 512])
    Sbf = sb("Sbf", [128, 2, 512], BF16)
    xn = [sb("xn%d" % i, [128, D], BF16) for i in range(2)]
    junk = sb("junk", [128, 2048])
    stat = sb("stat", [128, 16])
    QT = sb("QT", [128, 4, T], BF16)
    BCT = sb("BCT", [128, 4, T], BF16)
    raw = [sb("raw%d" % i, [128, T + 3]) for i in range(2)]
    cacc = [sb("cacc%d" % i, [128, T]) for i in range(2)]
    xsT = [sb("xsT%d" % i, [128, T], BF16) for i in range(2)]
    xs_tok = sb("xs_tok", [128, 4, D], BF16)
    vtmp = [sb("vtmp%d" % i, [128, 8, 65], BF16) for i in range(2)]
    VB_BLK = 8
    vbuf = [sb("vbuf%d" % i, [128, VB_BLK, 65], BF16) for i in range(3)]
    PT = [sb("PT%d" % i, [128, 2, T], BF16) for i in range(3)]
    OT = sb("OT", [65, 2, T])
    mixed = sb("mixed", [128, 4, 1536], BF16)
    mixedT = AT
    dtt = sb("dtt", [128, 96])
    Dg = sb("Dg", [128, 8, 128])
    Eb = sb("Eb", [128, 8, 128], BF16)
    Lb = sb("Lb", [128, 8, 128], BF16)
    MTb = sb("MTb", [128, 8, 128], BF16)
    Cdec = sb("Cdec", [128, 8, 128], BF16)
    cbm = sb("cbm", [128, 128], BF16)
    Btok = sb("Btok", [128, 128], BF16)
    Xg = sb("Xg", [128, 8, 64], BF16)
    Xdec = sb("Xdec", [128, 8, 64], BF16)
    ytmp = sb("ytmp", [128, 512])

    dma_rr = [0]

    def dma(outap, inap, reads, writes, key, eng="sp", **kw):
        return S.add(eng, lambda e: e.dma_start(out=outap, in_=inap, **kw), reads=reads, writes=writes, dma_key=key)

    def mm(bank_r, outap, lhsT, rhs, start, stop, reads, tp=None, sgc=False):
        if sgc:
            return S.add("pe", lambda e: e.matmul(outap, lhsT=lhsT, rhs=rhs, start=start, stop=stop, skip_group_check=True),
                         reads=reads, writes=[bank_r])
        if tp is not None:
            return S.add("pe", lambda e: e.matmul(outap, lhsT=lhsT, rhs=rhs, start=start, stop=stop, tile_position=tp),
                         reads=reads, writes=[bank_r])
        return S.add("pe", lambda e: e.matmul(outap, lhsT=lhsT, rhs=rhs, start=start, stop=stop),
                     reads=reads, writes=[bank_r])

    def tr(bank_r, outap, inap, ident, reads):
        return S.add("pe", lambda e: e.transpose(out=outap, in_=inap, identity=ident), reads=reads, writes=[bank_r])

    def act(outap, inap, func, reads, writes, excl=(), **kw):
        return S.add("act", lambda e: e.activation(out=outap, in_=inap, func=func, **kw), reads=reads, writes=writes, excl=excl)

    def tt(eng, outap, in0, in1, op, reads, writes, excl=()):
        return S.add(eng, lambda e: e.tensor_tensor(out=outap, in0=in0, in1=in1, op=op), reads=reads, writes=writes, excl=excl)

    def ts(eng, outap, in0, s1, s2, op0, op1, reads, writes, excl=()):
        if op1 is None:
            return S.add(eng, lambda e: e.tensor_scalar(out=outap, in0=in0, scalar1=s1, scalar2=None, op0=op0),
                         reads=reads, writes=writes, excl=excl)
        return S.add(eng, lambda e: e.tensor_scalar(out=outap, in0=in0, scalar1=s1, scalar2=s2, op0=op0, op1=op1),
                     reads=reads, writes=writes, excl=excl)

    def stt(outap, in0, scalar, in1, op0, op1, reads, writes, excl=()):
        return S.add("dve", lambda e: e.scalar_tensor_tensor(out=outap, in0=in0, scalar=scalar, in1=in1, op0=op0, op1=op1),
                     reads=reads, writes=writes, excl=excl)

    def cp(eng, outap, inap, reads, writes, excl=()):
        if eng == "act":
            return S.add("act", lambda e: e.copy(out=outap, in_=inap), reads=reads, writes=writes, excl=excl)
        return S.add(eng, lambda e: e.tensor_copy(out=outap, in_=inap), reads=reads, writes=writes, excl=excl)

    def memset(eng, tl, ap, val):
        return S.add(eng, lambda e: e.memset(ap, val), writes=[tl.r])

    def bc(ap, shape):
        return ap.broadcast_to(shape)

    dma(identb[:], ident_d, [], [identb.r], "c_idb", eng="pool")
    dma(umb[:], negm_d, [], [umb.r], "c_umb", eng="pool")
    dma(slopes2[:], slopes_d, [], [slopes2.r], "c_sl", eng="pool")
    dma(qpos[:], qpos_d, [], [qpos.r], "c_qp", eng="pool")
    memset("pool", gmixT, gmixT[:], 1.0)
    memset("pool", onesf, onesf[:], 1.0)
    memset("pool", halo, halo[:], 0.0)
    memset("pool", Sst, Sst[:], 0.0)
    memset("pool", Sbf, Sbf[:], 0.0)
    for i in range(2):
        memset("pool", vtmp[i], vtmp[i][:], 1.0)
    wgu1_res = {}
    bg = []

    def conv_thunks(k):
        th = []
        n0 = wshape[k][0]
        if len(wshape[k]) == 2:
            th.append(lambda: dma(wb[k], wf[k], [], [wres[k]], "cv_" + k, eng="pool"))
            return th
        step = 4096 // wshape[k][2]
        for i in range(0, n0, step):
            j = min(n0, i + step)
            if k == "wgu1":
                wgu1_res[i] = Res("wgu1_%d" % i)
                th.append(lambda i=i, j=j: dma(wb[k][i:j].rearrange("a p n -> p a n"), wf[k][i:j].rearrange("a p n -> p a n"), [],
                                               [wgu1_res[i], wres[k]], "cv_%s_%d" % (k, i), eng="pool"))
            else:
                th.append(lambda i=i, j=j: dma(wb[k][i:j].rearrange("a p n -> p a n"), wf[k][i:j].rearrange("a p n -> p a n"), [],
                                               [wres[k]], "cv_" + k, eng="pool"))
        return th
    for k in ["wgu1", "wd1", "wfm", "wtm", "wdt"]:
        for f in conv_thunks(k):
            f()
    late_conv = []
    for k in ["wo", "wgu2", "wd2"]:
        late_conv.extend(conv_thunks(k))
    if NPT == 0:
        for f in late_conv:
            f()
        late_conv = []
    dma(identf[:], ident_d, [], [identf.r], "c0")
    dma(umf[:], umat_d, [], [umf.r], "c1")
    dma(biasM[:], bias_d, [], [biasM.r], "c2")
    dma(flags[:], flag_d, [], [flags.r], "c3")
    dma(gpre[:, 0, :], gains[0].rearrange("(c p) -> p c", p=128), [], [gpre.r], "c4", allow_slow_non_contiguous=True)
    dma(gpre[:, 1, :], gains[2].rearrange("(c p) -> p c", p=128), [], [gpre.r], "c4", allow_slow_non_contiguous=True)
    dma(gpre[:, 2, :], gains[4].rearrange("(c p) -> p c", p=128), [], [gpre.r], "c4", allow_slow_non_contiguous=True)
    dma(gmixT[:, 4:12], ssm_g.rearrange("(c p) -> p c", p=128), [], [gmixT.r], "c5", allow_slow_non_contiguous=True)
    dma(gsub[:], sub_g.partition_broadcast(128), [], [gsub.r], "c6")
    for k_ in range(4):
        dma(cw[:, :, k_], conv_w[k_].rearrange("(c p) -> p c", p=128), [], [cw.r], "c7", allow_slow_non_contiguous=True)
    dma(cb[:], conv_b.rearrange("(c p) -> p c", p=128), [], [cb.r], "c8", allow_slow_non_contiguous=True)
    dma(hpb[:], hp.rearrange("a b -> (a b)").partition_broadcast(128), [], [hpb.r], "c9")
    dma(lamt[:], lamv.rearrange("a b -> (a b)").partition_broadcast(128), [], [lamt.r], "c10")
    ts("dve", biasP[:], biasM[:], flags[:, 0:1], None, ALU.add, None, [biasM.r, flags.r], [biasP.r])
    ts("dve", gsub[:], gsub[:], 1.0 - LAMBDA_INIT, None, ALU.mult, None, [gsub.r], [gsub.r])
    act(hpb[:, 16:32], hpb[:, 16:32], AF.Exp, [hpb.r], [hpb.r])
    ts("dve", hpb[:, 16:32], hpb[:, 16:32], -1.0, None, ALU.mult, None, [hpb.r], [hpb.r])
    tt("dve", junk[:, 0:32], lamt[:, 0, :], lamt[:, 1, :], ALU.mult, [lamt.r], [junk.r])
    tt("dve", junk[:, 32:64], lamt[:, 2, :], lamt[:, 3, :], ALU.mult, [lamt.r], [junk.r])
    S.add("dve", lambda e: e.tensor_reduce(out=lam[:, 0:2], in_=junk[:, 0:64].rearrange("p (a b) -> p a b", a=2), axis=AX.X, op=ALU.add),
          reads=[junk.r], writes=[lam.r])
    act(lam[:, 0:2], lam[:, 0:2], AF.Exp, [lam.r], [lam.r])
    tt("dve", lam[:, 2:3], lam[:, 0:1], lam[:, 1:2], ALU.subtract, [lam.r], [lam.r])
    ts("dve", lam[:, 3:4], lam[:, 2:3], LAMBDA_INIT, None, ALU.add, None, [lam.r], [lam.r])

    wslot = [0]
    wslotB = [0]
    wbufB = [sb("wbufB%d" % i, [128, 2048], BF16) for i in range(2)]

    def wload(src_ap, ncols, wr):
        i = wslot[0] % 4
        wslot[0] += 1
        assert ncols <= 2048
        dma(wbuf[i][:, 0:ncols], src_ap, [wr], [wbuf[i].r], "wb%d" % i)
        return wbuf[i]

    def rstd_from_ss(ssap, n, mult, reads_extra=()):
        pass

    class Ctx:
        pass

    def mkctx(xt_, hT_, stat_, sq_ap, sq_r, tb, two_pass):
        c = Ctx()
        c.xt, c.hT, c.stat, c.sq, c.sq_r, c.tb, c.two_pass = xt_, hT_, stat_, sq_ap, sq_r, tb, two_pass
        return c

    def prenorm(gi, cx=None):
        cx = cx or ctxM
        xt_, hT_, st_ = cx.xt, cx.hT, cx.stat
        for s in range(4):
            act(cx.sq, xt_[:, s, :], AF.Square, [xt_.r], [cx.sq_r, st_.r], accum_out=st_[:, s:s + 1])
        act(st_[:, 0:4], st_[:, 0:4], AF.Sqrt, [st_.r], [st_.r], scale=1.0 / D, bias=EPS)
        S.add("dve", lambda e: e.reciprocal(out=st_[:, 0:4], in_=st_[:, 0:4]), reads=[st_.r], writes=[st_.r])
        for s in range(4):
            x_ = xn[s % 2]
            ts("dve", x_[:], xt_[:, s, :], st_[:, s:s + 1], None, ALU.mult, None, [xt_.r, st_.r], [x_.r])
            b = cx.tb[s % 2]
            pv = banks[b].t[:].bitcast(BF16).rearrange("p (c n) -> p c n", c=8)
            for c in range(8):
                tr(banks[b].r, pv[:, c, :], x_[:, c * 128:(c + 1) * 128], identb[:], [x_.r, identb.r])
            tt("dve", hT_[:, :, s * 128:(s + 1) * 128], pv, bc(gpre[:, gi, :].unsqueeze(2), [128, 8, 128]), ALU.mult,
               [banks[b].r, gpre.r], [hT_.r], excl=[banks[b].r])

    def postnorm_residual(cmul, cx, subs, bank_of):
        xt_, st_ = cx.xt, cx.stat
        for s in subs:
            for half in range(2):
                b = bank_of(s, half)
                act(junk[:, half * 512:(half + 1) * 512], banks[b].t[:], AF.Square, [banks[b].r], [junk.r, st_.r], excl=[banks[b].r],
                    accum_out=st_[:, 4 + 2 * s + half:5 + 2 * s + half])
        s0, s1 = subs[0], subs[-1] + 1
        tt("dve", st_[:, 12 + s0:12 + s1], st_[:, 4 + 2 * s0:4 + 2 * s1:2], st_[:, 5 + 2 * s0:5 + 2 * s1:2], ALU.add, [st_.r], [st_.r])
        act(st_[:, 12 + s0:12 + s1], st_[:, 12 + s0:12 + s1], AF.Sqrt, [st_.r], [st_.r], scale=1.0 / D, bias=EPS)
        S.add("dve", lambda e: e.reciprocal(out=st_[:, 12 + s0:12 + s1], in_=st_[:, 12 + s0:12 + s1]), reads=[st_.r], writes=[st_.r])
        if cmul != 1.0:
            ts("dve", st_[:, 12 + s0:12 + s1], st_[:, 12 + s0:12 + s1], cmul, None, ALU.mult, None, [st_.r], [st_.r])
        for s in subs:
            for half in range(2):
                b = bank_of(s, half)
                hs = slice(half * 512, (half + 1) * 512)
                jj = junk[:, 1024 + half * 512:1536 + half * 512]
                stt(jj, banks[b].t[:], st_[:, 12 + s:13 + s], gpost[:, hs], ALU.mult, ALU.mult,
                    [banks[b].r, st_.r, gpost.r], [junk.r], excl=[banks[b].r])
                tt(getattr(cx, "add_eng", "pool"), xt_[:, s, hs], xt_[:, s, hs], jj, ALU.add, [xt_.r, junk.r], [xt_.r])

    def down_post_gen(src, nchunks, wkey, gidx, cmul, cx, hook=None):
        dma(gpost[:], gains[gidx].partition_broadcast(128), [], [gpost.r], "gpost")
        if cx.two_pass:
            pb_ = getattr(cx, "passB_base", 0)
            passes = [([0, 1], lambda s, half: 2 * (s % 2) + half), ([2, 3], lambda s, half: pb_ + 2 * (s % 2) + half)]
        else:
            passes = [([0, 1, 2, 3], lambda s, half: 2 * s + half)]
        for subs, bank_of in passes:
            for c0 in range(0, nchunks, 2):
                c1 = min(nchunks, c0 + 2)
                wsl = wload(wb[wkey][c0:c1].rearrange("a p n -> p a n"), (c1 - c0) * 1024, wres[wkey])
                wv = wsl.t[:].rearrange("p (a n) -> p a n", n=1024)
                for c in range(c0, c1):
                    for s in subs:
                        for half in range(2):
                            b = bank_of(s, half)
                            mm(banks[b].r, banks[b].t[:], src[:, c, s * 128:(s + 1) * 128], wv[:, c - c0, half * 512:(half + 1) * 512],
                               c == 0, c == nchunks - 1, [src.r, wsl.r])
                if c0 % 4 == 2 or c1 == nchunks:
                    yield
            if hook is not None and subs[-1] == 3:
                hook()
            postnorm_residual(cmul, cx, subs, bank_of)
            yield

    first_ffn = [True]

    def ffn_gen(wgu, wd, gpre_i, gpost_i, cx, skip_prenorm=False, hook=None):
        if not skip_prenorm:
            prenorm(gpre_i, cx)
        yield
        for f0 in range(0, NFC, 2):
            for fc in range(f0, f0 + 2):
                wsl = wload(wb[wgu][fc], 2048, wgu1_res[f0] if wgu == "wgu1" else wres[wgu])
                wv = wsl.t[:].rearrange("p (g k n) -> p g k n", g=2, k=8)
                bg = (fc % 2) * 2
                for g in range(2):
                    for kc in range(8):
                        mm(banks[bg + g].r, banks[bg + g].t[:], wv[:, g, kc, :], cx.hT[:, kc, :], kc == 0, kc == 7, [wsl.r, cx.hT.r])
                jg = junk[:, (fc % 2) * 512:(fc % 2 + 1) * 512]
                act(jg, banks[bg].t[:], AF.Silu, [banks[bg].r], [junk.r], excl=[banks[bg].r])
                tt("dve", AT[:, fc, :], jg, banks[bg + 1].t[:], ALU.mult, [junk.r, banks[bg + 1].r], [AT.r], excl=[banks[bg + 1].r])
            yield
        first_ffn[0] = False
        for _ in down_post_gen(AT, NFC, wd, gpost_i, 0.5, cx, hook=hook):
            yield

    def run_gen(g):
        for _ in g:
            pass

    def interleave(ga, gb, lead=0):
        da = db = False
        for _ in range(lead):
            try:
                next(ga)
            except StopIteration:
                da = True
                break
        while not (da and db):
            if bg:
                bg.pop(0)()
            if not da:
                try:
                    next(ga)
                except StopIteration:
                    da = True
            if not db:
                try:
                    next(gb)
                except StopIteration:
                    db = True

    def ffn(wgu, wd, gpre_i, gpost_i):
        run_gen(ffn_gen(wgu, wd, gpre_i, gpost_i, ctxM))

    def prefix_mixer_gen(tile_abs, cx):
        tok0 = tile_abs * T
        hT_ = cx.hT
        prenorm(1, cx)
        yield
        chunks = list(range(4, 20))
        wslB = {}

        def loadB(pi):
            if pi * 2 < len(chunks) and pi not in wslB:
                c0 = chunks[pi * 2]
                i_ = wslotB[0] % 2
                wslotB[0] += 1
                dma(wbufB[i_][:, 0:2048], wb["wfm"][c0:c0 + 2].rearrange("a p n -> p a n"), [wres["wfm"]], [wbufB[i_].r], "wbB%d" % i_)
                wslB[pi] = wbufB[i_]
        loadB(0)
        pend = []

        def flush():
            for f in pend:
                f()
            del pend[:]
        for ci, c in enumerate(chunks):
            if ci % 2 == 0:
                loadB(ci // 2 + 1)
            wsl = wslB[ci // 2]
            wv = wsl.t[:, 0:2048].rearrange("p (a k n) -> p a k n", k=8, n=128)
            a = ci % 2
            b = 6 + ci % 2
            for kc in range(8):
                mm(banks[b].r, banks[b].t[:], wv[:, a, kc, :], hT_[:, kc, :], kc == 0, kc == 7, [wsl.r, hT_.r])
            flush()
            if c < 8:
                kt_ = ktmp[c % 2]
                cp("act", kt_[:], banks[b].t[:], [banks[b].r], [kt_.r], excl=[banks[b].r])
                dma(kscr[(c - 4) * 128:(c - 3) * 128, tok0:tok0 + T], kt_[:], [kt_.r], [kres], "kw", eng="act")
            else:
                x = c - 8
                rw = raw[x % 2]
                ca = cacc[x % 2]
                cp("pool", rw[:, 0:3], halo[:, x, :], [halo.r], [rw.r])
                cp("act", rw[:, 3:T + 3], banks[b].t[:], [banks[b].r], [rw.r], excl=[banks[b].r])
                ts("dve", ca[:], rw[:, 0:T], cw[:, x, 0:1], cb[:, x:x + 1], ALU.mult, ALU.add, [rw.r, cw.r, cb.r], [ca.r])
                for k in range(1, 4):
                    stt(ca[:], rw[:, k:k + T], cw[:, x, k:k + 1], ca[:], ALU.mult, ALU.add, [rw.r, cw.r, ca.r], [ca.r])
                cp("pool", halo[:, x, :], rw[:, T:T + 3], [rw.r], [halo.r])
                if x < 8:
                    xo = xsT[x % 2]
                    act(xo[:], ca[:], AF.Silu, [ca.r], [xo.r])
                    def _trs(x=x, xo=xo, bt=6 + ci % 2):
                        pv = banks[bt].t[:].bitcast(BF16)[:, 0:512].rearrange("p (s n) -> p s n", s=4)
                        for s in range(4):
                            tr(banks[bt].r, pv[:, s, :], xo[:, s * 128:(s + 1) * 128], identb[:], [xo.r, identb.r])
                        cp("dve", xs_tok[:, :, x * 128:(x + 1) * 128], pv, [banks[bt].r], [xs_tok.r], excl=[banks[bt].r])
                    pend.append(_trs)
                elif x < 10:
                    act(BCT[:, x - 8, :], ca[:], AF.Silu, [ca.r], [BCT.r])
            yield
        flush()
        wvs = []
        for hf in range(2):
            i_ = wslotB[0] % 2
            wslotB[0] += 1
            dma(wbufB[i_][:, 0:2048].rearrange("p (k n) -> p k n", k=8), wb["wtm"][0].rearrange("p (k n) -> p k n", k=8)[:, :, hf * 256:(hf + 1) * 256],
                [wres["wtm"]], [wbufB[i_].r], "wbB%d" % i_)
            wvs.append(wbufB[i_])
        for s in range(4):
            b = 6 + s % 2
            for hf in range(2):
                wv = wvs[hf].t[:, 0:2048].rearrange("p (k n) -> p k n", k=8)
                for kc in range(8):
                    mm(banks[b].r, banks[b].t[:, hf * 256:(hf + 1) * 256], hT_[:, kc, s * 128:(s + 1) * 128], wv[:, kc, :], kc == 0, kc == 7,
                       [wvs[hf].r, hT_.r])
            vt = vtmp[s % 2]
            cp("act", vt[:, :, 0:64], banks[b].t[:].rearrange("p (h v) -> p h v", h=8), [banks[b].r], [vt.r], excl=[banks[b].r])
            blk = tile_abs * 4 + s
            dma(vscr[:, :, blk, :].rearrange("h p v -> p h v"), vt[:], [vt.r], [vres], "vw", eng="act")
            yield
        dt_proj(hT_, 6)
        yield
        for s in range(4):
            for _ in ssd_chunk(s, False, (6, 7, (6, 7))):
                yield

    def proj_and_state(tile_abs, is_main):
        tok0 = tile_abs * T
        prenorm(1)
        chunks = (list(range(0, 4)) if is_main else []) + list(range(4, 20))
        wslM = {}

        def st0(ci):
            if ci % 2 == 0:
                grp = chunks[ci:ci + 2]
                wslM[ci // 2] = wload(wb["wfm"][grp[0]:grp[0] + len(grp)].rearrange("a p n -> p a n"), len(grp) * 1024, wres["wfm"])
            wsl = wslM[ci // 2]
            wv = wsl.t[:].rearrange("p (a k n) -> p a k n", k=8, n=128)
            b = ci % 4
            for kc in range(8):
                mm(banks[b].r, banks[b].t[:], wv[:, ci % 2, kc, :], hT[:, kc, :], kc == 0, kc == 7, [wsl.r, hT.r])

        def st1(ci):
            c = chunks[ci]
            b = ci % 4
            if c < 4:
                act(QT[:, c, :], banks[b].t[:], AF.Copy, [banks[b].r], [QT.r], excl=[banks[b].r], scale=32.0 ** -0.5)
            elif c < 8:
                kt_ = ktmp[c % 2]
                cp("act", kt_[:], banks[b].t[:], [banks[b].r], [kt_.r], excl=[banks[b].r])
            else:
                x = c - 8
                rw = raw[x % 2]
                cp("pool", rw[:, 0:3], halo[:, x, :], [halo.r], [rw.r])
                cp("act", rw[:, 3:T + 3], banks[b].t[:], [banks[b].r], [rw.r], excl=[banks[b].r])

        def st2(ci):
            c = chunks[ci]
            if 4 <= c < 8:
                kt_ = ktmp[c % 2]
                dma(kscr[(c - 4) * 128:(c - 3) * 128, tok0:tok0 + T], kt_[:], [kt_.r], [kres], "kw", eng="act")
            elif c >= 8:
                x = c - 8
                rw = raw[x % 2]
                ca = cacc[x % 2]
                ts("dve", ca[:], rw[:, 0:T], cw[:, x, 0:1], cb[:, x:x + 1], ALU.mult, ALU.add, [rw.r, cw.r, cb.r], [ca.r])
                for k in range(1, 4):
                    stt(ca[:], rw[:, k:k + T], cw[:, x, k:k + 1], ca[:], ALU.mult, ALU.add, [rw.r, cw.r, ca.r], [ca.r])
                cp("pool", halo[:, x, :], rw[:, T:T + 3], [rw.r], [halo.r])

        def st3(ci):
            c = chunks[ci]
            if c >= 8:
                x = c - 8
                ca = cacc[x % 2]
                if x < 8:
                    act(xsT[x % 2][:], ca[:], AF.Silu, [ca.r], [xsT[x % 2].r])
                else:
                    act(BCT[:, x - 8, :], ca[:], AF.Silu, [ca.r], [BCT.r])

        def st4(ci):
            c = chunks[ci]
            if 8 <= c < 16:
                x = c - 8
                xo = xsT[x % 2]
                bt = 6 + (x % 2)
                pv = banks[bt].t[:].bitcast(BF16)[:, 0:512].rearrange("p (s n) -> p s n", s=4)
                for s in range(4):
                    tr(banks[bt].r, pv[:, s, :], xo[:, s * 128:(s + 1) * 128], identb[:], [xo.r, identb.r])
                cp("dve", xs_tok[:, :, x * 128:(x + 1) * 128], pv, [banks[bt].r], [xs_tok.r], excl=[banks[bt].r])
        stages = [st0, st1, st2, st3, st4]
        n = len(chunks)
        for u in range(n + len(stages) - 1):
            for k, f in enumerate(stages):
                ci = u - k
                if 0 <= ci < n:
                    f(ci)
        ngrp = 3 if is_main else 1
        for g in range(ngrp):
            wsrc = wb["wtm"][g].rearrange("p (k n) -> p k n", k=8)
            wsl2 = []
            for hf in range(2):
                i_ = wslot[0] % 4
                wslot[0] += 1
                dma(wbuf[i_][:].rearrange("p (k n) -> p k n", k=4), wsrc[:, hf * 4:(hf + 1) * 4, :], [wres["wtm"]], [wbuf[i_].r], "wb%d" % i_)
                wsl2.append(wbuf[i_])
            for s in range(4):
                if g > 0:
                    b = 4 * ((g - 1) % 2) + s
                else:
                    b = 4 + s
                for kc in range(8):
                    wsl = wsl2[kc // 4]
                    wv = wsl.t[:].rearrange("p (k n) -> p k n", k=4)
                    mm(banks[b].r, banks[b].t[:], hT[:, kc, s * 128:(s + 1) * 128], wv[:, kc % 4, :], kc == 0, kc == 7, [wsl.r, hT.r])
                if g == 0:
                    vt = vtmp[s % 2]
                    cp("act", vt[:, :, 0:64], banks[b].t[:].rearrange("p (h v) -> p h v", h=8), [banks[b].r], [vt.r], excl=[banks[b].r])
                    blk = tile_abs * 4 + s
                    dma(vscr[:, :, blk, :].rearrange("h p v -> p h v"), vt[:], [vt.r], [vres], "vw", eng="act")
                else:
                    zc = slice((g - 1) * 512, g * 512)
                    act(zs_all[s][:, zc], banks[b].t[:], AF.Silu, [banks[b].r], [zs_all[s].r], excl=[banks[b].r])
        return tok0

    class _V:
        def __init__(self, ap, r):
            self.ap = ap
            self.r = r

        def __getitem__(self, k):
            return self.ap[k]
    _atf = AT.t[:].rearrange("p c n -> p (c n)").bitcast(F32)
    zs_all = [_V(_atf[:, i * D:(i + 1) * D], AT.r) for i in range(4)]
    ao = _V(xs_tok.t[:].rearrange("p s n -> p (s n)").bitcast(F32).rearrange("p (s n) -> p s n", s=4), xs_tok.r)
    dt_all = sb("dt_all", [128, 4, 16])

    def dt_proj(hT_=None, b=3):
        if hT_ is None:
            hT_ = hT
            wsl = wload(wb["wdt"], 128, wres["wdt"])
        else:
            i_ = wslotB[0] % 2
            wslotB[0] += 1
            dma(wbufB[i_][:, 0:128], wb["wdt"], [wres["wdt"]], [wbufB[i_].r], "wbB%d" % i_)
            wsl = wbufB[i_]
        wv = wsl.t[:, 0:128].rearrange("p (k n) -> p k n", k=8)
        for s in range(4):
            for kc in range(8):
                mm(banks[b].r, banks[b].t[:, s * 16:(s + 1) * 16], hT_[:, kc, s * 128:(s + 1) * 128], wv[:, kc, :], kc == 0, kc == 7, [wsl.r, hT_.r])
        tt("dve", dt_all[:], banks[b].t[:, 0:64].rearrange("p (s j) -> p s j", s=4), bc(hpb[:, 0:16].unsqueeze(1), [128, 4, 16]), ALU.add,
           [banks[b].r, hpb.r], [dt_all.r], excl=[banks[b].r])
        act(dt_all[:], dt_all[:], AF.Exp, [dt_all.r], [dt_all.r])
        act(dt_all[:], dt_all[:], AF.Ln, [dt_all.r], [dt_all.r], bias=1.0)

    def ssd_chunk(s, is_main, bk=(3, 2, (6, 7))):
        cs = slice(s * 128, (s + 1) * 128)
        dt_ = dt_all[:, s, :]
        A_ = dtt[:, 16:32]
        tt("dve", A_, dt_, hpb[:, 16:32], ALU.mult, [dt_all.r, hpb.r], [dtt.r])
        b = bk[0]
        mm(banks[b].r, banks[b].t[:, 64:80], umf[:], A_, True, True, [umf.r, dtt.r])
        mm(banks[b].r, banks[b].t[:, 80:96], onesf[:], A_, True, True, [onesf.r, dtt.r])
        cp("dve", dtt[:, 32:64], banks[b].t[:, 64:96], [banks[b].r], [dtt.r], excl=[banks[b].r])
        acs = dtt[:, 32:48]
        tot = dtt[:, 48:64]
        tt("dve", dtt[:, 64:80], tot, acs, ALU.subtract, [dtt.r], [dtt.r])
        act(dtt[:, 64:80], dtt[:, 64:80], AF.Exp, [dtt.r], [dtt.r])
        act(dtt[:, 80:96], tot, AF.Exp, [dtt.r], [dtt.r])
        for g in range(2):
            gs = slice(8 * g, 8 * g + 8)
            xsg = xs_tok[:, s, g * 512:(g + 1) * 512].rearrange("p (j d) -> p j d", j=8)
            BTg = BCT[:, g, cs]
            CTg = BCT[:, 2 + g, cs]
            tt("dve", Xg[:], xsg, bc(dt_all[:, s, gs].unsqueeze(2), [128, 8, 64]), ALU.mult, [xs_tok.r, dt_all.r], [Xg.r])
            if is_main:
                tt("dve", Dg[:], bc(umf[:].unsqueeze(1), [128, 8, 128]), bc(dtt[:, 16 + 8 * g:24 + 8 * g].unsqueeze(2), [128, 8, 128]), ALU.mult,
                   [umf.r, dtt.r], [Dg.r])
                Dv = Dg[:].rearrange("p j l -> p (j l)")
                for hh in range(2):
                    mm(banks[hh].r, banks[hh].t[:], onesf[:], Dv[:, hh * 512:(hh + 1) * 512], True, True, [onesf.r, Dg.r])
                for hh in range(2):
                    act(Eb[:, hh * 4:(hh + 1) * 4, :], banks[hh].t[:].rearrange("p (j l) -> p j l", j=4), AF.Exp, [banks[hh].r], [Eb.r], excl=[banks[hh].r])
                tt("dve", Cdec[:], Eb[:], bc(CTg.unsqueeze(1), [128, 8, 128]), ALU.mult, [Eb.r, BCT.r], [Cdec.r])
                for j in range(8):
                    hh = j // 4
                    ts("dve", Dg[:, j, :], banks[hh].t[:, (j % 4) * 128:(j % 4 + 1) * 128], dtt[:, 32 + 8 * g + j:33 + 8 * g + j], 0.0,
                       ALU.subtract, ALU.min, [banks[hh].r, dtt.r], [Dg.r], excl=[banks[hh].r])
                act(Lb[:], Dg[:], AF.Exp, [Dg.r], [Lb.r])
                mm(banks[2].r, banks[2].t[:, 0:128], BTg, CTg, True, True, [BCT.r])
                tt("dve", cbm[:], banks[2].t[:, 0:128], umf[:], ALU.mult, [banks[2].r, umf.r], [cbm.r], excl=[banks[2].r])
                tt("dve", MTb[:], Lb[:], bc(cbm[:].unsqueeze(1), [128, 8, 128]), ALU.mult, [Lb.r, cbm.r], [MTb.r])
                yb = 4 + g
                for j in range(8):
                    mm(banks[yb].r, banks[yb].t[:, j * 64:(j + 1) * 64], MTb[:, j, :], Xg[:, j, :], True, False, [MTb.r, Xg.r])
                    mm(banks[yb].r, banks[yb].t[:, j * 64:(j + 1) * 64], Cdec[:, j, :], Sbf[:, g, j * 64:(j + 1) * 64], False, True, [Cdec.r, Sbf.r])
                tt("dve", ytmp[:].rearrange("p (j d) -> p j d", j=8), xsg, bc(hpb[:, 32 + 8 * g:40 + 8 * g].unsqueeze(2), [128, 8, 64]), ALU.mult,
                   [xs_tok.r, hpb.r], [ytmp.r])
                tt("dve", ytmp[:], ytmp[:], banks[yb].t[:], ALU.add, [ytmp.r, banks[yb].r], [ytmp.r], excl=[banks[yb].r])
                tt("dve", ytmp[:], ytmp[:], zs_all[s][:, g * 512:(g + 1) * 512], ALU.mult, [ytmp.r, zs_all[s].r], [ytmp.r])
                act(junk[:, 1024:1536], ytmp[:], AF.Square, [ytmp.r], [junk.r, stat.r], accum_out=stat[:, g:g + 1])
                act(stat[:, g:g + 1], stat[:, g:g + 1], AF.Sqrt, [stat.r], [stat.r], scale=1.0 / 512, bias=EPS)
                S.add("dve", lambda e, g=g: e.reciprocal(out=stat[:, g:g + 1], in_=stat[:, g:g + 1]), reads=[stat.r], writes=[stat.r])
                ts("dve", mixed[:, s, 512 + g * 512:1024 + g * 512], ytmp[:], stat[:, g:g + 1], None, ALU.mult, None, [ytmp.r, stat.r], [mixed.r])
            tt("dve", Xdec[:], Xg[:], bc(dtt[:, 64 + 8 * g:72 + 8 * g].unsqueeze(2), [128, 8, 64]), ALU.mult, [Xg.r, dtt.r], [Xdec.r])
            pvb = banks[bk[1]].t[:].bitcast(BF16)[:, 512:640]
            tr(banks[bk[1]].r, pvb, BTg, identb[:], [BCT.r, identb.r])
            cp("dve", Btok[:], pvb, [banks[bk[1]].r], [Btok.r], excl=[banks[bk[1]].r])
            lb = bk[2][g]
            mm(banks[lb].r, banks[lb].t[:], Btok[:], Xdec[:].rearrange("p j d -> p (j d)"), True, True, [Btok.r, Xdec.r])
            Sg = Sst[:, g, :].rearrange("p (j d) -> p j d", j=8)
            tt("dve", Sg, Sg, bc(dtt[:, 80 + 8 * g:88 + 8 * g].unsqueeze(2), [128, 8, 64]), ALU.mult, [Sst.r, dtt.r], [Sst.r])
            tt("dve", Sst[:, g, :], Sst[:, g, :], banks[lb].t[:], ALU.add, [Sst.r, banks[lb].r], [Sst.r], excl=[banks[lb].r])
            cp("pool", Sbf[:, g, :], Sst[:, g, :], [Sst.r], [Sbf.r])
            yield

    def ssd_main_gen(s):
        cs = slice(s * 128, (s + 1) * 128)
        dt_ = dt_all[:, s, :]
        A_ = dtt[:, 16:32]
        st_ = statB
        b7, b6 = banks[7], banks[6]
        tt("dve", A_, dt_, hpb[:, 16:32], ALU.mult, [dt_all.r, hpb.r], [dtt.r])
        yield True
        mm(b7.r, b7.t[:, 64:80], umf[:], A_, True, True, [umf.r, dtt.r])
        mm(b7.r, b7.t[:, 80:96], onesf[:], A_, True, True, [onesf.r, dtt.r])
        yield False
        cp("dve", dtt[:, 32:64], b7.t[:, 64:96], [b7.r], [dtt.r], excl=[b7.r])
        acs = dtt[:, 32:48]
        tot = dtt[:, 48:64]
        tt("dve", dtt[:, 64:80], tot, acs, ALU.subtract, [dtt.r], [dtt.r])
        yield True
        act(dtt[:, 64:80], dtt[:, 64:80], AF.Exp, [dtt.r], [dtt.r])
        act(dtt[:, 80:96], tot, AF.Exp, [dtt.r], [dtt.r])
        yield True
        hb = [b6, b7]
        for g in range(2):
            gs = slice(8 * g, 8 * g + 8)
            xsg = xs_tok[:, s, g * 512:(g + 1) * 512].rearrange("p (j d) -> p j d", j=8)
            BTg = BCT[:, g, cs]
            CTg = BCT[:, 2 + g, cs]
            tt("dve", Xg[:], xsg, bc(dt_all[:, s, gs].unsqueeze(2), [128, 8, 64]), ALU.mult, [xs_tok.r, dt_all.r], [Xg.r])
            tt("dve", Dg[:], bc(umf[:].unsqueeze(1), [128, 8, 128]), bc(dtt[:, 16 + 8 * g:24 + 8 * g].unsqueeze(2), [128, 8, 128]), ALU.mult,
               [umf.r, dtt.r], [Dg.r])
            tt("dve", Xdec[:], Xg[:], bc(dtt[:, 64 + 8 * g:72 + 8 * g].unsqueeze(2), [128, 8, 64]), ALU.mult, [Xg.r, dtt.r], [Xdec.r])
            mm(b7.r, b7.t[:, 0:128], BTg, CTg, True, True, [BCT.r])
            yield False
            tt("dve", cbm[:], b7.t[:, 0:128], umf[:], ALU.mult, [b7.r, umf.r], [cbm.r], excl=[b7.r])
            Dv = Dg[:].rearrange("p j l -> p (j l)")
            for hh in range(2):
                mm(hb[hh].r, hb[hh].t[:], onesf[:], Dv[:, hh * 512:(hh + 1) * 512], True, True, [onesf.r, Dg.r])
            yield False
            for hh in range(2):
                act(Eb[:, hh * 4:(hh + 1) * 4, :], hb[hh].t[:].rearrange("p (j l) -> p j l", j=4), AF.Exp, [hb[hh].r], [Eb.r], excl=[hb[hh].r])
            for j in range(8):
                hh = j // 4
                ts("dve", Dg[:, j, :], hb[hh].t[:, (j % 4) * 128:(j % 4 + 1) * 128], dtt[:, 32 + 8 * g + j:33 + 8 * g + j], 0.0,
                   ALU.subtract, ALU.min, [hb[hh].r, dtt.r], [Dg.r], excl=[hb[hh].r])
            yield True
            yield True
            tt("dve", Cdec[:], Eb[:], bc(CTg.unsqueeze(1), [128, 8, 128]), ALU.mult, [Eb.r, BCT.r], [Cdec.r])
            act(Lb[:], Dg[:], AF.Exp, [Dg.r], [Lb.r])
            pvb = b7.t[:].bitcast(BF16)[:, 512:640]
            tr(b7.r, pvb, BTg, identb[:], [BCT.r, identb.r])
            yield False
            tt("dve", MTb[:], Lb[:], bc(cbm[:].unsqueeze(1), [128, 8, 128]), ALU.mult, [Lb.r, cbm.r], [MTb.r])
            cp("dve", Btok[:], pvb, [b7.r], [Btok.r], excl=[b7.r])
            yield True
            for j in range(8):
                mm(b6.r, b6.t[:, j * 64:(j + 1) * 64], MTb[:, j, :], Xg[:, j, :], True, False, [MTb.r, Xg.r])
                mm(b6.r, b6.t[:, j * 64:(j + 1) * 64], Cdec[:, j, :], Sbf[:, g, j * 64:(j + 1) * 64], False, True, [Cdec.r, Sbf.r])
            mm(b7.r, b7.t[:], Btok[:], Xdec[:].rearrange("p j d -> p (j d)"), True, True, [Btok.r, Xdec.r])
            yield False
            tt("dve", ytmp[:].rearrange("p (j d) -> p j d", j=8), xsg, bc(hpb[:, 32 + 8 * g:40 + 8 * g].unsqueeze(2), [128, 8, 64]), ALU.mult,
               [xs_tok.r, hpb.r], [ytmp.r])
            tt("dve", ytmp[:], ytmp[:], b6.t[:], ALU.add, [ytmp.r, b6.r], [ytmp.r], excl=[b6.r])
            tt("dve", ytmp[:], ytmp[:], zs_all[s][:, g * 512:(g + 1) * 512], ALU.mult, [ytmp.r, zs_all[s].r], [ytmp.r])
            Dsq = Dg[:].rearrange("p j l -> p (j l)")[:, 0:512]
            tt("dve", Dsq, ytmp[:], ytmp[:], ALU.mult, [ytmp.r], [Dg.r])
            S.add("dve", lambda e, g=g, Dsq=Dsq: e.tensor_reduce(out=st_[:, g:g + 1], in_=Dsq, axis=AX.X, op=ALU.add),
                  reads=[Dg.r], writes=[st_.r])
            Sg = Sst[:, g, :].rearrange("p (j d) -> p j d", j=8)
            tt("dve", Sg, Sg, bc(dtt[:, 80 + 8 * g:88 + 8 * g].unsqueeze(2), [128, 8, 64]), ALU.mult, [Sst.r, dtt.r], [Sst.r])
            tt("dve", Sst[:, g, :], Sst[:, g, :], b7.t[:], ALU.add, [Sst.r, b7.r], [Sst.r], excl=[b7.r])
            cp("dve", Sbf[:, g, :], Sst[:, g, :], [Sst.r], [Sbf.r])
            yield True
            yield True
            act(st_[:, g:g + 1], st_[:, g:g + 1], AF.Ln, [st_.r], [st_.r], scale=1.0 / 512, bias=EPS)
            act(st_[:, g:g + 1], st_[:, g:g + 1], AF.Exp, [st_.r], [st_.r], scale=-0.5)
            yield True
            ts("dve", mixed[:, s, 512 + g * 512:1024 + g * 512], ytmp[:], st_[:, g:g + 1], None, ALU.mult, None, [ytmp.r, st_.r], [mixed.r])
            yield True

    ssd_clean = [True]

    def ssd_all_gen():
        for s_ in range(4):
            for c_ in ssd_main_gen(s_):
                yield c_

    pt_i = [0]
    vb_i = [0]

    def att_units(tile_abs):
        tb = tile_abs * 4
        n = 0
        for h in range(8):
            dmax = 140.0 / (2.0 ** -(h + 1))
            kb_lo = 0
            while kb_lo < tb and (tb * 128 - (kb_lo * 128 + 127)) > dmax:
                kb_lo += 1
            n += tb + 4 - kb_lo
        return n

    def attention_gen(tile_abs):
        tb = tile_abs * 4
        nkb = tb + 4
        grp_all = {}
        obase = 4

        def kb_lo_of(hh):
            dm = 140.0 / (2.0 ** -(hh + 1))
            lo = 0
            while lo < tb and (tb * 128 - (lo * 128 + 127)) > dm:
                lo += 1
            return lo

        def get_grp(hh, kb):
            g = kb // VB_BLK
            if (hh, g) not in grp_all:
                k0 = max(g * VB_BLK, kb_lo_of(hh))
                k1 = min(nkb, (g + 1) * VB_BLK)
                vi = vb_i[0] % 3
                vb_i[0] += 1
                p0 = (hh % 2) * 64
                dma(vbuf[vi][:, 0:k1 - k0, :], vscr[hh, :, k0:k1, :], [vres], [vbuf[vi].r], "vb%d" % vi)
                dma(kbuf[vi][p0:p0 + 64, 0:(k1 - k0) * 128], kscr[hh * 64:(hh + 1) * 64, k0 * 128:k1 * 128], [kres], [kbuf[vi].r], "kb%d" % vi)
                grp_all[(hh, g)] = (vi, k0)
            return grp_all[(hh, g)]

        def prefetch(hh, kb):
            g = kb // VB_BLK
            if kb != max(kb_lo_of(hh), g * VB_BLK):
                return
            nxt = (g + 1) * VB_BLK
            if nxt < nkb:
                get_grp(hh, nxt)
            elif hh + 1 < 8:
                get_grp(hh + 1, kb_lo_of(hh + 1))

        items = [(hh, kb) for hh in range(8) for kb in range(kb_lo_of(hh), nkb)]
        st = {}

        def s1(idx):
            h, kb = items[idx]
            c = h // 2
            pb0 = (h % 2) * 64
            use_pos = h < 3
            vi, k0 = get_grp(h, kb)
            prefetch(h, kb)
            di = kb - tb
            q0 = 128 * di if di > 0 else 0
            pair = idx % 2
            kl = (kb - k0) * 128
            diag = di >= 0
            for m in range(2):
                pb = pb0 + 32 * m
                sbank = banks[2 * pair + m]
                tp = (96, 0) if pb == 96 else None
                mm(sbank.r, sbank.t[:, q0:T], kbuf[vi][pb:pb + 32, kl:kl + 128], QT[pb:pb + 32, c, q0:T], True, not (use_pos or diag),
                   [kbuf[vi].r, QT.r], tp=tp)
            if use_pos:
                for m in range(2):
                    pb = pb0 + 32 * m
                    sbank = banks[2 * pair + m]
                    tp = (96, 0) if pb == 96 else None
                    mm(sbank.r, sbank.t[:, q0:T], slopes2[pb:pb + 2, h * 128:(h + 1) * 128], qpos[pb:pb + 2, q0:T], False, not diag,
                       [slopes2.r, qpos.r], tp=tp)
            if diag:
                for m in range(2):
                    sbank = banks[2 * pair + m]
                    mm(sbank.r, sbank.t[:, q0:q0 + 128], identb[:], umb[:], False, True, [identb.r, umb.r])
            st[idx] = (vi, k0, q0, pair, di)

        def s2(idx):
            h, kb = items[idx]
            vi, k0, q0, pair, di = st[idx]
            p_ = PT[pt_i[0] % 3]
            pt_i[0] += 1
            st[idx] = (vi, k0, q0, pair, di, p_)
            btab = biasP if kb < NPT * 4 else biasM
            jidx = (tb - kb) + 3
            src = psum_all[:, pair * 1024:(pair + 1) * 1024].rearrange("p (m n) -> p m n", m=2)[:, :, q0:T]
            b0, b1 = banks[2 * pair], banks[2 * pair + 1]
            act(p_[:, :, q0:T], src, AF.Exp, [b0.r, b1.r, btab.r], [p_.r], excl=[b0.r, b1.r],
                bias=btab[:, h * NJ + jidx:h * NJ + jidx + 1])

        def s3(idx):
            h, kb = items[idx]
            vi, k0, q0, pair, di, p_ = st[idx]
            first = kb == kb_lo_of(h)
            for m in range(2):
                ob = obase + m
                for sq in range(max(di, 0), 4):
                    mm(banks[ob].r, banks[ob].t[:, sq * 65:(sq + 1) * 65], p_[:, m, sq * 128:(sq + 1) * 128], vbuf[vi][:, kb - k0, :],
                       first and sq == 0, kb == nkb - 1, [p_.r, vbuf[vi].r], sgc=True)

        ott = junk[:, 0:520].rearrange("p (m s v) -> p m s v", m=2, s=4)
        otmp = junk[:, 1792:2048].rearrange("p (s v) -> p s v", s=4)
        sq_ = junk[:, 520:776].rearrange("p (s v) -> p s v", s=4)
        tn_ = junk[:, 776:1032].rearrange("p (s v) -> p s v", s=4)

        def epi0(h):
            for m in range(2):
                ob = obase + m
                cp("act", ott[:, m], banks[ob].t[:, 0:4 * 65].rearrange("p (s v) -> p s v", s=4), [banks[ob].r], [junk.r], excl=[banks[ob].r])

        def epi1(h):
            o1 = ott[:, 0]
            o2 = ott[:, 1]
            S.add("dve", lambda e: e.reciprocal(out=stat[:, 0:4], in_=o1[:, :, 64]), reads=[junk.r], writes=[stat.r])
            S.add("dve", lambda e: e.reciprocal(out=stat[:, 4:8], in_=o2[:, :, 64]), reads=[junk.r], writes=[stat.r])
            ts("dve", stat[:, 4:8], stat[:, 4:8], lam[:, 3:4], None, ALU.mult, None, [stat.r, lam.r], [stat.r])
            for s in range(4):
                ts("dve", junk[:, 1536 + s * 64:1600 + s * 64], o2[:, s, 0:64], stat[:, 4 + s:5 + s], None, ALU.mult, None,
                   [junk.r, stat.r], [junk.r])
                stt(otmp[:, s, :], o1[:, s, 0:64], stat[:, s:s + 1], junk[:, 1536 + s * 64:1600 + s * 64], ALU.mult, ALU.subtract,
                    [stat.r, junk.r], [junk.r])
            tt("dve", sq_, otmp, otmp, ALU.mult, [junk.r], [junk.r])
            S.add("dve", lambda e: e.tensor_reduce(out=stat[:, 8:12], in_=sq_, axis=AX.X, op=ALU.add), reads=[junk.r], writes=[stat.r])

        def epi2(h):
            act(stat[:, 8:12], stat[:, 8:12], AF.Ln, [stat.r], [stat.r], scale=1.0 / 64, bias=EPS)
            act(stat[:, 8:12], stat[:, 8:12], AF.Exp, [stat.r], [stat.r], scale=-0.5)

        def epi3(h):
            tt("dve", tn_, otmp, bc(stat[:, 8:12].unsqueeze(2), [128, 4, 64]), ALU.mult, [junk.r, stat.r], [junk.r])
            tt("dve", mixed[:, :, h * 64:(h + 1) * 64], tn_, bc(gsub[:].unsqueeze(1), [128, 4, 64]), ALU.mult, [junk.r, gsub.r], [mixed.r])

        sched = {}

        def plan_epi(h, idx_last):
            for k, f in enumerate((epi0, epi1, epi2, epi3)):
                sched.setdefault(idx_last + 2 * k, []).append(lambda f=f, h=h: f(h))
        for idx, (h, kb) in enumerate(items):
            if kb == nkb - 1:
                plan_epi(h, idx)
        s1(0)
        for idx in range(len(items)):
            s2(idx)
            if idx + 1 < len(items):
                s1(idx + 1)
            s3(idx)
            for f in sched.pop(idx, []):
                f()
            yield
        for idx in sorted(sched):
            for f in sched[idx]:
                f()

    def out_proj():
        for s in range(4):
            for part, (c0, c1) in enumerate([(0, 8), (8, 12)]):
                b = 2 * (s % 2) + part
                pv = banks[b].t[:].bitcast(BF16)[:, 0:(c1 - c0) * 128].rearrange("p (c n) -> p c n", n=128)
                for c in range(c0, c1):
                    tr(banks[b].r, pv[:, c - c0, :], mixed[:, s, c * 128:(c + 1) * 128], identb[:], [mixed.r, identb.r])
                tt("dve", mixedT[:, c0:c1, s * 128:(s + 1) * 128], pv, bc(gmixT[:, c0:c1].unsqueeze(2), [128, c1 - c0, 128]), ALU.mult,
                   [banks[b].r, gmixT.r], [mixedT.r], excl=[banks[b].r])
        run_gen(down_post_gen(mixedT, 12, "wo", 3, 1.0, ctxM))

    def load_x(src, i):
        dma(xt[:], src[i * T:(i + 1) * T, :].rearrange("(s p) d -> p s d", p=128), [], [xt.r], "xt")

    xt2 = sb("xt2", [128, 4, D])
    hTB = sb("hTB", [128, 8, T], BF16)
    statA = sb("statA", [128, 16])
    statB = sb("statB", [128, 16])
    sqB = sb("sqB", [128, D], BF16)
    ctxM = mkctx(xt, hT, stat, junk[:, 0:D], junk.r, (4, 5), True)
    ctxM.passB_base = 4
    xts = [xt, xt2]
    _ysq = ytmp.t[:].bitcast(BF16)
    ctxA = [mkctx(xts[k], hT, statA, _ysq, ytmp.r, (4, 5), True) for k in range(2)]
    ctxB = [mkctx(xts[k], hTB, statB, sqB[:], sqB.r, (6, 7), True) for k in range(2)]

    def load_x(src, i, xt_):
        dma(xt_[:], src[i * T:(i + 1) * T, :].rearrange("(s p) d -> p s d", p=128), [], [xt_.r], "xt_" + xt_.r.name)

    if NPT > 0:
        load_x(xp, 0, xts[0])
        run_gen(ffn_gen("wgu1", "wd1", 0, 1, ctxA[0]))
        for i in range(NPT):
            if i == 1 or NPT == 1:
                bg.extend(late_conv)
            gb = prefix_mixer_gen(i, ctxB[i % 2])
            if i + 1 < NPT:
                load_x(xp, i + 1, xts[(i + 1) % 2])
                ga = ffn_gen("wgu1", "wd1", 0, 1, ctxA[(i + 1) % 2])
            else:
                ga = iter(())
            interleave(ga, gb, lead=0)
        while bg:
            bg.pop(0)()
        ts("dve", Sst[:].rearrange("p g n -> p (g n)"), Sst[:].rearrange("p g n -> p (g n)"), flags[:, 1:2], None, ALU.mult, None, [Sst.r, flags.r], [Sst.r])
        cp("pool", Sbf[:].rearrange("p g n -> p (g n)"), Sst[:].rearrange("p g n -> p (g n)"), [Sst.r], [Sbf.r])
        ts("dve", halo[:].rearrange("p c k -> p (c k)"), halo[:].rearrange("p c k -> p (c k)"), flags[:, 1:2], None, ALU.mult, None, [halo.r, flags.r], [halo.r])
    last = None
    lasts = {}
    if NMT > 0:
        load_x(xm, 0, xts[0])
    ctxF1 = [mkctx(xts[k], hTB, statA, sqB[:], sqB.r, (0, 1), True) for k in range(2)]
    for k in range(2):
        ctxF1[k].passB_base = 4
    for i in range(NMT):
        ctxM.xt = xts[i % 2]
        xc = xts[i % 2]
        run_gen(ffn_gen("wgu1", "wd1", 0, 1, ctxF1[i % 2], skip_prenorm=(i > 0)))
        if i + 1 < NMT:
            load_x(xm, i + 1, xts[(i + 1) % 2])
        proj_and_state(NPT + i, True)
        dt_proj()
        ga = attention_gen(NPT + i)
        gs_ = ssd_all_gen()
        n_att = att_units(NPT + i)
        ratio = max(1, n_att // 110)
        da = ds = False
        while not (da and ds):
            for _ in range(ratio):
                if not da:
                    try:
                        next(ga)
                    except StopIteration:
                        da = True
            if not ds:
                try:
                    ssd_clean[0] = bool(next(gs_))
                except StopIteration:
                    ds = True
                    ssd_clean[0] = True
        out_proj()
        hk = (lambda i=i: prenorm(0, ctxF1[(i + 1) % 2])) if i + 1 < NMT else None
        run_gen(ffn_gen("wgu2", "wd2", 2, 5, ctxM, hook=hk))
        last = dma(out[i * T:(i + 1) * T, :].rearrange("(s p) d -> p s d", p=128), xc[:], [xc.r], [Res("o")], "out%d" % (i % 2), eng="pool")
        lasts[i % 2] = last
    nsem = S.emit(final_waits=list(lasts.values()))
    return nc, (S.nops, nsem, nc.sbuf_bytes_remaining() if callable(nc.sbuf_bytes_remaining) else nc.sbuf_bytes_remaining)


def _prep_weights(inp):
    f = np.float32
    w = {}

    def gu(g, u):
        g = np.asarray(g[0], f).reshape(8, 128, NFC, 128)
        u = np.asarray(u[0], f).reshape(8, 128, NFC, 128)
        a = np.stack([g, u], 0)
        return np.ascontiguousarray(a.transpose(3, 2, 0, 1, 4)).reshape(NFC, 128, 2048)

    w["wgu1"] = gu(inp["ffn1_w_gate"], inp["ffn1_w_up"])
    w["wgu2"] = gu(inp["ffn2_w_gate"], inp["ffn2_w_up"])
    w["wd1"] = np.ascontiguousarray(np.asarray(inp["ffn1_w_down"][0], f).reshape(NFC, 128, 1024))
    w["wd2"] = np.ascontiguousarray(np.asarray(inp["ffn2_w_down"][0], f).reshape(NFC, 128, 1024))
    win = np.asarray(inp["w_in"][0], f)
    q, k, v, z, xbc, dt = np.split(win, [512, 1024, 1536, 2560, 4096], axis=1)
    fm = np.concatenate([q, k, xbc], axis=1)
    fm = fm.reshape(8, 128, 20, 128).transpose(2, 1, 0, 3)
    w["wfm"] = np.ascontiguousarray(fm).reshape(20, 128, 1024)
    tm = np.concatenate([v, z], axis=1).reshape(8, 128, 3, 512).transpose(2, 1, 0, 3)
    w["wtm"] = np.ascontiguousarray(tm).reshape(3, 128, 4096)
    w["wdt"] = np.ascontiguousarray(dt.reshape(8, 128, 16).transpose(1, 0, 2)).reshape(128, 128)
    w["wo"] = np.ascontiguousarray(np.asarray(inp["w_out"][0], f).reshape(12, 128, 1024))
    return w


def _consts(NPT, NMT):
    f = np.float32
    NBLK = (NPT + NMT) * 4
    NJ = NBLK + 3
    slopes = (2.0 ** (-8.0 * np.arange(1, 9) / 8)).astype(np.float64)
    p = np.arange(128)[:, None, None]
    j = (np.arange(NJ) - 3)[None, None, :]
    bias = slopes[None, :, None] * (p - 128.0 * j)
    bias[:, 3:, :] -= slopes[None, 3:, None] * (T - 1)
    c = {"ident": np.eye(128, dtype=f),
         "umat": np.triu(np.ones((128, 128), f)),
         "negm": (-30000.0 * np.tril(np.ones((128, 128), f), -1)).astype(f),
         "biastab": bias.reshape(128, 8 * NJ).astype(f),
         }
    sl2 = np.zeros((128, 8 * 128), f)
    qp = np.zeros((128, T), f)
    qq = np.arange(T)
    for pb in (0, 32, 64, 96):
        sl2[pb:pb + 2] = np.repeat(slopes, 128)[None, :]
        qp[pb] = -(16.0 * (qq // 16))
        qp[pb + 1] = -(qq % 16)
    c["slopes2"] = sl2
    c["qpos"] = qp
    return c


_CACHE = {}


def run(inp, NPT, NMT, nbatch, debug=False):
    key = (NPT, NMT)
    if key not in _CACHE:
        _CACHE[key] = build(NPT, NMT)
    nc, info = _CACHE[key]
    f = np.float32
    w = _prep_weights(inp)
    c = _consts(NPT, NMT)
    small = {
        "gains": np.stack([np.asarray(inp[k][0], f) for k in
                           ["ffn1_pre_g", "ffn1_post_g", "mix_pre_g", "mix_post_g", "ffn2_pre_g", "ffn2_post_g"]], 0),
        "ssm_g": np.asarray(inp["ssm_norm_g"][0], f),
        "sub_g": np.asarray(inp["attn_subln_g"][0], f),
        "lamv": np.stack([np.asarray(inp[k][0], f) for k in ["lambda_q1", "lambda_k1", "lambda_q2", "lambda_k2"]], 0),
        "conv_w": np.asarray(inp["conv_w"][0], f),
        "conv_b": np.asarray(inp["conv_b"][0], f),
        "hp": np.stack([np.asarray(inp[k][0], f) for k in ["dt_bias", "a_log", "d_skip"]], 0),
    }
    x = np.asarray(inp["x"], f)
    half = NMT * T
    in_maps = []
    for b in range(nbatch):
        for j in range(2):
            m = dict(w)
            m.update(c)
            m.update(small)
            m["xp"] = np.ascontiguousarray(x[b, 0:max(NPT, 1) * T])
            m["xm"] = np.ascontiguousarray(x[b, j * half:(j + 1) * half])
            fl = np.zeros((128, 2), f)
            fl[:, 0] = 0.0 if j == 1 else -30000.0
            fl[:, 1] = 1.0 if j == 1 else 0.0
            m["flags"] = fl
            in_maps.append(m)
    res = run_bass_kernel_spmd(nc, in_maps, core_ids=list(range(2 * nbatch)))
    outs = [r["out"] for r in res.results]
    y = np.zeros((nbatch, 2 * half, D), f)
    for b in range(nbatch):
        for j in range(2):
            y[b, j * half:(j + 1) * half] = outs[2 * b + j]
    return y


def kernel(**inputs):
    return run(inputs, 8, 8, 4)
```

```python
import numpy as np
import ml_dtypes
import concourse.bass as bass
import concourse.mybir as mybir
from concourse.bass_utils import run_bass_kernel_spmd

F32 = mybir.dt.float32
BF16 = mybir.dt.bfloat16
AF = mybir.ActivationFunctionType
ALU = mybir.AluOpType
AX = mybir.AxisListType

EPOCH = 4000
D = 1024
DFF = 2816
NFC = 22
T = 512
EPS = 1e-6
LAMBDA_INIT = 0.8 - 0.6 * 1.0
ENGS = ("pe", "act", "dve", "pool", "sp")


class Res:
    __slots__ = ("name", "w", "r")

    def __init__(self, name):
        self.name = name
        self.w = None
        self.r = []


class Op:
    __slots__ = ("eng", "fn", "deps", "signal", "cnt", "dma_key")

    def __init__(self, eng, fn):
        self.eng = eng
        self.fn = fn
        self.deps = []
        self.signal = False
        self.cnt = None
        self.dma_key = None


class Sched:
    def __init__(self, nc):
        self.nc = nc
        self.ops = {e: [] for e in ENGS}
        self.dma_cnt = {}
        self.nops = 0

    def add(self, eng, fn, reads=(), writes=(), excl=(), dma_key=None):
        op = Op(eng, fn)
        op.dma_key = dma_key
        self.nops += 1
        cand = []
        for r in reads:
            if r.w is not None:
                cand.append((r.w, True))
        for w in writes:
            if w.w is not None:
                cand.append((w.w, False))
            for x in w.r:
                cand.append((x, False))
        for w in excl:
            if w.w is not None:
                cand.append((w.w, False))
            for x in w.r:
                cand.append((x, False))
        seen = {}
        for d, raw in cand:
            if d is op:
                continue
            seen[id(d)] = (d, seen.get(id(d), (d, False))[1] or raw)
        for d, raw in seen.values():
            if d.dma_key is None and d.eng == eng:
                if eng == "pe":
                    continue
            op.deps.append(d)
        for r in reads:
            r.r.append(op)
        for w in writes:
            w.w = op
            w.r = []
        for w in excl:
            w.r.append(op)
        if dma_key is not None:
            self.dma_cnt[dma_key] = self.dma_cnt.get(dma_key, 0) + 16
            op.cnt = self.dma_cnt[dma_key]
        self.ops[eng].append(op)
        return op

    def emit(self, final_waits=()):
        nc = self.nc
        for e in ENGS:
            for op in self.ops[e]:
                for d in op.deps:
                    d.signal = True
        for op in final_waits:
            op.signal = True
        for e in ENGS:
            c = 0
            for op in self.ops[e]:
                if op.dma_key is None and op.signal:
                    c += 1
                    op.cnt = c
        sems = {}

        def sem_for(op):
            if op.dma_key is not None:
                k = ("dma", op.dma_key)
                v = op.cnt
            else:
                k = (op.eng, (op.cnt - 1) // EPOCH)
                v = (op.cnt - 1) % EPOCH + 1
            if k not in sems:
                sems[k] = nc.alloc_semaphore("s_%s_%s" % (k[0], k[1]))
            return k, sems[k], v

        with nc.Block() as block:
            def run(ename):
                def body(eng):
                    waited = {}
                    for op in self.ops[ename]:
                        need = {}
                        for d in op.deps:
                            k, s, v = sem_for(d)
                            if waited.get(k, 0) >= v:
                                continue
                            if k not in need or need[k][1] < v:
                                need[k] = (s, v)
                        for k, (s, v) in need.items():
                            eng.wait_ge(s, v)
                            waited[k] = v
                        ins = op.fn(eng)
                        if op.dma_key is not None:
                            k, s, v = sem_for(op)
                            ins.then_inc(s, 16)
                        elif op.signal:
                            k, s, v = sem_for(op)
                            ins.then_inc(s, 1)
                    if ename == "sp":
                        for op in final_waits:
                            k, s, v = sem_for(op)
                            eng.wait_ge(s, v)
                return body
            block.tensor(run("pe"))
            block.scalar(run("act"))
            block.vector(run("dve"))
            block.gpsimd(run("pool"))
            block.sync(run("sp"))
        return len(sems)


class Tl:
    def __init__(self, t, name):
        self.t = t
        self.r = Res(name)

    def __getitem__(self, k):
        return self.t[k]


def build(NPT, NMT, debug=False):
    nc = bass.Bass("TRN2", target_bir_lowering=False)
    S = Sched(nc)
    NT = NPT + NMT
    NTOK = NT * T
    NBLK = NT * 4
    NJ = NBLK + 3

    def din(name, shape, dt=F32):
        return nc.dram_tensor(name, list(shape), dt, kind="ExternalInput").ap()

    def dscr(name, shape, dt=BF16):
        return nc.dram_tensor(name, list(shape), dt, kind="Internal").ap()

    xp = din("xp", [max(NPT, 1) * T, D])
    xm = din("xm", [NMT * T, D])
    out = nc.dram_tensor("out", [NMT * T, D], F32, kind="ExternalOutput").ap()
    wshape = {"wgu1": [NFC, 128, 2048], "wd1": [NFC, 128, 1024], "wfm": [20, 128, 1024],
              "wtm": [3, 128, 4096], "wdt": [128, 128], "wo": [12, 128, 1024],
              "wgu2": [NFC, 128, 2048], "wd2": [NFC, 128, 1024]}
    wf = {k: din(k, v) for k, v in wshape.items()}
    wb = {k: dscr(k + "_bf", v) for k, v in wshape.items()}
    wres = {k: Res(k) for k in wshape}
    gains = din("gains", [6, D])
    ssm_g = din("ssm_g", [D])
    sub_g = din("sub_g", [64])
    lamv = din("lamv", [4, 32])
    conv_w = din("conv_w", [4, 1536])
    conv_b = din("conv_b", [1536])
    hp = din("hp", [3, 16])
    ident_d = din("ident", [128, 128])
    umat_d = din("umat", [128, 128])
    negm_d = din("negm", [128, 128])
    bias_d = din("biastab", [128, 8 * NJ])
    slopes_d = din("slopes2", [128, 8 * 128])
    qpos_d = din("qpos", [128, T])
    flag_d = din("flags", [128, 2])
    vscr = dscr("vscr", [8, 128, NBLK, 65])
    vres = Res("vscr")
    kscr = dscr("kscr", [8 * 64, NTOK])
    kres = Res("kscr")
    dbg = {}

    def sb(name, shape, dt=F32):
        return Tl(nc.alloc_sbuf_tensor("sb_" + name, list(shape), dt), name)

    psum_all = nc.alloc_psum_tensor("psum_all", [128, 4096], F32)

    class _B:
        def __init__(self, i):
            self.t = psum_all[:, i * 512:(i + 1) * 512]
            self.r = Res("bank%d" % i)
    banks = [_B(i) for i in range(8)]

    def bview(b, dt, shape):
        t = banks[b].t
        v = t[:] if dt == F32 else t[:].bitcast(dt)
        return v

    ktmp = [sb("ktmp%d" % i, [128, T], BF16) for i in range(2)]
    KG = 8 * 128
    kbuf = [sb("kbuf%d" % i, [128, KG], BF16) for i in range(3)]
    xt = sb("xt", [128, 4, D])
    hT = sb("hT", [128, 8, T], BF16)
    AT = sb("AT", [128, NFC, T], BF16)
    wbuf = [sb("wbuf%d" % i, [128, 2048], BF16) for i in range(4)]
    gpost = sb("gpost", [128, D])
    gpre = sb("gpre", [128, 3, 8])
    gmixT = sb("gmixT", [128, 12])
    gsub = sb("gsub", [128, 64])
    identb = sb("identb", [128, 128], BF16)
    identf = sb("identf", [128, 128])
    umf = sb("umf", [128, 128])
    umb = sb("umb", [128, 128], BF16)
    onesf = sb("onesf", [128, 128])
    biasM = sb("biasM", [128, 8 * NJ])
    biasP = sb("biasP", [128, 8 * NJ])
    slopes2 = sb("slopes2", [128, 8 * 128], BF16)
    qpos = sb("qpos", [128, T], BF16)
    flags = sb("flags", [128, 2])
    cw = sb("cw", [128, 12, 4])
    cb = sb("cb", [128, 12])
    hpb = sb("hpb", [128, 48])
    lam = sb("lam", [128, 4])
    lamt = sb("lamt", [128, 4, 32])
    halo = sb("halo", [128, 12, 3])
    Sst = sb("Sst", [128, 2, 512])
    Sbf = sb("Sbf", [128, 2, 512], BF16)
    xn = [sb("xn%d" % i, [128, D], BF16) for i in range(2)]
    junk = sb("junk", [128, 2048])
    stat = sb("stat", [128, 16])
    QT = sb("QT", [128, 4, T], BF16)
    BCT = sb("BCT", [128, 4, T], BF16)
    raw = [sb("raw%d" % i, [128, T + 3]) for i in range(2)]
    cacc = [sb("cacc%d" % i, [128, T]) for i in range(2)]
    xsT = [sb("xsT%d" % i, [128, T], BF16) for i in range(2)]
    xs_tok = sb("xs_tok", [128, 4, D], BF16)
    vtmp = [sb("vtmp%d" % i, [128, 8, 65], BF16) for i in range(2)]
    VB_BLK = 8
    vbuf = [sb("vbuf%d" % i, [128, VB_BLK, 65], BF16) for i in range(3)]
    PT = [sb("PT%d" % i, [128, 2, T], BF16) for i in range(3)]
    OT = sb("OT", [65, 2, T])
    mixed = sb("mixed", [128, 4, 1536], BF16)
    mixedT = AT
    dtt = sb("dtt", [128, 96])
    Dg = sb("Dg", [128, 8, 128])
    Eb = sb("Eb", [128, 8, 128], BF16)
    Lb = sb("Lb", [128, 8, 128], BF16)
    MTb = sb("MTb", [128, 8, 128], BF16)
    Cdec = sb("Cdec", [128, 8, 128], BF16)
    cbm = sb("cbm", [128, 128], BF16)
    Btok = sb("Btok", [128, 128], BF16)
    Xg = sb("Xg", [128, 8, 64], BF16)
    Xdec = sb("Xdec", [128, 8, 64], BF16)
    ytmp = sb("ytmp", [128, 512])

    dma_rr = [0]

    def dma(outap, inap, reads, writes, key, eng="sp", **kw):
        return S.add(eng, lambda e: e.dma_start(out=outap, in_=inap, **kw), reads=reads, writes=writes, dma_key=key)

    def mm(bank_r, outap, lhsT, rhs, start, stop, reads, tp=None, sgc=False):
        if sgc:
            return S.add("pe", lambda e: e.matmul(outap, lhsT=lhsT, rhs=rhs, start=start, stop=stop, skip_group_check=True),
                         reads=reads, writes=[bank_r])
        if tp is not None:
            return S.add("pe", lambda e: e.matmul(outap, lhsT=lhsT, rhs=rhs, start=start, stop=stop, tile_position=tp),
                         reads=reads, writes=[bank_r])
        return S.add("pe", lambda e: e.matmul(outap, lhsT=lhsT, rhs=rhs, start=start, stop=stop),
                     reads=reads, writes=[bank_r])

    def tr(bank_r, outap, inap, ident, reads):
        return S.add("pe", lambda e: e.transpose(out=outap, in_=inap, identity=ident), reads=reads, writes=[bank_r])

    def act(outap, inap, func, reads, writes, excl=(), **kw):
        return S.add("act", lambda e: e.activation(out=outap, in_=inap, func=func, **kw), reads=reads, writes=writes, excl=excl)

    def tt(eng, outap, in0, in1, op, reads, writes, excl=()):
        return S.add(eng, lambda e: e.tensor_tensor(out=outap, in0=in0, in1=in1, op=op), reads=reads, writes=writes, excl=excl)

    def ts(eng, outap, in0, s1, s2, op0, op1, reads, writes, excl=()):
        if op1 is None:
            return S.add(eng, lambda e: e.tensor_scalar(out=outap, in0=in0, scalar1=s1, scalar2=None, op0=op0),
                         reads=reads, writes=writes, excl=excl)
        return S.add(eng, lambda e: e.tensor_scalar(out=outap, in0=in0, scalar1=s1, scalar2=s2, op0=op0, op1=op1),
                     reads=reads, writes=writes, excl=excl)

    def stt(outap, in0, scalar, in1, op0, op1, reads, writes, excl=()):
        return S.add("dve", lambda e: e.scalar_tensor_tensor(out=outap, in0=in0, scalar=scalar, in1=in1, op0=op0, op1=op1),
                     reads=reads, writes=writes, excl=excl)

    def cp(eng, outap, inap, reads, writes, excl=()):
        if eng == "act":
            return S.add("act", lambda e: e.copy(out=outap, in_=inap), reads=reads, writes=writes, excl=excl)
        return S.add(eng, lambda e: e.tensor_copy(out=outap, in_=inap), reads=reads, writes=writes, excl=excl)

    def memset(eng, tl, ap, val):
        return S.add(eng, lambda e: e.memset(ap, val), writes=[tl.r])

    def bc(ap, shape):
        return ap.broadcast_to(shape)

    dma(identb[:], ident_d, [], [identb.r], "c_idb", eng="pool")
    dma(umb[:], negm_d, [], [umb.r], "c_umb", eng="pool")
    dma(slopes2[:], slopes_d, [], [slopes2.r], "c_sl", eng="pool")
    dma(qpos[:], qpos_d, [], [qpos.r], "c_qp", eng="pool")
    memset("pool", gmixT, gmixT[:], 1.0)
    memset("pool", onesf, onesf[:], 1.0)
    memset("pool", halo, halo[:], 0.0)
    memset("pool", Sst, Sst[:], 0.0)
    memset("pool", Sbf, Sbf[:], 0.0)
    for i in range(2):
        memset("pool", vtmp[i], vtmp[i][:], 1.0)
    wgu1_res = {}
    bg = []

    def conv_thunks(k):
        th = []
        n0 = wshape[k][0]
        if len(wshape[k]) == 2:
            th.append(lambda: dma(wb[k], wf[k], [], [wres[k]], "cv_" + k, eng="pool"))
            return th
        step = 4096 // wshape[k][2]
        for i in range(0, n0, step):
            j = min(n0, i + step)
            if k == "wgu1":
                wgu1_res[i] = Res("wgu1_%d" % i)
                th.append(lambda i=i, j=j: dma(wb[k][i:j].rearrange("a p n -> p a n"), wf[k][i:j].rearrange("a p n -> p a n"), [],
                                               [wgu1_res[i], wres[k]], "cv_%s_%d" % (k, i), eng="pool"))
            else:
                th.append(lambda i=i, j=j: dma(wb[k][i:j].rearrange("a p n -> p a n"), wf[k][i:j].rearrange("a p n -> p a n"), [],
                                               [wres[k]], "cv_" + k, eng="pool"))
        return th
    for k in ["wgu1", "wd1", "wfm", "wtm", "wdt"]:
        for f in conv_thunks(k):
            f()
    late_conv = []
    for k in ["wo", "wgu2", "wd2"]:
        late_conv.extend(conv_thunks(k))
    if NPT == 0:
        for f in late_conv:
            f()
        late_conv = []
    dma(identf[:], ident_d, [], [identf.r], "c0")
    dma(umf[:], umat_d, [], [umf.r], "c1")
    dma(biasM[:], bias_d, [], [biasM.r], "c2")
    dma(flags[:], flag_d, [], [flags.r], "c3")
    dma(gpre[:, 0, :], gains[0].rearrange("(c p) -> p c", p=128), [], [gpre.r], "c4", allow_slow_non_contiguous=True)
    dma(gpre[:, 1, :], gains[2].rearrange("(c p) -> p c", p=128), [], [gpre.r], "c4", allow_slow_non_contiguous=True)
    dma(gpre[:, 2, :], gains[4].rearrange("(c p) -> p c", p=128), [], [gpre.r], "c4", allow_slow_non_contiguous=True)
    dma(gmixT[:, 4:12], ssm_g.rearrange("(c p) -> p c", p=128), [], [gmixT.r], "c5", allow_slow_non_contiguous=True)
    dma(gsub[:], sub_g.partition_broadcast(128), [], [gsub.r], "c6")
    for k_ in range(4):
        dma(cw[:, :, k_], conv_w[k_].rearrange("(c p) -> p c", p=128), [], [cw.r], "c7", allow_slow_non_contiguous=True)
    dma(cb[:], conv_b.rearrange("(c p) -> p c", p=128), [], [cb.r], "c8", allow_slow_non_contiguous=True)
    dma(hpb[:], hp.rearrange("a b -> (a b)").partition_broadcast(128), [], [hpb.r], "c9")
    dma(lamt[:], lamv.rearrange("a b -> (a b)").partition_broadcast(128), [], [lamt.r], "c10")
    ts("dve", biasP[:], biasM[:], flags[:, 0:1], None, ALU.add, None, [biasM.r, flags.r], [biasP.r])
    ts("dve", gsub[:], gsub[:], 1.0 - LAMBDA_INIT, None, ALU.mult, None, [gsub.r], [gsub.r])
    act(hpb[:, 16:32], hpb[:, 16:32], AF.Exp, [hpb.r], [hpb.r])
    ts("dve", hpb[:, 16:32], hpb[:, 16:32], -1.0, None, ALU.mult, None, [hpb.r], [hpb.r])
    tt("dve", junk[:, 0:32], lamt[:, 0, :], lamt[:, 1, :], ALU.mult, [lamt.r], [junk.r])
    tt("dve", junk[:, 32:64], lamt[:, 2, :], lamt[:, 3, :], ALU.mult, [lamt.r], [junk.r])
    S.add("dve", lambda e: e.tensor_reduce(out=lam[:, 0:2], in_=junk[:, 0:64].rearrange("p (a b) -> p a b", a=2), axis=AX.X, op=ALU.add),
          reads=[junk.r], writes=[lam.r])
    act(lam[:, 0:2], lam[:, 0:2], AF.Exp, [lam.r], [lam.r])
    tt("dve", lam[:, 2:3], lam[:, 0:1], lam[:, 1:2], ALU.subtract, [lam.r], [lam.r])
    ts("dve", lam[:, 3:4], lam[:, 2:3], LAMBDA_INIT, None, ALU.add, None, [lam.r], [lam.r])

    wslot = [0]
    wslotB = [0]
    wbufB = [sb("wbufB%d" % i, [128, 2048], BF16) for i in range(2)]

    def wload(src_ap, ncols, wr):
        i = wslot[0] % 4
        wslot[0] += 1
        assert ncols <= 2048
        dma(wbuf[i][:, 0:ncols], src_ap, [wr], [wbuf[i].r], "wb%d" % i)
        return wbuf[i]

    def rstd_from_ss(ssap, n, mult, reads_extra=()):
        pass

    class Ctx:
        pass

    def mkctx(xt_, hT_, stat_, sq_ap, sq_r, tb, two_pass):
        c = Ctx()
        c.xt, c.hT, c.stat, c.sq, c.sq_r, c.tb, c.two_pass = xt_, hT_, stat_, sq_ap, sq_r, tb, two_pass
        return c

    def prenorm(gi, cx=None):
        cx = cx or ctxM
        xt_, hT_, st_ = cx.xt, cx.hT, cx.stat
        for s in range(4):
            act(cx.sq, xt_[:, s, :], AF.Square, [xt_.r], [cx.sq_r, st_.r], accum_out=st_[:, s:s + 1])
        act(st_[:, 0:4], st_[:, 0:4], AF.Sqrt, [st_.r], [st_.r], scale=1.0 / D, bias=EPS)
        S.add("dve", lambda e: e.reciprocal(out=st_[:, 0:4], in_=st_[:, 0:4]), reads=[st_.r], writes=[st_.r])
        for s in range(4):
            x_ = xn[s % 2]
            ts("dve", x_[:], xt_[:, s, :], st_[:, s:s + 1], None, ALU.mult, None, [xt_.r, st_.r], [x_.r])
            b = cx.tb[s % 2]
            pv = banks[b].t[:].bitcast(BF16).rearrange("p (c n) -> p c n", c=8)
            for c in range(8):
                tr(banks[b].r, pv[:, c, :], x_[:, c * 128:(c + 1) * 128], identb[:], [x_.r, identb.r])
            tt("dve", hT_[:, :, s * 128:(s + 1) * 128], pv, bc(gpre[:, gi, :].unsqueeze(2), [128, 8, 128]), ALU.mult,
               [banks[b].r, gpre.r], [hT_.r], excl=[banks[b].r])

    def postnorm_residual(cmul, cx, subs, bank_of):
        xt_, st_ = cx.xt, cx.stat
        for s in subs:
            for half in range(2):
                b = bank_of(s, half)
                act(junk[:, half * 512:(half + 1) * 512], banks[b].t[:], AF.Square, [banks[b].r], [junk.r, st_.r], excl=[banks[b].r],
                    accum_out=st_[:, 4 + 2 * s + half:5 + 2 * s + half])
        s0, s1 = subs[0], subs[-1] + 1
        tt("dve", st_[:, 12 + s0:12 + s1], st_[:, 4 + 2 * s0:4 + 2 * s1:2], st_[:, 5 + 2 * s0:5 + 2 * s1:2], ALU.add, [st_.r], [st_.r])
        act(st_[:, 12 + s0:12 + s1], st_[:, 12 + s0:12 + s1], AF.Sqrt, [st_.r], [st_.r], scale=1.0 / D, bias=EPS)
        S.add("dve", lambda e: e.reciprocal(out=st_[:, 12 + s0:12 + s1], in_=st_[:, 12 + s0:12 + s1]), reads=[st_.r], writes=[st_.r])
        if cmul != 1.0:
            ts("dve", st_[:, 12 + s0:12 + s1], st_[:, 12 + s0:12 + s1], cmul, None, ALU.mult, None, [st_.r], [st_.r])
        for s in subs:
            for half in range(2):
                b = bank_of(s, half)
                hs = slice(half * 512, (half + 1) * 512)
                jj = junk[:, 1024 + half * 512:1536 + half * 512]
                stt(jj, banks[b].t[:], st_[:, 12 + s:13 + s], gpost[:, hs], ALU.mult, ALU.mult,
                    [banks[b].r, st_.r, gpost.r], [junk.r], excl=[banks[b].r])
                tt(getattr(cx, "add_eng", "pool"), xt_[:, s, hs], xt_[:, s, hs], jj, ALU.add, [xt_.r, junk.r], [xt_.r])

    def down_post_gen(src, nchunks, wkey, gidx, cmul, cx, hook=None):
        dma(gpost[:], gains[gidx].partition_broadcast(128), [], [gpost.r], "gpost")
        if cx.two_pass:
            pb_ = getattr(cx, "passB_base", 0)
            passes = [([0, 1], lambda s, half: 2 * (s % 2) + half), ([2, 3], lambda s, half: pb_ + 2 * (s % 2) + half)]
        else:
            passes = [([0, 1, 2, 3], lambda s, half: 2 * s + half)]
        for subs, bank_of in passes:
            for c0 in range(0, nchunks, 2):
                c1 = min(nchunks, c0 + 2)
                wsl = wload(wb[wkey][c0:c1].rearrange("a p n -> p a n"), (c1 - c0) * 1024, wres[wkey])
                wv = wsl.t[:].rearrange("p (a n) -> p a n", n=1024)
                for c in range(c0, c1):
                    for s in subs:
                        for half in range(2):
                            b = bank_of(s, half)
                            mm(banks[b].r, banks[b].t[:], src[:, c, s * 128:(s + 1) * 128], wv[:, c - c0, half * 512:(half + 1) * 512],
                               c == 0, c == nchunks - 1, [src.r, wsl.r])
                if c0 % 4 == 2 or c1 == nchunks:
                    yield
            if hook is not None and subs[-1] == 3:
                hook()
            postnorm_residual(cmul, cx, subs, bank_of)
            yield

    first_ffn = [True]

    def ffn_gen(wgu, wd, gpre_i, gpost_i, cx, skip_prenorm=False, hook=None):
        if not skip_prenorm:
            prenorm(gpre_i, cx)
        yield
        for f0 in range(0, NFC, 2):
            for fc in range(f0, f0 + 2):
                wsl = wload(wb[wgu][fc], 2048, wgu1_res[f0] if wgu == "wgu1" else wres[wgu])
                wv = wsl.t[:].rearrange("p (g k n) -> p g k n", g=2, k=8)
                bg = (fc % 2) * 2
                for g in range(2):
                    for kc in range(8):
                        mm(banks[bg + g].r, banks[bg + g].t[:], wv[:, g, kc, :], cx.hT[:, kc, :], kc == 0, kc == 7, [wsl.r, cx.hT.r])
                jg = junk[:, (fc % 2) * 512:(fc % 2 + 1) * 512]
                act(jg, banks[bg].t[:], AF.Silu, [banks[bg].r], [junk.r], excl=[banks[bg].r])
                tt("dve", AT[:, fc, :], jg, banks[bg + 1].t[:], ALU.mult, [junk.r, banks[bg + 1].r], [AT.r], excl=[banks[bg + 1].r])
            yield
        first_ffn[0] = False
        for _ in down_post_gen(AT, NFC, wd, gpost_i, 0.5, cx, hook=hook):
            yield

    def run_gen(g):
        for _ in g:
            pass

    def interleave(ga, gb):
        da = db = False
        while not (da and db):
            if bg:
                bg.pop(0)()
            if not da:
                try:
                    next(ga)
                except StopIteration:
                    da = True
            if not db:
                try:
                    next(gb)
                except StopIteration:
                    db = True

    def ffn(wgu, wd, gpre_i, gpost_i):
        run_gen(ffn_gen(wgu, wd, gpre_i, gpost_i, ctxM))

    def prefix_mixer_gen(tile_abs, cx):
        tok0 = tile_abs * T
        hT_ = cx.hT
        prenorm(1, cx)
        yield
        chunks = list(range(4, 20))
        wslB = {}

        def loadB(pi):
            if pi * 2 < len(chunks) and pi not in wslB:
                c0 = chunks[pi * 2]
                i_ = wslotB[0] % 2
                wslotB[0] += 1
                dma(wbufB[i_][:, 0:2048], wb["wfm"][c0:c0 + 2].rearrange("a p n -> p a n"), [wres["wfm"]], [wbufB[i_].r], "wbB%d" % i_)
                wslB[pi] = wbufB[i_]
        loadB(0)
        pend = []

        def flush():
            for f in pend:
                f()
            del pend[:]
        for ci, c in enumerate(chunks):
            if ci % 2 == 0:
                loadB(ci // 2 + 1)
            wsl = wslB[ci // 2]
            wv = wsl.t[:, 0:2048].rearrange("p (a k n) -> p a k n", k=8, n=128)
            a = ci % 2
            b = 6 + ci % 2
            for kc in range(8):
                mm(banks[b].r, banks[b].t[:], wv[:, a, kc, :], hT_[:, kc, :], kc == 0, kc == 7, [wsl.r, hT_.r])
            flush()
            if c < 8:
                kt_ = ktmp[c % 2]
                cp("act", kt_[:], banks[b].t[:], [banks[b].r], [kt_.r], excl=[banks[b].r])
                dma(kscr[(c - 4) * 128:(c - 3) * 128, tok0:tok0 + T], kt_[:], [kt_.r], [kres], "kw", eng="act")
            else:
                x = c - 8
                rw = raw[x % 2]
                ca = cacc[x % 2]
                cp("pool", rw[:, 0:3], halo[:, x, :], [halo.r], [rw.r])
                cp("act", rw[:, 3:T + 3], banks[b].t[:], [banks[b].r], [rw.r], excl=[banks[b].r])
                ts("dve", ca[:], rw[:, 0:T], cw[:, x, 0:1], cb[:, x:x + 1], ALU.mult, ALU.add, [rw.r, cw.r, cb.r], [ca.r])
                for k in range(1, 4):
                    stt(ca[:], rw[:, k:k + T], cw[:, x, k:k + 1], ca[:], ALU.mult, ALU.add, [rw.r, cw.r, ca.r], [ca.r])
                cp("pool", halo[:, x, :], rw[:, T:T + 3], [rw.r], [halo.r])
                if x < 8:
                    xo = xsT[x % 2]
                    act(xo[:], ca[:], AF.Silu, [ca.r], [xo.r])
                    def _trs(x=x, xo=xo, bt=6 + ci % 2):
                        pv = banks[bt].t[:].bitcast(BF16)[:, 0:512].rearrange("p (s n) -> p s n", s=4)
                        for s in range(4):
                            tr(banks[bt].r, pv[:, s, :], xo[:, s * 128:(s + 1) * 128], identb[:], [xo.r, identb.r])
                        cp("dve", xs_tok[:, :, x * 128:(x + 1) * 128], pv, [banks[bt].r], [xs_tok.r], excl=[banks[bt].r])
                    pend.append(_trs)
                elif x < 10:
                    act(BCT[:, x - 8, :], ca[:], AF.Silu, [ca.r], [BCT.r])
            yield
        flush()
        wvs = []
        for hf in range(2):
            i_ = wslotB[0] % 2
            wslotB[0] += 1
            dma(wbufB[i_][:, 0:2048].rearrange("p (k n) -> p k n", k=8), wb["wtm"][0].rearrange("p (k n) -> p k n", k=8)[:, :, hf * 256:(hf + 1) * 256],
                [wres["wtm"]], [wbufB[i_].r], "wbB%d" % i_)
            wvs.append(wbufB[i_])
        for s in range(4):
            b = 6 + s % 2
            for hf in range(2):
                wv = wvs[hf].t[:, 0:2048].rearrange("p (k n) -> p k n", k=8)
                for kc in range(8):
                    mm(banks[b].r, banks[b].t[:, hf * 256:(hf + 1) * 256], hT_[:, kc, s * 128:(s + 1) * 128], wv[:, kc, :], kc == 0, kc == 7,
                       [wvs[hf].r, hT_.r])
            vt = vtmp[s % 2]
            cp("act", vt[:, :, 0:64], banks[b].t[:].rearrange("p (h v) -> p h v", h=8), [banks[b].r], [vt.r], excl=[banks[b].r])
            blk = tile_abs * 4 + s
            dma(vscr[:, :, blk, :].rearrange("h p v -> p h v"), vt[:], [vt.r], [vres], "vw", eng="act")
            yield
        dt_proj(hT_, 6)
        yield
        for s in range(4):
            for _ in ssd_chunk(s, False, (6, 7, (6, 7))):
                yield

    def proj_and_state(tile_abs, is_main, skip_prenorm=False):
        tok0 = tile_abs * T
        if not skip_prenorm:
            prenorm(1)
        chunks = (list(range(0, 4)) if is_main else []) + list(range(4, 20))
        wslM = {}

        def st0(ci):
            if ci % 2 == 0:
                grp = chunks[ci:ci + 2]
                wslM[ci // 2] = wload(wb["wfm"][grp[0]:grp[0] + len(grp)].rearrange("a p n -> p a n"), len(grp) * 1024, wres["wfm"])
            wsl = wslM[ci // 2]
            wv = wsl.t[:].rearrange("p (a k n) -> p a k n", k=8, n=128)
            b = ci % 4
            for kc in range(8):
                mm(banks[b].r, banks[b].t[:], wv[:, ci % 2, kc, :], hT[:, kc, :], kc == 0, kc == 7, [wsl.r, hT.r])

        def st1(ci):
            c = chunks[ci]
            b = ci % 4
            if c < 4:
                act(QT[:, c, :], banks[b].t[:], AF.Copy, [banks[b].r], [QT.r], excl=[banks[b].r], scale=32.0 ** -0.5)
            elif c < 8:
                kt_ = ktmp[c % 2]
                cp("act", kt_[:], banks[b].t[:], [banks[b].r], [kt_.r], excl=[banks[b].r])
            else:
                x = c - 8
                rw = raw[x % 2]
                cp("pool", rw[:, 0:3], halo[:, x, :], [halo.r], [rw.r])
                cp("act", rw[:, 3:T + 3], banks[b].t[:], [banks[b].r], [rw.r], excl=[banks[b].r])

        def st2(ci):
            c = chunks[ci]
            if 4 <= c < 8:
                kt_ = ktmp[c % 2]
                dma(kscr[(c - 4) * 128:(c - 3) * 128, tok0:tok0 + T], kt_[:], [kt_.r], [kres], "kw", eng="act")
            elif c >= 8:
                x = c - 8
                rw = raw[x % 2]
                ca = cacc[x % 2]
                ts("dve", ca[:], rw[:, 0:T], cw[:, x, 0:1], cb[:, x:x + 1], ALU.mult, ALU.add, [rw.r, cw.r, cb.r], [ca.r])
                for k in range(1, 4):
                    stt(ca[:], rw[:, k:k + T], cw[:, x, k:k + 1], ca[:], ALU.mult, ALU.add, [rw.r, cw.r, ca.r], [ca.r])
                cp("pool", halo[:, x, :], rw[:, T:T + 3], [rw.r], [halo.r])

        def st3(ci):
            c = chunks[ci]
            if c >= 8:
                x = c - 8
                ca = cacc[x % 2]
                if x < 8:
                    act(xsT[x % 2][:], ca[:], AF.Silu, [ca.r], [xsT[x % 2].r])
                else:
                    act(BCT[:, x - 8, :], ca[:], AF.Silu, [ca.r], [BCT.r])

        def st4(ci):
            c = chunks[ci]
            if 8 <= c < 16:
                x = c - 8
                xo = xsT[x % 2]
                bt = 6 + (x % 2)
                pv = banks[bt].t[:].bitcast(BF16)[:, 0:512].rearrange("p (s n) -> p s n", s=4)
                for s in range(4):
                    tr(banks[bt].r, pv[:, s, :], xo[:, s * 128:(s + 1) * 128], identb[:], [xo.r, identb.r])
                cp("dve", xs_tok[:, :, x * 128:(x + 1) * 128], pv, [banks[bt].r], [xs_tok.r], excl=[banks[bt].r])
        stages = [st0, st1, st2, st3, st4]
        n = len(chunks)
        for u in range(n + len(stages) - 1):
            for k, f in enumerate(stages):
                ci = u - k
                if 0 <= ci < n:
                    f(ci)
        ngrp = 3 if is_main else 1
        for g in range(ngrp):
            wsrc = wb["wtm"][g].rearrange("p (k n) -> p k n", k=8)
            wsl2 = []
            for hf in range(2):
                i_ = wslot[0] % 4
                wslot[0] += 1
                dma(wbuf[i_][:].rearrange("p (k n) -> p k n", k=4), wsrc[:, hf * 4:(hf + 1) * 4, :], [wres["wtm"]], [wbuf[i_].r], "wb%d" % i_)
                wsl2.append(wbuf[i_])
            for s in range(4):
                if g > 0:
                    b = 4 * ((g - 1) % 2) + s
                else:
                    b = 4 + s
                for kc in range(8):
                    wsl = wsl2[kc // 4]
                    wv = wsl.t[:].rearrange("p (k n) -> p k n", k=4)
                    mm(banks[b].r, banks[b].t[:], hT[:, kc, s * 128:(s + 1) * 128], wv[:, kc % 4, :], kc == 0, kc == 7, [wsl.r, hT.r])
                if g == 0:
                    vt = vtmp[s % 2]
                    cp("act", vt[:, :, 0:64], banks[b].t[:].rearrange("p (h v) -> p h v", h=8), [banks[b].r], [vt.r], excl=[banks[b].r])
                    blk = tile_abs * 4 + s
                    dma(vscr[:, :, blk, :].rearrange("h p v -> p h v"), vt[:], [vt.r], [vres], "vw", eng="act")
                else:
                    zc = slice((g - 1) * 512, g * 512)
                    act(zs_all[s][:, zc], banks[b].t[:], AF.Silu, [banks[b].r], [zs_all[s].r], excl=[banks[b].r])
        return tok0

    class _V:
        def __init__(self, ap, r):
            self.ap = ap
            self.r = r

        def __getitem__(self, k):
            return self.ap[k]
    _atf = AT.t[:].rearrange("p c n -> p (c n)").bitcast(F32)
    zs_all = [_V(_atf[:, i * D:(i + 1) * D], AT.r) for i in range(4)]
    ao = _V(xs_tok.t[:].rearrange("p s n -> p (s n)").bitcast(F32).rearrange("p (s n) -> p s n", s=4), xs_tok.r)
    dt_all = sb("dt_all", [128, 4, 16])

    def dt_proj(hT_=None, b=3):
        if hT_ is None:
            hT_ = hT
            wsl = wload(wb["wdt"], 128, wres["wdt"])
        else:
            i_ = wslotB[0] % 2
            wslotB[0] += 1
            dma(wbufB[i_][:, 0:128], wb["wdt"], [wres["wdt"]], [wbufB[i_].r], "wbB%d" % i_)
            wsl = wbufB[i_]
        wv = wsl.t[:, 0:128].rearrange("p (k n) -> p k n", k=8)
        for s in range(4):
            for kc in range(8):
                mm(banks[b].r, banks[b].t[:, s * 16:(s + 1) * 16], hT_[:, kc, s * 128:(s + 1) * 128], wv[:, kc, :], kc == 0, kc == 7, [wsl.r, hT_.r])
        tt("dve", dt_all[:], banks[b].t[:, 0:64].rearrange("p (s j) -> p s j", s=4), bc(hpb[:, 0:16].unsqueeze(1), [128, 4, 16]), ALU.add,
           [banks[b].r, hpb.r], [dt_all.r], excl=[banks[b].r])
        act(dt_all[:], dt_all[:], AF.Exp, [dt_all.r], [dt_all.r])
        act(dt_all[:], dt_all[:], AF.Ln, [dt_all.r], [dt_all.r], bias=1.0)

    def ssd_chunk(s, is_main, bk=(3, 2, (6, 7))):
        cs = slice(s * 128, (s + 1) * 128)
        dt_ = dt_all[:, s, :]
        A_ = dtt[:, 16:32]
        tt("dve", A_, dt_, hpb[:, 16:32], ALU.mult, [dt_all.r, hpb.r], [dtt.r])
        b = bk[0]
        mm(banks[b].r, banks[b].t[:, 64:80], umf[:], A_, True, True, [umf.r, dtt.r])
        mm(banks[b].r, banks[b].t[:, 80:96], onesf[:], A_, True, True, [onesf.r, dtt.r])
        cp("dve", dtt[:, 32:64], banks[b].t[:, 64:96], [banks[b].r], [dtt.r], excl=[banks[b].r])
        acs = dtt[:, 32:48]
        tot = dtt[:, 48:64]
        tt("dve", dtt[:, 64:80], tot, acs, ALU.subtract, [dtt.r], [dtt.r])
        act(dtt[:, 64:80], dtt[:, 64:80], AF.Exp, [dtt.r], [dtt.r])
        act(dtt[:, 80:96], tot, AF.Exp, [dtt.r], [dtt.r])
        for g in range(2):
            gs = slice(8 * g, 8 * g + 8)
            xsg = xs_tok[:, s, g * 512:(g + 1) * 512].rearrange("p (j d) -> p j d", j=8)
            BTg = BCT[:, g, cs]
            CTg = BCT[:, 2 + g, cs]
            tt("dve", Xg[:], xsg, bc(dt_all[:, s, gs].unsqueeze(2), [128, 8, 64]), ALU.mult, [xs_tok.r, dt_all.r], [Xg.r])
            if is_main:
                tt("dve", Dg[:], bc(umf[:].unsqueeze(1), [128, 8, 128]), bc(dtt[:, 16 + 8 * g:24 + 8 * g].unsqueeze(2), [128, 8, 128]), ALU.mult,
                   [umf.r, dtt.r], [Dg.r])
                Dv = Dg[:].rearrange("p j l -> p (j l)")
                for hh in range(2):
                    mm(banks[hh].r, banks[hh].t[:], onesf[:], Dv[:, hh * 512:(hh + 1) * 512], True, True, [onesf.r, Dg.r])
                for hh in range(2):
                    act(Eb[:, hh * 4:(hh + 1) * 4, :], banks[hh].t[:].rearrange("p (j l) -> p j l", j=4), AF.Exp, [banks[hh].r], [Eb.r], excl=[banks[hh].r])
                tt("dve", Cdec[:], Eb[:], bc(CTg.unsqueeze(1), [128, 8, 128]), ALU.mult, [Eb.r, BCT.r], [Cdec.r])
                for j in range(8):
                    hh = j // 4
                    ts("dve", Dg[:, j, :], banks[hh].t[:, (j % 4) * 128:(j % 4 + 1) * 128], dtt[:, 32 + 8 * g + j:33 + 8 * g + j], 0.0,
                       ALU.subtract, ALU.min, [banks[hh].r, dtt.r], [Dg.r], excl=[banks[hh].r])
                act(Lb[:], Dg[:], AF.Exp, [Dg.r], [Lb.r])
                mm(banks[2].r, banks[2].t[:, 0:128], BTg, CTg, True, True, [BCT.r])
                tt("dve", cbm[:], banks[2].t[:, 0:128], umf[:], ALU.mult, [banks[2].r, umf.r], [cbm.r], excl=[banks[2].r])
                tt("dve", MTb[:], Lb[:], bc(cbm[:].unsqueeze(1), [128, 8, 128]), ALU.mult, [Lb.r, cbm.r], [MTb.r])
                yb = 4 + g
                for j in range(8):
                    mm(banks[yb].r, banks[yb].t[:, j * 64:(j + 1) * 64], MTb[:, j, :], Xg[:, j, :], True, False, [MTb.r, Xg.r])
                    mm(banks[yb].r, banks[yb].t[:, j * 64:(j + 1) * 64], Cdec[:, j, :], Sbf[:, g, j * 64:(j + 1) * 64], False, True, [Cdec.r, Sbf.r])
                tt("dve", ytmp[:].rearrange("p (j d) -> p j d", j=8), xsg, bc(hpb[:, 32 + 8 * g:40 + 8 * g].unsqueeze(2), [128, 8, 64]), ALU.mult,
                   [xs_tok.r, hpb.r], [ytmp.r])
                tt("dve", ytmp[:], ytmp[:], banks[yb].t[:], ALU.add, [ytmp.r, banks[yb].r], [ytmp.r], excl=[banks[yb].r])
                tt("dve", ytmp[:], ytmp[:], zs_all[s][:, g * 512:(g + 1) * 512], ALU.mult, [ytmp.r, zs_all[s].r], [ytmp.r])
                act(junk[:, 1024:1536], ytmp[:], AF.Square, [ytmp.r], [junk.r, stat.r], accum_out=stat[:, g:g + 1])
                act(stat[:, g:g + 1], stat[:, g:g + 1], AF.Sqrt, [stat.r], [stat.r], scale=1.0 / 512, bias=EPS)
                S.add("dve", lambda e, g=g: e.reciprocal(out=stat[:, g:g + 1], in_=stat[:, g:g + 1]), reads=[stat.r], writes=[stat.r])
                ts("dve", mixed[:, s, 512 + g * 512:1024 + g * 512], ytmp[:], stat[:, g:g + 1], None, ALU.mult, None, [ytmp.r, stat.r], [mixed.r])
            tt("dve", Xdec[:], Xg[:], bc(dtt[:, 64 + 8 * g:72 + 8 * g].unsqueeze(2), [128, 8, 64]), ALU.mult, [Xg.r, dtt.r], [Xdec.r])
            pvb = banks[bk[1]].t[:].bitcast(BF16)[:, 512:640]
            tr(banks[bk[1]].r, pvb, BTg, identb[:], [BCT.r, identb.r])
            cp("dve", Btok[:], pvb, [banks[bk[1]].r], [Btok.r], excl=[banks[bk[1]].r])
            lb = bk[2][g]
            mm(banks[lb].r, banks[lb].t[:], Btok[:], Xdec[:].rearrange("p j d -> p (j d)"), True, True, [Btok.r, Xdec.r])
            Sg = Sst[:, g, :].rearrange("p (j d) -> p j d", j=8)
            tt("dve", Sg, Sg, bc(dtt[:, 80 + 8 * g:88 + 8 * g].unsqueeze(2), [128, 8, 64]), ALU.mult, [Sst.r, dtt.r], [Sst.r])
            tt("dve", Sst[:, g, :], Sst[:, g, :], banks[lb].t[:], ALU.add, [Sst.r, banks[lb].r], [Sst.r], excl=[banks[lb].r])
            cp("pool", Sbf[:, g, :], Sst[:, g, :], [Sst.r], [Sbf.r])
            yield

    def ssd_main_gen(s):
        cs = slice(s * 128, (s + 1) * 128)
        dt_ = dt_all[:, s, :]
        A_ = dtt[:, 16:32]
        st_ = statB
        b7, b6 = banks[7], banks[6]
        tt("dve", A_, dt_, hpb[:, 16:32], ALU.mult, [dt_all.r, hpb.r], [dtt.r])
        yield True
        mm(b7.r, b7.t[:, 64:80], umf[:], A_, True, True, [umf.r, dtt.r])
        mm(b7.r, b7.t[:, 80:96], onesf[:], A_, True, True, [onesf.r, dtt.r])
        yield False
        cp("dve", dtt[:, 32:64], b7.t[:, 64:96], [b7.r], [dtt.r], excl=[b7.r])
        acs = dtt[:, 32:48]
        tot = dtt[:, 48:64]
        tt("dve", dtt[:, 64:80], tot, acs, ALU.subtract, [dtt.r], [dtt.r])
        yield True
        act(dtt[:, 64:80], dtt[:, 64:80], AF.Exp, [dtt.r], [dtt.r])
        act(dtt[:, 80:96], tot, AF.Exp, [dtt.r], [dtt.r])
        yield True
        hb = [b6, b7]
        for g in range(2):
            gs = slice(8 * g, 8 * g + 8)
            xsg = xs_tok[:, s, g * 512:(g + 1) * 512].rearrange("p (j d) -> p j d", j=8)
            BTg = BCT[:, g, cs]
            CTg = BCT[:, 2 + g, cs]
            tt("dve", Xg[:], xsg, bc(dt_all[:, s, gs].unsqueeze(2), [128, 8, 64]), ALU.mult, [xs_tok.r, dt_all.r], [Xg.r])
            tt("dve", Dg[:], bc(umf[:].unsqueeze(1), [128, 8, 128]), bc(dtt[:, 16 + 8 * g:24 + 8 * g].unsqueeze(2), [128, 8, 128]), ALU.mult,
               [umf.r, dtt.r], [Dg.r])
            tt("dve", Xdec[:], Xg[:], bc(dtt[:, 64 + 8 * g:72 + 8 * g].unsqueeze(2), [128, 8, 64]), ALU.mult, [Xg.r, dtt.r], [Xdec.r])
            mm(b7.r, b7.t[:, 0:128], BTg, CTg, True, True, [BCT.r])
            yield False
            tt("dve", cbm[:], b7.t[:, 0:128], umf[:], ALU.mult, [b7.r, umf.r], [cbm.r], excl=[b7.r])
            Dv = Dg[:].rearrange("p j l -> p (j l)")
            for hh in range(2):
                mm(hb[hh].r, hb[hh].t[:], onesf[:], Dv[:, hh * 512:(hh + 1) * 512], True, True, [onesf.r, Dg.r])
            yield False
            for hh in range(2):
                act(Eb[:, hh * 4:(hh + 1) * 4, :], hb[hh].t[:].rearrange("p (j l) -> p j l", j=4), AF.Exp, [hb[hh].r], [Eb.r], excl=[hb[hh].r])
            for j in range(8):
                hh = j // 4
                ts("dve", Dg[:, j, :], hb[hh].t[:, (j % 4) * 128:(j % 4 + 1) * 128], dtt[:, 32 + 8 * g + j:33 + 8 * g + j], 0.0,
                   ALU.subtract, ALU.min, [hb[hh].r, dtt.r], [Dg.r], excl=[hb[hh].r])
            yield True
            yield True
            tt("dve", Cdec[:], Eb[:], bc(CTg.unsqueeze(1), [128, 8, 128]), ALU.mult, [Eb.r, BCT.r], [Cdec.r])
            act(Lb[:], Dg[:], AF.Exp, [Dg.r], [Lb.r])
            pvb = b7.t[:].bitcast(BF16)[:, 512:640]
            tr(b7.r, pvb, BTg, identb[:], [BCT.r, identb.r])
            yield False
            tt("dve", MTb[:], Lb[:], bc(cbm[:].unsqueeze(1), [128, 8, 128]), ALU.mult, [Lb.r, cbm.r], [MTb.r])
            cp("dve", Btok[:], pvb, [b7.r], [Btok.r], excl=[b7.r])
            yield True
            for j in range(8):
                mm(b6.r, b6.t[:, j * 64:(j + 1) * 64], MTb[:, j, :], Xg[:, j, :], True, False, [MTb.r, Xg.r])
                mm(b6.r, b6.t[:, j * 64:(j + 1) * 64], Cdec[:, j, :], Sbf[:, g, j * 64:(j + 1) * 64], False, True, [Cdec.r, Sbf.r])
            mm(b7.r, b7.t[:], Btok[:], Xdec[:].rearrange("p j d -> p (j d)"), True, True, [Btok.r, Xdec.r])
            yield False
            tt("dve", ytmp[:].rearrange("p (j d) -> p j d", j=8), xsg, bc(hpb[:, 32 + 8 * g:40 + 8 * g].unsqueeze(2), [128, 8, 64]), ALU.mult,
               [xs_tok.r, hpb.r], [ytmp.r])
            tt("dve", ytmp[:], ytmp[:], b6.t[:], ALU.add, [ytmp.r, b6.r], [ytmp.r], excl=[b6.r])
            tt("dve", ytmp[:], ytmp[:], zs_all[s][:, g * 512:(g + 1) * 512], ALU.mult, [ytmp.r, zs_all[s].r], [ytmp.r])
            Dsq = Dg[:].rearrange("p j l -> p (j l)")[:, 0:512]
            tt("dve", Dsq, ytmp[:], ytmp[:], ALU.mult, [ytmp.r], [Dg.r])
            S.add("dve", lambda e, g=g, Dsq=Dsq: e.tensor_reduce(out=st_[:, g:g + 1], in_=Dsq, axis=AX.X, op=ALU.add),
                  reads=[Dg.r], writes=[st_.r])
            Sg = Sst[:, g, :].rearrange("p (j d) -> p j d", j=8)
            tt("dve", Sg, Sg, bc(dtt[:, 80 + 8 * g:88 + 8 * g].unsqueeze(2), [128, 8, 64]), ALU.mult, [Sst.r, dtt.r], [Sst.r])
            tt("dve", Sst[:, g, :], Sst[:, g, :], b7.t[:], ALU.add, [Sst.r, b7.r], [Sst.r], excl=[b7.r])
            cp("dve", Sbf[:, g, :], Sst[:, g, :], [Sst.r], [Sbf.r])
            yield True
            yield True
            act(st_[:, g:g + 1], st_[:, g:g + 1], AF.Ln, [st_.r], [st_.r], scale=1.0 / 512, bias=EPS)
            act(st_[:, g:g + 1], st_[:, g:g + 1], AF.Exp, [st_.r], [st_.r], scale=-0.5)
            yield True
            ts("dve", mixed[:, s, 512 + g * 512:1024 + g * 512], ytmp[:], st_[:, g:g + 1], None, ALU.mult, None, [ytmp.r, st_.r], [mixed.r])
            yield True

    ssd_clean = [True]

    def ssd_all_gen():
        for s_ in range(4):
            for c_ in ssd_main_gen(s_):
                yield c_

    pt_i = [0]
    vb_i = [0]

    def att_units(tile_abs):
        tb = tile_abs * 4
        n = 0
        for h in range(8):
            dmax = 140.0 / (2.0 ** -(h + 1))
            kb_lo = 0
            while kb_lo < tb and (tb * 128 - (kb_lo * 128 + 127)) > dmax:
                kb_lo += 1
            n += tb + 4 - kb_lo
        return n

    def attention_gen(tile_abs):
        tb = tile_abs * 4
        nkb = tb + 4
        grp_all = {}
        obase = 4

        def kb_lo_of(hh):
            dm = 140.0 / (2.0 ** -(hh + 1))
            lo = 0
            while lo < tb and (tb * 128 - (lo * 128 + 127)) > dm:
                lo += 1
            return lo

        def get_grp(hh, kb):
            g = kb // VB_BLK
            if (hh, g) not in grp_all:
                k0 = max(g * VB_BLK, kb_lo_of(hh))
                k1 = min(nkb, (g + 1) * VB_BLK)
                vi = vb_i[0] % 3
                vb_i[0] += 1
                p0 = (hh % 2) * 64
                dma(vbuf[vi][:, 0:k1 - k0, :], vscr[hh, :, k0:k1, :], [vres], [vbuf[vi].r], "vb%d" % vi)
                dma(kbuf[vi][p0:p0 + 64, 0:(k1 - k0) * 128], kscr[hh * 64:(hh + 1) * 64, k0 * 128:k1 * 128], [kres], [kbuf[vi].r], "kb%d" % vi)
                grp_all[(hh, g)] = (vi, k0)
            return grp_all[(hh, g)]

        def prefetch(hh, kb):
            g = kb // VB_BLK
            if kb != max(kb_lo_of(hh), g * VB_BLK):
                return
            nxt = (g + 1) * VB_BLK
            if nxt < nkb:
                get_grp(hh, nxt)
            elif hh + 1 < 8:
                get_grp(hh + 1, kb_lo_of(hh + 1))

        items = [(hh, kb) for hh in range(8) for kb in range(kb_lo_of(hh), nkb)]
        st = {}

        def s1(idx):
            h, kb = items[idx]
            c = h // 2
            pb0 = (h % 2) * 64
            use_pos = h < 3
            vi, k0 = get_grp(h, kb)
            prefetch(h, kb)
            di = kb - tb
            q0 = 128 * di if di > 0 else 0
            pair = idx % 2
            kl = (kb - k0) * 128
            diag = di >= 0
            for m in range(2):
                pb = pb0 + 32 * m
                sbank = banks[2 * pair + m]
                tp = (96, 0) if pb == 96 else None
                mm(sbank.r, sbank.t[:, q0:T], kbuf[vi][pb:pb + 32, kl:kl + 128], QT[pb:pb + 32, c, q0:T], True, not (use_pos or diag),
                   [kbuf[vi].r, QT.r], tp=tp)
            if use_pos:
                for m in range(2):
                    pb = pb0 + 32 * m
                    sbank = banks[2 * pair + m]
                    tp = (96, 0) if pb == 96 else None
                    mm(sbank.r, sbank.t[:, q0:T], slopes2[pb:pb + 2, h * 128:(h + 1) * 128], qpos[pb:pb + 2, q0:T], False, not diag,
                       [slopes2.r, qpos.r], tp=tp)
            if diag:
                for m in range(2):
                    sbank = banks[2 * pair + m]
                    mm(sbank.r, sbank.t[:, q0:q0 + 128], identb[:], umb[:], False, True, [identb.r, umb.r])
            st[idx] = (vi, k0, q0, pair, di)

        def s2(idx):
            h, kb = items[idx]
            vi, k0, q0, pair, di = st[idx]
            p_ = PT[pt_i[0] % 3]
            pt_i[0] += 1
            st[idx] = (vi, k0, q0, pair, di, p_)
            btab = biasP if kb < NPT * 4 else biasM
            jidx = (tb - kb) + 3
            src = psum_all[:, pair * 1024:(pair + 1) * 1024].rearrange("p (m n) -> p m n", m=2)[:, :, q0:T]
            b0, b1 = banks[2 * pair], banks[2 * pair + 1]
            act(p_[:, :, q0:T], src, AF.Exp, [b0.r, b1.r, btab.r], [p_.r], excl=[b0.r, b1.r],
                bias=btab[:, h * NJ + jidx:h * NJ + jidx + 1])

        def s3(idx):
            h, kb = items[idx]
            vi, k0, q0, pair, di, p_ = st[idx]
            first = kb == kb_lo_of(h)
            for m in range(2):
                ob = obase + m
                for sq in range(max(di, 0), 4):
                    mm(banks[ob].r, banks[ob].t[:, sq * 65:(sq + 1) * 65], p_[:, m, sq * 128:(sq + 1) * 128], vbuf[vi][:, kb - k0, :],
                       first and sq == 0, kb == nkb - 1, [p_.r, vbuf[vi].r], sgc=True)

        ott = junk[:, 0:520].rearrange("p (m s v) -> p m s v", m=2, s=4)
        otmp = junk[:, 1792:2048].rearrange("p (s v) -> p s v", s=4)
        sq_ = junk[:, 520:776].rearrange("p (s v) -> p s v", s=4)
        tn_ = junk[:, 776:1032].rearrange("p (s v) -> p s v", s=4)

        def epi0(h):
            for m in range(2):
                ob = obase + m
                cp("act", ott[:, m], banks[ob].t[:, 0:4 * 65].rearrange("p (s v) -> p s v", s=4), [banks[ob].r], [junk.r], excl=[banks[ob].r])

        def epi1(h):
            o1 = ott[:, 0]
            o2 = ott[:, 1]
            S.add("dve", lambda e: e.reciprocal(out=stat[:, 0:4], in_=o1[:, :, 64]), reads=[junk.r], writes=[stat.r])
            S.add("dve", lambda e: e.reciprocal(out=stat[:, 4:8], in_=o2[:, :, 64]), reads=[junk.r], writes=[stat.r])
            ts("dve", stat[:, 4:8], stat[:, 4:8], lam[:, 3:4], None, ALU.mult, None, [stat.r, lam.r], [stat.r])
            for s in range(4):
                ts("dve", junk[:, 1536 + s * 64:1600 + s * 64], o2[:, s, 0:64], stat[:, 4 + s:5 + s], None, ALU.mult, None,
                   [junk.r, stat.r], [junk.r])
                stt(otmp[:, s, :], o1[:, s, 0:64], stat[:, s:s + 1], junk[:, 1536 + s * 64:1600 + s * 64], ALU.mult, ALU.subtract,
                    [stat.r, junk.r], [junk.r])
            tt("dve", sq_, otmp, otmp, ALU.mult, [junk.r], [junk.r])
            S.add("dve", lambda e: e.tensor_reduce(out=stat[:, 8:12], in_=sq_, axis=AX.X, op=ALU.add), reads=[junk.r], writes=[stat.r])

        def epi2(h):
            act(stat[:, 8:12], stat[:, 8:12], AF.Ln, [stat.r], [stat.r], scale=1.0 / 64, bias=EPS)
            act(stat[:, 8:12], stat[:, 8:12], AF.Exp, [stat.r], [stat.r], scale=-0.5)

        def epi3(h):
            tt("dve", tn_, otmp, bc(stat[:, 8:12].unsqueeze(2), [128, 4, 64]), ALU.mult, [junk.r, stat.r], [junk.r])
            tt("dve", mixed[:, :, h * 64:(h + 1) * 64], tn_, bc(gsub[:].unsqueeze(1), [128, 4, 64]), ALU.mult, [junk.r, gsub.r], [mixed.r])

        sched = {}

        def plan_epi(h, idx_last):
            for k, f in enumerate((epi0, epi1, epi2, epi3)):
                sched.setdefault(idx_last + 2 * k, []).append(lambda f=f, h=h: f(h))
        for idx, (h, kb) in enumerate(items):
            if kb == nkb - 1:
                plan_epi(h, idx)
        s1(0)
        for idx in range(len(items)):
            s2(idx)
            if idx + 1 < len(items):
                s1(idx + 1)
            s3(idx)
            for f in sched.pop(idx, []):
                f()
            yield
        for idx in sorted(sched):
            for f in sched[idx]:
                f()

    def out_proj(hook=None):
        for s in range(4):
            for part, (c0, c1) in enumerate([(0, 8), (8, 12)]):
                b = 2 * (s % 2) + part
                pv = banks[b].t[:].bitcast(BF16)[:, 0:(c1 - c0) * 128].rearrange("p (c n) -> p c n", n=128)
                for c in range(c0, c1):
                    tr(banks[b].r, pv[:, c - c0, :], mixed[:, s, c * 128:(c + 1) * 128], identb[:], [mixed.r, identb.r])
                tt("dve", mixedT[:, c0:c1, s * 128:(s + 1) * 128], pv, bc(gmixT[:, c0:c1].unsqueeze(2), [128, c1 - c0, 128]), ALU.mult,
                   [banks[b].r, gmixT.r], [mixedT.r], excl=[banks[b].r])
        run_gen(down_post_gen(mixedT, 12, "wo", 3, 1.0, ctxM, hook=hook))

    def load_x(src, i):
        dma(xt[:], src[i * T:(i + 1) * T, :].rearrange("(s p) d -> p s d", p=128), [], [xt.r], "xt")

    xt2 = sb("xt2", [128, 4, D])
    hTB = sb("hTB", [128, 8, T], BF16)
    statA = sb("statA", [128, 16])
    statB = sb("statB", [128, 16])
    sqB = sb("sqB", [128, D], BF16)
    ctxM = mkctx(xt, hT, stat, junk[:, 0:D], junk.r, (4, 5), True)
    ctxM.passB_base = 4
    xts = [xt, xt2]
    ctxA = [mkctx(xts[k], hT, statA, junk[:, 0:D], junk.r, (4, 5), True) for k in range(2)]
    ctxB = [mkctx(xts[k], hTB, statB, sqB[:], sqB.r, (6, 7), True) for k in range(2)]

    def load_x(src, i, xt_):
        dma(xt_[:], src[i * T:(i + 1) * T, :].rearrange("(s p) d -> p s d", p=128), [], [xt_.r], "xt_" + xt_.r.name)

    if NPT > 0:
        load_x(xp, 0, xts[0])
        run_gen(ffn_gen("wgu1", "wd1", 0, 1, ctxA[0]))
        for i in range(NPT):
            if i == 1 or NPT == 1:
                bg.extend(late_conv)
            gb = prefix_mixer_gen(i, ctxB[i % 2])
            if i + 1 < NPT:
                load_x(xp, i + 1, xts[(i + 1) % 2])
                ga = ffn_gen("wgu1", "wd1", 0, 1, ctxA[(i + 1) % 2])
            else:
                ga = iter(())
            interleave(ga, gb)
        while bg:
            bg.pop(0)()
        ts("dve", Sst[:].rearrange("p g n -> p (g n)"), Sst[:].rearrange("p g n -> p (g n)"), flags[:, 1:2], None, ALU.mult, None, [Sst.r, flags.r], [Sst.r])
        cp("pool", Sbf[:].rearrange("p g n -> p (g n)"), Sst[:].rearrange("p g n -> p (g n)"), [Sst.r], [Sbf.r])
        ts("dve", halo[:].rearrange("p c k -> p (c k)"), halo[:].rearrange("p c k -> p (c k)"), flags[:, 1:2], None, ALU.mult, None, [halo.r, flags.r], [halo.r])
    last = None
    lasts = {}
    if NMT > 0:
        load_x(xm, 0, xts[0])
    ctxF1 = [mkctx(xts[k], hTB, statA, sqB[:], sqB.r, (0, 1), True) for k in range(2)]
    for k in range(2):
        ctxF1[k].passB_base = 4
    _ysq = ytmp.t[:].bitcast(BF16)
    _dsq = Dg.t[:].rearrange("p j l -> p (j l)").bitcast(BF16)[:, 0:D]
    ctxH2 = [mkctx(xts[k], hT, statB, _ysq, ytmp.r, (0, 1), True) for k in range(2)]
    ctxHm = [mkctx(xts[k], hT, statB, _dsq, Dg.r, (0, 1), True) for k in range(2)]
    if NMT > 0:
        run_gen(ffn_gen("wgu1", "wd1", 0, 1, ctxF1[0]))
        if NMT > 1:
            load_x(xm, 1, xts[1])
    for i in range(NMT):
        ctxM.xt = xts[i % 2]
        xc = xts[i % 2]
        nxt = i + 1 < NMT
        proj_and_state(NPT + i, True, skip_prenorm=(i > 0))
        dt_proj()
        if nxt and i >= 1:
            load_x(xm, i + 1, xts[(i + 1) % 2])
        ga = attention_gen(NPT + i)
        gs_ = ssd_all_gen()
        n_att = att_units(NPT + i)
        ratio = max(1, n_att // 110)
        da = ds = False
        while not (da and ds):
            for _ in range(ratio):
                if not da:
                    try:
                        next(ga)
                    except StopIteration:
                        da = True
            if not ds:
                try:
                    ssd_clean[0] = bool(next(gs_))
                except StopIteration:
                    ds = True
                    ssd_clean[0] = True
        out_proj(hook=(lambda i=i: prenorm(0, ctxF1[(i + 1) % 2])) if nxt else None)
        if nxt:
            run_gen(ffn_gen("wgu1", "wd1", 0, 1, ctxF1[(i + 1) % 2], skip_prenorm=True,
                            hook=lambda i=i: prenorm(2, ctxH2[i % 2])))
            run_gen(ffn_gen("wgu2", "wd2", 2, 5, ctxM, skip_prenorm=True,
                            hook=lambda i=i: prenorm(1, ctxHm[(i + 1) % 2])))
        else:
            run_gen(ffn_gen("wgu2", "wd2", 2, 5, ctxM))
        last = dma(out[i * T:(i + 1) * T, :].rearrange("(s p) d -> p s d", p=128), xc[:], [xc.r], [Res("o")], "out%d" % (i % 2), eng="pool")
        lasts[i % 2] = last
    nsem = S.emit(final_waits=list(lasts.values()))
    return nc, (S.nops, nsem, nc.sbuf_bytes_remaining() if callable(nc.sbuf_bytes_remaining) else nc.sbuf_bytes_remaining)


def _prep_weights(inp):
    f = np.float32
    w = {}

    def gu(g, u):
        g = np.asarray(g[0], f).reshape(8, 128, NFC, 128)
        u = np.asarray(u[0], f).reshape(8, 128, NFC, 128)
        a = np.stack([g, u], 0)
        return np.ascontiguousarray(a.transpose(3, 2, 0, 1, 4)).reshape(NFC, 128, 2048)

    w["wgu1"] = gu(inp["ffn1_w_gate"], inp["ffn1_w_up"])
    w["wgu2"] = gu(inp["ffn2_w_gate"], inp["ffn2_w_up"])
    w["wd1"] = np.ascontiguousarray(np.asarray(inp["ffn1_w_down"][0], f).reshape(NFC, 128, 1024))
    w["wd2"] = np.ascontiguousarray(np.asarray(inp["ffn2_w_down"][0], f).reshape(NFC, 128, 1024))
    win = np.asarray(inp["w_in"][0], f)
    q, k, v, z, xbc, dt = np.split(win, [512, 1024, 1536, 2560, 4096], axis=1)
    fm = np.concatenate([q, k, xbc], axis=1)
    fm = fm.reshape(8, 128, 20, 128).transpose(2, 1, 0, 3)
    w["wfm"] = np.ascontiguousarray(fm).reshape(20, 128, 1024)
    tm = np.concatenate([v, z], axis=1).reshape(8, 128, 3, 512).transpose(2, 1, 0, 3)
    w["wtm"] = np.ascontiguousarray(tm).reshape(3, 128, 4096)
    w["wdt"] = np.ascontiguousarray(dt.reshape(8, 128, 16).transpose(1, 0, 2)).reshape(128, 128)
    w["wo"] = np.ascontiguousarray(np.asarray(inp["w_out"][0], f).reshape(12, 128, 1024))
    return w


def _consts(NPT, NMT):
    f = np.float32
    NBLK = (NPT + NMT) * 4
    NJ = NBLK + 3
    slopes = (2.0 ** (-8.0 * np.arange(1, 9) / 8)).astype(np.float64)
    p = np.arange(128)[:, None, None]
    j = (np.arange(NJ) - 3)[None, None, :]
    bias = slopes[None, :, None] * (p - 128.0 * j)
    bias[:, 3:, :] -= slopes[None, 3:, None] * (T - 1)
    c = {"ident": np.eye(128, dtype=f),
         "umat": np.triu(np.ones((128, 128), f)),
         "negm": (-30000.0 * np.tril(np.ones((128, 128), f), -1)).astype(f),
         "biastab": bias.reshape(128, 8 * NJ).astype(f),
         }
    sl2 = np.zeros((128, 8 * 128), f)
    qp = np.zeros((128, T), f)
    qq = np.arange(T)
    for pb in (0, 32, 64, 96):
        sl2[pb:pb + 2] = np.repeat(slopes, 128)[None, :]
        qp[pb] = -(16.0 * (qq // 16))
        qp[pb + 1] = -(qq % 16)
    c["slopes2"] = sl2
    c["qpos"] = qp
    return c


_CACHE = {}


def run(inp, NPT, NMT, nbatch, debug=False):
    key = (NPT, NMT)
    if key not in _CACHE:
        _CACHE[key] = build(NPT, NMT)
    nc, info = _CACHE[key]
    f = np.float32
    w = _prep_weights(inp)
    c = _consts(NPT, NMT)
    small = {
        "gains": np.stack([np.asarray(inp[k][0], f) for k in
                           ["ffn1_pre_g", "ffn1_post_g", "mix_pre_g", "mix_post_g", "ffn2_pre_g", "ffn2_post_g"]], 0),
        "ssm_g": np.asarray(inp["ssm_norm_g"][0], f),
        "sub_g": np.asarray(inp["attn_subln_g"][0], f),
        "lamv": np.stack([np.asarray(inp[k][0], f) for k in ["lambda_q1", "lambda_k1", "lambda_q2", "lambda_k2"]], 0),
        "conv_w": np.asarray(inp["conv_w"][0], f),
        "conv_b": np.asarray(inp["conv_b"][0], f),
        "hp": np.stack([np.asarray(inp[k][0], f) for k in ["dt_bias", "a_log", "d_skip"]], 0),
    }
    x = np.asarray(inp["x"], f)
    half = NMT * T
    in_maps = []
    for b in range(nbatch):
        for j in range(2):
            m = dict(w)
            m.update(c)
            m.update(small)
            m["xp"] = np.ascontiguousarray(x[b, 0:max(NPT, 1) * T])
            m["xm"] = np.ascontiguousarray(x[b, j * half:(j + 1) * half])
            fl = np.zeros((128, 2), f)
            fl[:, 0] = 0.0 if j == 1 else -30000.0
            fl[:, 1] = 1.0 if j == 1 else 0.0
            m["flags"] = fl
            in_maps.append(m)
    res = run_bass_kernel_spmd(nc, in_maps, core_ids=list(range(2 * nbatch)))
    outs = [r["out"] for r in res.results]
    y = np.zeros((nbatch, 2 * half, D), f)
    for b in range(nbatch):
        for j in range(2):
            y[b, j * half:(j + 1) * half] = outs[2 * b + j]
    return y


def kernel(**inputs):
    return run(inputs, 8, 8, 4)
```
